# Optimizing a Trainium2 kernel written in Bass

```python
import math
import jax, jax.numpy as jnp
from jax import lax
import numpy as np

D_MODEL = 1024
BATCH = 4
SEQ = 8192
DEPTH = 1
DEC_BATCH = 8
DEC_SEQ = 4096
PAST_LEN = 128

GROUP_SIZE = 64
CONV_WIDTH = D_MODEL // 2
HYENA_WIDTH = D_MODEL - CONV_WIDTH
CONV_GROUPS = CONV_WIDTH // GROUP_SIZE
HYENA_GROUPS = HYENA_WIDTH // GROUP_SIZE
PROJ_WIDTH = 3 * CONV_WIDTH + 3 * HYENA_WIDTH
SHORT_K = 3
POS_BANDS = 16
POS_EMB = 1 + 2 * POS_BANDS
FILTER_HIDDEN = 64
DECAY_TARGET = 1e-2
FAST_DECAY_PCT = 0.3
SLOW_DECAY_PCT = 1.5
PEER_HEADS = 8
PEER_KEYS = 128
PEER_EXPERTS = PEER_KEYS * PEER_KEYS
PEER_TOPK = 16
PEER_QDIM = 256
PEER_HALF = PEER_QDIM // 2
PEER_TOKEN_BLOCK = 128
ALPHA = (2.0 * DEPTH) ** 0.25
BETA = (8.0 * DEPTH) ** -0.25
LN_EPS = 1e-5

kernel_name = 'hybrid_conv_hyena_peer_encoder'

F32 = jnp.float32


def layer_norm(x, g, b):
    xf = x.astype(F32)
    mu = jnp.mean(xf, -1, keepdims=True)
    var = jnp.mean(jnp.square(xf - mu), -1, keepdims=True)
    return ((xf - mu) * lax.rsqrt(var + LN_EPS) * g.astype(F32) + b.astype(F32)).astype(x.dtype)


def short_conv(z, w):
    L = z.shape[1]
    pad = SHORT_K // 2
    zp = jnp.pad(z, ((0, 0), (pad, pad), (0, 0)))
    out = zp[:, 0:L] * w[0]
    for j in range(1, SHORT_K):
        out = out + zp[:, j:j + L] * w[j]
    return out


def hyena_filter(L, w1, b1, freq, w2, b2, w3, decay):
    t = jnp.linspace(0.0, 1.0, L, dtype=F32)[:, None]
    w = (2.0 * math.pi) * jnp.arange(L, dtype=F32)[:, None] / L
    f = jnp.linspace(1e-4, POS_BANDS - 1, POS_BANDS, dtype=F32)[None, :]
    feat = jnp.concatenate([t, jnp.cos(f * w), -jnp.sin(f * w)], -1)
    fr = freq.astype(F32)
    h = jnp.sin(fr * (feat @ w1.astype(F32) + b1.astype(F32)))
    h = jnp.sin(fr * (h @ w2.astype(F32) + b2.astype(F32)))
    h = (h @ w3.astype(F32)).reshape(L, 2, HYENA_WIDTH)
    h = h * jnp.exp(-t[:, :, None] * jnp.abs(decay.astype(F32))[None])
    fwd, bwd = h[:, 0], h[:, 1]
    k = jnp.concatenate([fwd, jnp.zeros((1, HYENA_WIDTH), F32), bwd[:0:-1]], 0)
    return k / jnp.sum(jnp.abs(k), 0, keepdims=True)


def long_conv(z, k):
    L = z.shape[1]
    n = 2 * L
    zf = jnp.fft.rfft(z.astype(F32), n=n, axis=1)
    kf = jnp.fft.rfft(k, n=n, axis=0)
    return jnp.fft.irfft(zf * kf[None], n=n, axis=1)[:, :L].astype(z.dtype)


def peer(x, wq, keys, u, v):
    B, L, D = x.shape
    xs = x.reshape(-1, PEER_TOKEN_BLOCK, D)

    def block(xb):
        T = xb.shape[0]
        q = (xb @ wq).reshape(T, PEER_HEADS, 2, PEER_HALF)
        s = jnp.einsum('thpk,hpnk->thpn', q, keys).astype(F32)
        s_top, i_top = lax.top_k(s, PEER_TOPK)
        cand = (s_top[:, :, 0, :, None] + s_top[:, :, 1, None, :]).reshape(T, PEER_HEADS, PEER_TOPK * PEER_TOPK)
        cidx = (i_top[:, :, 0, :, None] * PEER_KEYS + i_top[:, :, 1, None, :]).reshape(T, PEER_HEADS, PEER_TOPK * PEER_TOPK)
        best, pos = lax.top_k(cand, PEER_TOPK)
        eidx = jnp.take_along_axis(cidx, pos, -1)
        g = jax.nn.softmax(best, -1)
        u_sel = jnp.take(u, eidx, axis=0)
        act = jax.nn.gelu(jnp.einsum('thkd,td->thk', u_sel, xb).astype(F32), approximate=False)
        coef = (g * act).astype(x.dtype)
        v_sel = jnp.take(v, eidx, axis=0)
        return jnp.einsum('thk,thkd->td', coef, v_sel)

    return lax.map(block, xs).reshape(B, L, D)


def encoder_layer(x, w_in, b_in, a_conv_w, h_conv_w, h_conv_b, hf_w1, hf_b1, hf_freq, hf_w2, hf_b2,
                  hf_w3, hy_decay, hy_bias, w_out, ln1_g, ln1_b, peer_wq, peer_keys, peer_u, peer_v,
                  ln2_g, ln2_b):
    L = x.shape[1]
    p = x @ w_in + b_in
    cw = CONV_WIDTH
    a_b, a_c, a_h = p[..., :cw], p[..., cw:2 * cw], p[..., 2 * cw:3 * cw]
    y_a = a_b * short_conv(a_c * a_h, a_conv_w)
    hy = short_conv(p[..., 3 * cw:], h_conv_w) + h_conv_b
    hw = HYENA_WIDTH
    x0, x1, hv = hy[..., :hw], hy[..., hw:2 * hw], hy[..., 2 * hw:]
    k = hyena_filter(L, hf_w1, hf_b1, hf_freq, hf_w2, hf_b2, hf_w3, hy_decay)
    z = hv * x1
    z = long_conv(z, k) + z * hy_bias
    y_h = x0 * z
    mix = jnp.concatenate([y_a, y_h], -1) @ w_out
    x = layer_norm(ALPHA * x + mix, ln1_g, ln1_b)
    x = layer_norm(ALPHA * x + peer(x, peer_wq, peer_keys, peer_u, peer_v), ln2_g, ln2_b)
    return x


def setup_inputs(seed: int = 0) -> dict:
    key = jax.random.key(seed)
    ks = jax.random.split(key, 24)
    nrm = lambda k, shape, s: jax.random.normal(k, shape, F32) * s
    decay_lo = abs(math.log(DECAY_TARGET)) / SLOW_DECAY_PCT
    decay_hi = abs(math.log(DECAY_TARGET)) / FAST_DECAY_PCT
    decay_base = jnp.linspace(decay_lo, decay_hi, HYENA_WIDTH, dtype=F32)
    return {
        'x_prompt': nrm(ks[0], (BATCH, SEQ, D_MODEL), 1.0),
        'x_sample': nrm(ks[1], (DEC_BATCH, DEC_SEQ, D_MODEL), 1.0),
        'w_in': nrm(ks[2], (DEPTH, D_MODEL, PROJ_WIDTH), D_MODEL ** -0.5),
        'b_in': nrm(ks[3], (DEPTH, PROJ_WIDTH), 0.02),
        'a_conv_w': nrm(ks[4], (DEPTH, SHORT_K, CONV_WIDTH), SHORT_K ** -0.5),
        'h_conv_w': nrm(ks[5], (DEPTH, SHORT_K, 3 * HYENA_WIDTH), SHORT_K ** -0.5),
        'h_conv_b': nrm(ks[6], (DEPTH, 3 * HYENA_WIDTH), 0.02),
        'hf_w1': nrm(ks[7], (DEPTH, POS_EMB, FILTER_HIDDEN), POS_EMB ** -0.5),
        'hf_b1': nrm(ks[8], (DEPTH, FILTER_HIDDEN), 0.02),
        'hf_freq': 1.0 + nrm(ks[9], (DEPTH, FILTER_HIDDEN), 0.01),
        'hf_w2': nrm(ks[10], (DEPTH, FILTER_HIDDEN, FILTER_HIDDEN), FILTER_HIDDEN ** -0.5),
        'hf_b2': nrm(ks[11], (DEPTH, FILTER_HIDDEN), 0.02),
        'hf_w3': nrm(ks[12], (DEPTH, FILTER_HIDDEN, 2 * HYENA_WIDTH), FILTER_HIDDEN ** -0.5),
        'hy_decay': decay_base + nrm(ks[13], (DEPTH, 2, HYENA_WIDTH), 0.1),
        'hy_bias': nrm(ks[14], (DEPTH, HYENA_WIDTH), 1.0),
        'w_out': nrm(ks[15], (DEPTH, D_MODEL, D_MODEL), BETA * D_MODEL ** -0.5),
        'ln1_g': 1.0 + nrm(ks[16], (DEPTH, D_MODEL), 0.02),
        'ln1_b': nrm(ks[17], (DEPTH, D_MODEL), 0.02),
        'peer_wq': nrm(ks[18], (DEPTH, D_MODEL, PEER_HEADS * PEER_QDIM), D_MODEL ** -0.5),
        'peer_keys': nrm(ks[19], (DEPTH, PEER_HEADS, 2, PEER_KEYS, PEER_HALF), PEER_HALF ** -0.5),
        'peer_u': nrm(ks[20], (DEPTH, PEER_EXPERTS, D_MODEL), D_MODEL ** -0.5),
        'peer_v': nrm(ks[21], (DEPTH, PEER_EXPERTS, D_MODEL), BETA * PEER_HEADS ** -0.5),
        'ln2_g': 1.0 + nrm(ks[22], (DEPTH, D_MODEL), 0.02),
        'ln2_b': nrm(ks[23], (DEPTH, D_MODEL), 0.02),
    }


def reference(x_prompt, x_sample, w_in, b_in, a_conv_w, h_conv_w, h_conv_b, hf_w1, hf_b1, hf_freq,
              hf_w2, hf_b2, hf_w3, hy_decay, hy_bias, w_out, ln1_g, ln1_b, peer_wq, peer_keys,
              peer_u, peer_v, ln2_g, ln2_b):
    def trunk(x):
        for l in range(DEPTH):
            x = encoder_layer(x, w_in[l], b_in[l], a_conv_w[l], h_conv_w[l], h_conv_b[l], hf_w1[l],
                              hf_b1[l], hf_freq[l], hf_w2[l], hf_b2[l], hf_w3[l], hy_decay[l],
                              hy_bias[l], w_out[l], ln1_g[l], ln1_b[l], peer_wq[l], peer_keys[l],
                              peer_u[l], peer_v[l], ln2_g[l], ln2_b[l])
        return x

    y_prompt = trunk(x_prompt)
    y_sample = trunk(x_sample)
    return (y_prompt, y_sample)
```

```python
import math
from contextlib import ExitStack
import numpy as np
import concourse.bass as bass
import concourse.mybir as mybir
from concourse.bass_utils import run_bass_kernel_spmd
from concourse.bass_types import AP

F32 = mybir.dt.float32
BF16 = mybir.dt.bfloat16
U32 = mybir.dt.uint32
I32 = mybir.dt.int32
AF = mybir.ActivationFunctionType
ALU = mybir.AluOpType
AX = mybir.AxisListType

D = 1024
NCORES = 8
ALPHA = 2.0 ** 0.25
LN_EPS = 1e-5
TWO_PI = 6.283185
TA = 512
TT = 256
import os
_DBG_NOINTER = bool(os.environ.get('NOINTER'))


class Sched:
    EP = 30000
    DEP = 1800

    def __init__(self, nc, es):
        self.nc = nc
        self.es = es
        self.eng = {'pe': nc.tensor, 'act': nc.scalar, 'dve': nc.vector, 'pool': nc.gpsimd, 'sp': nc.sync}
        self.ops = []
        self.lastw = {}
        self.readers = {}
        self.lastkey = {}
        self.cnt = {e: 0 for e in self.eng}
        self.sems = {}
        self.ksem = {}
        self.free_sems = []
        self.waited = {}
        self.nsem = 0
        self.n_inst = 0
        self.capture = None

    def record(self, fns):
        self.capture = []
        for f in fns:
            f()
        out = self.capture
        self.capture = None
        return out

    @staticmethod
    def hoist(items, H):
        items = list(items)
        for idx in range(len(items)):
            eng, fn, r, w, key = items[idx]
            if key is None:
                continue
            toks = set(w) | set(r)
            lim = max(0, idx - H)
            pos = idx
            while pos > lim:
                pe, pf, pr, pw, pk = items[pos - 1]
                if pk is not None and (pk == key or pe == eng):
                    break
                if (set(pw) & toks) or (set(pr) & set(w)):
                    break
                pos -= 1
            if pos != idx:
                it_ = items.pop(idx)
                items.insert(pos, it_)
        return items

    def replay(self, items):
        for it in items:
            self.add(*it)

    def _sem(self, name):
        self.nsem += 1
        return self.es.enter_context(self.nc.semaphore(f"s{self.nsem}_{name}"))

    def add(self, eng, fn, r=(), w=(), key=None):
        if self.capture is not None:
            self.capture.append((eng, fn, tuple(r), tuple(w), key))
            return -1
        i = len(self.ops)
        deps = set()
        for t in r:
            if t in self.lastw:
                deps.add(self.lastw[t])
        for t in w:
            if t in self.lastw:
                deps.add(self.lastw[t])
            for o in self.readers.get(t, {}).values():
                deps.add(o)
        if key is not None and key in self.lastkey:
            deps.add(self.lastkey[key])
        rk = eng if key is None else ('dma', i)
        for t in r:
            self.readers.setdefault(t, {})[rk] = i
        for t in w:
            self.lastw[t] = i
            self.readers[t] = {}
        if key is not None:
            self.lastkey[key] = i
        self.ops.append(dict(eng=eng, fn=fn, deps=deps, key=key, marked=False, sem=None))
        return i

    def end_phase(self):
        last = {}
        dmas = []
        for i, op in enumerate(self.ops):
            if op['key'] is None:
                if op['fn'] is not None:
                    last[op['eng']] = i
            else:
                dmas.append(i)
        deps = set(last.values()) | set(dmas)
        for e in self.eng:
            self.ops.append(dict(eng=e, fn=None, deps=set(deps), key=None, marked=False, sem=None))
        self._emit()
        for k, sc_ in self.ksem.items():
            if sc_[1] < self.DEP:
                self.free_sems.append(sc_)
        self.ksem = {}
        self.ops = []
        self.lastw = {}
        self.readers = {}
        self.lastkey = {}

    def _emit(self):
        ops = self.ops
        for op in ops:
            best = {}
            keep = []
            for d in op['deps']:
                p = ops[d]
                if p['fn'] is None:
                    continue
                if p['key'] is not None:
                    keep.append(d)
                else:
                    if p['eng'] == 'pe' and op['eng'] == 'pe' and op['key'] is None:
                        continue
                    if p['eng'] not in best or best[p['eng']] < d:
                        best[p['eng']] = d
            keep += list(best.values())
            op['deps'] = keep
            for d in keep:
                ops[d]['marked'] = True
        for op in ops:
            e = self.eng[op['eng']]
            for d in sorted(op['deps']):
                sem, val = ops[d]['sem']
                k = (op['eng'], id(sem))
                if self.waited.get(k, 0) >= val:
                    continue
                e.wait_ge(sem, val)
                self.waited[k] = val
                self.n_inst += 1
            if op['fn'] is None:
                continue
            inst = op['fn']()
            self.n_inst += 1
            if op['key'] is not None:
                sc_ = self.ksem.get(op['key'])
                if sc_ is None or sc_[1] >= self.DEP:
                    sc_ = self.free_sems.pop() if self.free_sems else [self._sem("d"), 0]
                    self.ksem[op['key']] = sc_
                sc_[1] += 1
                inst.then_inc(sc_[0], 16)
                op['sem'] = (sc_[0], 16 * sc_[1])
            elif op['marked']:
                n = self.cnt[op['eng']]
                self.cnt[op['eng']] = n + 1
                sk = (op['eng'], n // self.EP)
                if sk not in self.sems:
                    self.sems[sk] = self._sem(op['eng'])
                sem = self.sems[sk]
                inst.then_inc(sem, 1)
                op['sem'] = (sem, n % self.EP + 1)


def _r3(x):
    return x


def build_nc(Lp, Ls, dbg=False):
    nc = bass.Bass("TRN2", target_bir_lowering=False)
    Lmax = max(Lp, Ls)
    LpO = Lp // 2

    def din(name, shape, dt=F32):
        return nc.dram_tensor(name, list(shape), dt, kind="ExternalInput")

    xp = din("xp", [Lp, D]); xs = din("xs", [Ls, D])
    w_in = din("w_in", [D, 3072]); b_in_t = din("b_in_t", [128, 24])
    acw = {'P': din("acw_p", [128, 4, 3]), 'S': din("acw_s", [128, 4, 3])}
    hcw = {'P': din("hcw_p", [128, 12, 3]), 'S': din("hcw_s", [128, 12, 3])}
    hcb = din("hcb", [128, 12])
    w1aug = din("w1aug", [34, 64]); w2aug = din("w2aug", [65, 64]); freq = din("freq", [64, 1])
    w3pos = {'P': din("w3pos_p", [64, 512]), 'S': din("w3pos_s", [64, 512])}
    w3neg = {'P': din("w3neg_p", [64, 512]), 'S': din("w3neg_s", [64, 512])}
    w3zero = din("w3zero", [64, 512])
    decpos = {'P': din("decpos_p", [1, 512]), 'S': din("decpos_s", [1, 512])}
    decneg = {'P': din("decneg_p", [1, 512]), 'S': din("decneg_s", [1, 512])}
    feat = {'P': din("feat_p", [34, 2 * Lp]), 'S': din("feat_s", [34, 2 * Ls])}
    tpos = {'P': din("tpos_p", [1, 2 * Lp]), 'S': din("tpos_s", [1, 2 * Ls])}
    hyb = din("hyb", [128, 4])
    w_out = din("w_out", [D, D])
    ln1g = din("ln1g", [1, D]); ln1b = din("ln1b", [1, D]); ln2g = din("ln2g", [1, D]); ln2b = din("ln2b", [1, D])
    wq = din("wq", [D, 2048]); keys = din("keys", [2048, 128])
    u_d = din("u", [16384, D]); v_d = din("v", [16384, D])
    ident_d = din("ident", [128, 128]); jmat_d = din("jmat", [128, 128])
    iota128_d = din("iota128", [128, 128]); iota16_d = din("iota16", [128, 16])

    yp = nc.dram_tensor("yp", [LpO, D], F32, kind="ExternalOutput")
    ys = nc.dram_tensor("ys", [Ls, D], F32, kind="ExternalOutput")

    def dscr(name, shape, dt=BF16):
        return nc.dram_tensor(name, list(shape), dt, kind="Internal")

    w_in_s = dscr("w_in_s", [24, 128, 1024])
    wq_s = dscr("wq_s", [16, 128, 1024])
    uT_s = dscr("uT_s", [128, 128, 1024])
    v_s = dscr("v_s", [128, 128, 1024])
    ya_s = dscr("ya_s", [8, 128, Lmax])
    wout_s = dscr("wout_s", [8, 128, 1024])
    keysT_s = dscr("keysT_s", [128, 2048])
    kscr = dscr("kscr", [512, 2 * Lmax])

    es = ExitStack()
    S = Sched(nc, es)

    _nm = [0]

    def sb(stack, name, shape, dt=F32):
        _nm[0] += 1
        return stack.enter_context(nc.sbuf_tensor(f"{name}_{_nm[0]}", list(shape), dt))

    ps = [es.enter_context(nc.psum_tensor(f"ps{i}", [128, 512], F32)) for i in range(8)]
    PS = [f"ps{i}" for i in range(8)]

    ident = sb(es, "ident", [128, 128]); jbf = sb(es, "jbf", [128, 128], BF16)
    iota128 = sb(es, "iota128", [128, 128]); iota16 = sb(es, "iota16", [128, 16])
    iota128b = sb(es, "iota128b", [128, 128], BF16)
    b_in_sb = sb(es, "b_in_sb", [128, 24])
    acw_sb = {r: sb(es, f"acw_{r}", [128, 4, 3]) for r in 'PS'}
    hcw_sb = {r: sb(es, f"hcw_{r}", [128, 12, 3]) for r in 'PS'}
    hcb_sb = sb(es, "hcb_sb", [128, 12]); hyb_sb = sb(es, "hyb_sb", [128, 4])
    w1_sb = sb(es, "w1_sb", [34, 64]); w2_sb = sb(es, "w2_sb", [65, 64]); fq_sb = sb(es, "fq_sb", [64, 1])
    dec_sb = {}
    for r in 'PS':
        dec_sb[('pos', r)] = sb(es, f"decpos_{r}", [1, 512])
        dec_sb[('neg', r)] = sb(es, f"decneg_{r}", [1, 512])
    rnorm = sb(es, "rnorm", [128, 4])
    tmpj = sb(es, "tmpj", [128, 128])

    def dma(out, in_, r, w, key, eng='sp', slow=False):
        if slow:
            S.add(eng, lambda: S.eng[eng].dma_start(out=out, in_=in_, allow_slow_non_contiguous=True), r, w, key)
        else:
            S.add(eng, lambda: S.eng[eng].dma_start(out=out, in_=in_), r, w, key)

    def mm(out, lhsT, rhs, start, stop, r, w):
        S.add('pe', lambda: nc.tensor.matmul(out, lhsT, rhs, start=start, stop=stop), r, w)

    def tr(out, in_, r, w):
        S.add('pe', lambda: nc.tensor.transpose(out, in_, ident[:]), r + ['ident'], w)

    def act(out, in_, func, r, w, **kw):
        S.add('act', lambda: nc.scalar.activation(out=out, in_=in_, func=func, **kw), r, w)

    def cp(eng, out, in_, r, w):
        if eng == 'act':
            S.add('act', lambda: nc.scalar.copy(out=out, in_=in_), r, w)
        else:
            S.add(eng, lambda: S.eng[eng].tensor_copy(out=out, in_=in_), r, w)

    def tt(eng, out, in0, in1, op, r, w):
        S.add(eng, lambda: S.eng[eng].tensor_tensor(out=out, in0=in0, in1=in1, op=op), r, w)

    def ts(eng, out, in0, s1, s2, op0, op1, r, w):
        if op1 is None:
            S.add(eng, lambda: S.eng[eng].tensor_scalar(out=out, in0=in0, scalar1=s1, scalar2=None, op0=op0), r, w)
        else:
            S.add(eng, lambda: S.eng[eng].tensor_scalar(out=out, in0=in0, scalar1=s1, scalar2=s2, op0=op0, op1=op1), r, w)

    def stt(out, in0, scalar, in1, op0, op1, r, w):
        S.add('dve', lambda: nc.vector.scalar_tensor_tensor(out=out, in0=in0, scalar=scalar, in1=in1, op0=op0, op1=op1), r, w)

    rr = [0]

    def rot(engs):
        rr[0] += 1
        return engs[rr[0] % len(engs)]

    def ld(t, src, name, key="c0"):
        dma(t[:], src, [], [name], key)

    ld(ident, ident_d.ap(), 'ident'); ld(tmpj, jmat_d.ap(), 'tmpj', "c1")
    ld(iota128, iota128_d.ap(), 'iota128', "c2"); ld(iota16, iota16_d.ap(), 'iota16', "c3")
    ld(b_in_sb, b_in_t.ap(), 'b_in_sb')
    for r in 'PS':
        ld(acw_sb[r], acw[r].ap(), f'acw_{r}', "c1"); ld(hcw_sb[r], hcw[r].ap(), f'hcw_{r}', "c2")
        ld(dec_sb[('pos', r)], decpos[r].ap(), f'decpos_{r}', "c1"); ld(dec_sb[('neg', r)], decneg[r].ap(), f'decneg_{r}', "c2")
    ld(hcb_sb, hcb.ap(), 'hcb_sb', "c3"); ld(hyb_sb, hyb.ap(), 'hyb_sb')
    ld(w1_sb, w1aug.ap(), 'w1_sb', "c1"); ld(w2_sb, w2aug.ap(), 'w2_sb', "c2"); ld(fq_sb, freq.ap(), 'fq_sb', "c3")
    cp('dve', jbf[:], tmpj[:], ['tmpj'], ['jbf'])
    cp('dve', iota128b[:], iota128[:], ['iota128'], ['iota128b'])
    for r in 'PS':
        for s in ('pos', 'neg'):
            t = dec_sb[(s, r)]
            nm = f'dec{s}_{r}'
            act(t[:], t[:], AF.Abs, [nm], [nm])
    ts('dve', fq_sb[:], fq_sb[:], 1.0 / (2.0 * math.pi), None, ALU.mult, None, ['fq_sb'], ['fq_sb'])

    with ExitStack() as es2:
        stg = [sb(es2, f"stg{i}", [128, 3072]) for i in range(2)]
        stgb = [sb(es2, f"stgb{i}", [128, 3072], BF16) for i in range(2)]
        ust = [sb(es2, f"ust{i}", [128, D]) for i in range(3)]
        utb = [sb(es2, f"utb{i}", [128, 8, 128], BF16) for i in range(2)]
        vst = [sb(es2, f"vst{i}", [128, D]) for i in range(3)]
        vb = [sb(es2, f"vb{i}", [128, D], BF16) for i in range(2)]
        n = 0
        for (src, dst, ncol, nm) in ((w_in, w_in_s, 3072, 'w_in_s'), (wq, wq_s, 2048, 'wq_s')):
            for dk in range(8):
                sl = n % 2
                n += 1
                dma(stg[sl][:, 0:ncol], src.ap()[dk * 128:(dk + 1) * 128, :], [], [f'stg{sl}'], f"stg{sl}")
                cp(rot(['act', 'dve']), stgb[sl][:, 0:ncol], stg[sl][:, 0:ncol], [f'stg{sl}'], [f'stgb{sl}'])
                dma(dst.ap()[:, :, dk * 128:(dk + 1) * 128].rearrange("m p c -> p m c"),
                    stgb[sl][:, 0:ncol].rearrange("p (m c) -> p m c", c=128),
                    [f'stgb{sl}'], [nm], f"stgo{sl}", slow=True)
        for ck in range(8):
            sl = n % 2
            n += 1
            dma(stg[sl][:, 0:D], w_out.ap()[ck * 128:(ck + 1) * 128, :], [], [f'stg{sl}'], f"stg{sl}")
            cp(rot(['act', 'dve']), stgb[sl][:, 0:D], stg[sl][:, 0:D], [f'stg{sl}'], [f'stgb{sl}'])
            dma(wout_s.ap()[ck], stgb[sl][:, 0:D], [f'stgb{sl}'], ['wout_s'], f"stgo{sl}")
        for cg in range(4):
            sl = n % 2
            n += 1
            dma(stg[sl][:, 0:512].rearrange("p (c k) -> p c k", k=128),
                keys.ap()[cg * 512:(cg + 1) * 512, :].rearrange("(c p) k -> p c k", p=128), [], [f'stg{sl}'], f"stg{sl}")
            for c4 in range(4):
                tr(ps[cg % 2][:, c4 * 128:(c4 + 1) * 128], stg[sl][:, c4 * 128:(c4 + 1) * 128], [f'stg{sl}'], [PS[cg % 2]])
            cp('dve', stgb[sl][:, 0:512], ps[cg % 2][:, :], [PS[cg % 2]], [f'stgb{sl}'])
            dma(keysT_s.ap()[:, cg * 512:(cg + 1) * 512], stgb[sl][:, 0:512], [f'stgb{sl}'], ['keysT_s'], f"stgo{sl}")
        for i in range(128):
            s3 = i % 3
            s2 = i % 2
            dma(ust[s3][:], u_d.ap()[i * 128:(i + 1) * 128, :], [], [f'ust{s3}'], f"ust{s3}")
            for b in range(2):
                bank = 2 + 2 * s2 + b
                for c4 in range(4):
                    dk = b * 4 + c4
                    tr(ps[bank][:, c4 * 128:(c4 + 1) * 128], ust[s3][:, dk * 128:(dk + 1) * 128], [f'ust{s3}'], [PS[bank]])
                cp('act' if b == 0 else 'dve', utb[s2][:, b * 4:(b + 1) * 4, :],
                   ps[bank][:, :].rearrange("p (c k) -> p c k", k=128), [PS[bank]], [f'utb{s2}_{b}'])
            dma(uT_s.ap()[i], utb[s2][:].rearrange("p a b -> p (a b)"), [f'utb{s2}_0', f'utb{s2}_1'], ['uT_s'], f"uto{s2}")
            dma(vst[s3][:], v_d.ap()[i * 128:(i + 1) * 128, :], [], [f'vst{s3}'], f"vst{s3}", eng='pool')
            cp('pool', vb[s2][:], vst[s3][:], [f'vst{s3}'], [f'vb{s2}'])
            dma(v_s.ap()[i], vb[s2][:], [f'vb{s2}'], ['v_s'], f"vo{s2}", eng='pool')
        S.end_phase()

    def run(R, xin, yout, L, Lown):
        nS = L // 128
        nT = Lown // 128
        ntile = L // TA
        ntile_own = Lown // TA
        if True:
            with ExitStack() as esa:
                w3_sb = {}
                w3_sb[('pos', R)] = sb(esa, f"w3pos_{R}", [64, 512])
                w3_sb[('neg', R)] = sb(esa, f"w3neg_{R}", [64, 512])
                w3z_sb = sb(esa, "w3z_sb", [64, 512])
                ld(w3_sb[('pos', R)], w3pos[R].ap(), f'w3pos_{R}', "c3"); ld(w3_sb[('neg', R)], w3neg[R].ap(), f'w3neg_{R}', "c0")
                ld(w3z_sb, w3zero.ap(), 'w3z_sb')
                ft = [sb(esa, f"ft{i}", [34, 512]) for i in range(2)]
                tp = [sb(esa, f"tp{i}", [1, 512]) for i in range(2)]
                va = [sb(esa, f"va{i}", [64, 512]) for i in range(2)]
                vi = [sb(esa, f"vi{i}", [64, 512], I32) for i in range(2)]
                h1 = [sb(esa, f"h1{i}", [65, 512]) for i in range(2)]
                h2 = [sb(esa, f"h2{i}", [64, 512]) for i in range(2)]
                ew = [sb(esa, f"ew{i}", [128, 512]) for i in range(2)]
                kf = [sb(esa, f"kf{i}", [128, 512]) for i in range(2)]
                kb = [sb(esa, f"kb{i}", [128, 512], BF16) for i in range(2)]
                nacc = sb(esa, "nacc", [128, 4, 2 * L // 512])
                k0 = sb(esa, "k0", [128, 4]); k0b = sb(esa, "k0b", [128, 4], BF16)
                nrm = sb(esa, "nrm", [128, 4])
                for i in range(2):
                    S.add('pool', (lambda i=i: nc.gpsimd.memset(h1[i][64:65, :], 1.0)), [], [f'h1one{i}'])
                S.add('pool', lambda: nc.gpsimd.memset(nacc[:], 0.0), [], ['nacc'])
                nft = 2 * L // 512

                def sin_layer(sl, psb, lhs, lhs_nm, rhs, rhs_nm, out, out_nm):
                    va_, vi_ = va[sl], vi[sl]
                    vn, vin = f'va{sl}', f'vi{sl}'
                    mm(ps[psb][0:64, :], lhs, rhs, True, True, [lhs_nm] + rhs_nm, [PS[psb]])
                    ts('dve', va_[:], ps[psb][0:64, :], fq_sb[:, 0:1], 8.5, ALU.mult, ALU.add, [PS[psb], 'fq_sb'], [vn])
                    cp('dve', vi_[:], va_[:], [vn], [vin])
                    stt(va_[:], va_[:], -0.5, vi_[:], ALU.add, ALU.subtract, [vn, vin], [vn])
                    stt(va_[:], va_[:], -0.5, va_[:], ALU.is_lt, ALU.add, [vn], [vn])
                    act(out, va_[:], AF.Sin, [vn], out_nm, scale=TWO_PI)

                def L1(jt):
                    sl = jt % 2
                    j0 = jt * 512
                    dma(ft[sl][:], feat[R].ap()[:, j0:j0 + 512], [], [f'ft{sl}'], f"ft{sl}")
                    dma(tp[sl][:], tpos[R].ap()[:, j0:j0 + 512], [], [f'tp{sl}'], f"tp{sl}")
                    sin_layer(sl, 0 if sl == 0 else 6, w1_sb[:], 'w1_sb', ft[sl][:], [f'ft{sl}'], h1[sl][0:64, :], [f'h1{sl}'])

                def L2(jt):
                    sl = jt % 2
                    sin_layer(sl, 1 if sl == 0 else 7, w2_sb[:], 'w2_sb', h1[sl][:], [f'h1{sl}', f'h1one{sl}'], h2[sl][:], [f'h2{sl}'])

                def L3(jt, qs):
                    sl = jt % 2
                    j0 = jt * 512
                    dirn = 'pos' if j0 < L else 'neg'
                    H2 = h2[sl]
                    hn = f'h2{sl}'
                    for q in qs:
                        e = (jt * 4 + q) % 2
                        pa = 2 + 2 * e
                        pb_ = 3 + 2 * e
                        mm(ps[pa][:, :], w3_sb[(dirn, R)][:, q * 128:(q + 1) * 128], H2[:], True, True,
                           [f'w3{dirn}_{R}', hn], [PS[pa]])
                        mm(ps[pb_][:, :], dec_sb[(dirn, R)][:, q * 128:(q + 1) * 128], tp[sl][:], True, True,
                           [f'dec{dirn}_{R}', f'tp{sl}'], [PS[pb_]])
                        act(ew[e][:], ps[pb_][:, :], AF.Exp, [PS[pb_]], [f'ew{e}'], scale=-1.0)
                        tt('dve', kf[e][:], ps[pa][:, :], ew[e][:], ALU.mult, [PS[pa], f'ew{e}'], [f'kf{e}'])
                        if jt == 0:
                            S.add('dve', (lambda e=e: nc.vector.memset(kf[e][:, 0:1], 0.0)), [f'kf{e}'], [f'kf{e}'])
                        if j0 == L:
                            mm(ps[pb_][:, 0:1], w3z_sb[:, q * 128:(q + 1) * 128], H2[:, 0:1], True, True,
                               ['w3z_sb', hn], [PS[pb_]])
                            cp('dve', kf[e][:, 0:1], ps[pb_][:, 0:1], [PS[pb_], f'kf{e}'], [f'kf{e}'])
                            cp('dve', k0[:, q:q + 1], ps[pb_][:, 0:1], [PS[pb_]], ['k0'])
                        S.add('dve', (lambda e=e, q=q, jt=jt: nc.vector.tensor_reduce(
                            out=nacc[:, q, jt:jt + 1], in_=kf[e][:], axis=AX.X, op=ALU.add, apply_absolute_value=True)),
                            [f'kf{e}'], ['nacc'])
                        cp('pool', kb[e][:], kf[e][:], [f'kf{e}'], [f'kb{e}'])
                        dma(kscr.ap()[q * 128:(q + 1) * 128, j0:j0 + 512], kb[e][:], [f'kb{e}'], [f'kscr_{q}_{jt}'], f"kbo{e}")

                L1(0); L2(0)
                for jt in range(nft):
                    if jt + 1 < nft:
                        L1(jt + 1)
                    L3(jt, [0, 1])
                    if jt + 1 < nft:
                        L2(jt + 1)
                    L3(jt, [2, 3])
                S.add('dve', lambda: nc.vector.tensor_reduce(out=nrm[:], in_=nacc[:], axis=AX.X, op=ALU.add), ['nacc'], ['nrm'])
                S.add('dve', lambda: nc.vector.reciprocal(out=rnorm[:], in_=nrm[:]), ['nrm'], ['rnorm'])
                tt('dve', nrm[:], nrm[:], hyb_sb[:], ALU.mult, ['nrm', 'hyb_sb'], ['nrm'])
                tt('dve', k0[:], k0[:], nrm[:], ALU.add, ['k0', 'nrm'], ['k0'])
                cp('dve', k0b[:], k0[:], ['k0'], ['k0b'])
                for q in range(4):
                    dma(kscr.ap()[q * 128:(q + 1) * 128, L:L + 1], k0b[:, q:q + 1], ['k0b', f'kscr_{q}_{L // 512}'],
                        [f'kscr_{q}_{L // 512}'], "k0o", slow=True)
                S.end_phase()
        with ExitStack() as esr:
            ZT = sb(esr, f"ZT{R}", [128, nS, 512], BF16)
            x0T = sb(esr, f"x0T{R}", [128, 4, Lown], BF16)
            with ExitStack() as esa:

                xst = [sb(esa, f"xst{i}", [128, 4, D]) for i in range(1)]
                xT = [sb(esa, f"xT{i}", [128, 8, TA], BF16) for i in range(2)]
                wb = [sb(esa, f"wb{i}", [128, 8, 128], BF16) for i in range(3)]
                pbf = [sb(esa, f"pbf{i}", [128, 3, TA + 2]) for i in range(2)]
                carry = sb(esa, "carry", [128, 8, 3, 2])
                tmm = sb(esa, "tmm", [128, TA + 2]); acc = [sb(esa, f"acc{i}", [128, TA]) for i in range(3)]
                yast = [sb(esa, f"yast{i}", [128, TA], BF16) for i in range(2)]
                zst = [sb(esa, f"zst{i}", [128, TA]) for i in range(2)]
                x0c = sb(esa, "x0c", [128, 8, 1]); x0cb = sb(esa, "x0cb", [128, 8, 1], BF16)
                wcnt = [0]

                def load_w(m):
                    s = wcnt[0] % 3
                    wcnt[0] += 1
                    dma(wb[s][:].rearrange("p a b -> p (a b)"), w_in_s.ap()[m], ['w_in_s'], [f'wb{s}'], f"wb{s}")
                    return s

                S.add('pool', lambda: nc.gpsimd.memset(carry[:], 0.0), [], ['carry'])
                dma(x0c[:], AP(xin, 0, [[1, 128], [128, 8], [1, 1]]), [], ['x0c'], "x0c", slow=True)
                cp('dve', x0cb[:], x0c[:], ['x0c'], ['x0cb'])
                for m in range(24):
                    s = load_w(m)
                    for dk in range(8):
                        mm(ps[7][:, m:m + 1], wb[s][:, dk, :], x0cb[:, dk, :], dk == 0, dk == 7, [f'wb{s}', 'x0cb'], [PS[7]])
                for half in range(2):
                    tt('dve', carry[:, half * 4:(half + 1) * 4, :, 1:2].rearrange("p g j o -> p j g o"),
                       ps[7][:, half * 12:(half + 1) * 12].rearrange("p (j g o) -> p j g o", j=3, o=1),
                       b_in_sb[:, half * 12:(half + 1) * 12].rearrange("p (j g o) -> p j g o", j=3, o=1),
                       ALU.add, [PS[7], 'b_in_sb', 'carry'], ['carry'])

                for it in range(ntile):
                    own = it < ntile_own
                    t0 = it * TA
                    xs_ = 0
                    xt_ = it % 2
                    nrow = min(TA, L - 1 - t0)
                    last = (it == ntile - 1)
                    if last:
                        S.add('pool', (lambda xs_=xs_: nc.gpsimd.memset(xst[xs_][:, 3, :], 0.0)), [f'xst{xs_}'], [f'xst{xs_}'])
                        dma(xst[xs_][:, 0:3, :], xin.ap()[t0 + 1:t0 + 385, :].rearrange("(s p) d -> p s d", p=128),
                            [f'xst{xs_}'], [f'xst{xs_}'], f"xst{xs_}")
                        dma(xst[xs_][0:127, 3, :], xin.ap()[t0 + 385:t0 + 512, :], [f'xst{xs_}'], [f'xst{xs_}'], f"xst{xs_}")
                    else:
                        dma(xst[xs_][:], xin.ap()[t0 + 1:t0 + 513, :].rearrange("(s p) d -> p s d", p=128),
                            [], [f'xst{xs_}'], f"xst{xs_}")
                    for dk in range(8):
                        bank = dk % 2
                        for s4 in range(4):
                            tr(ps[bank][:, s4 * 128:(s4 + 1) * 128], xst[xs_][:, s4, dk * 128:(dk + 1) * 128], [f'xst{xs_}'], [PS[bank]])
                        cp('act' if dk % 2 == 0 else 'dve', xT[xt_][:, dk, :], ps[bank][:, :], [PS[bank]], [f'xT{xt_}'])
                    groups = list(range(8)) if own else list(range(4, 8))
                    for g in groups:
                        q = g % 4
                        isA = g < 4
                        ms = [j * 4 + q for j in range(3)] if isA else [12 + j * 4 + q for j in range(3)]
                        js = [0, 1, 2] if (isA or own) else [1, 2]
                        pbs = g % 2
                        P_ = pbf[pbs]
                        pn = f'pbf{pbs}'
                        cp('pool', P_[:, :, 0:2], carry[:, g, :, :], ['carry', pn], [pn])
                        for j in js:
                            m = ms[j]
                            s = load_w(m)
                            bank = (2 if g % 2 == 0 else 5) + (j % 3)
                            for dk in range(8):
                                mm(ps[bank][:, :], wb[s][:, dk, :], xT[xt_][:, dk, :], dk == 0, dk == 7, [f'wb{s}', f'xT{xt_}'], [PS[bank]])
                            act(P_[:, j, 2:TA + 2], ps[bank][:, :], AF.Identity, [PS[bank], 'b_in_sb', pn], [pn],
                                bias=b_in_sb[:, m:m + 1], scale=1.0)
                        if last:
                            S.add('pool', (lambda P_=P_: nc.gpsimd.memset(P_[:, :, TA + 1:TA + 2], 0.0)), [pn], [pn])
                        cp('pool', carry[:, g, :, :], P_[:, :, TA:TA + 2], [pn, 'carry'], ['carry'])
                        if isA:
                            cw = acw_sb[R]
                            cn = f'acw_{R}'
                            tt('dve', tmm[:], P_[:, 1, :], P_[:, 2, :], ALU.mult, [pn], ['tmm'])
                            ts('dve', acc[0][:], tmm[:, 0:TA], cw[:, q, 0:1], None, ALU.mult, None, ['tmm', cn], ['acc0'])
                            stt(acc[0][:], tmm[:, 1:TA + 1], cw[:, q, 1:2], acc[0][:], ALU.mult, ALU.add, ['tmm', cn, 'acc0'], ['acc0'])
                            stt(acc[0][:], tmm[:, 2:TA + 2], cw[:, q, 2:3], acc[0][:], ALU.mult, ALU.add, ['tmm', cn, 'acc0'], ['acc0'])
                            ys_ = q % 2
                            tt('dve', yast[ys_][:], P_[:, 0, 1:TA + 1], acc[0][:], ALU.mult, [pn, 'acc0'], [f'yast{ys_}'])
                            dma(ya_s.ap()[q, :, t0:t0 + TA], yast[ys_][:], [f'yast{ys_}'], ['ya_s'], f"yao{ys_}")
                        else:
                            cw = hcw_sb[R]
                            cn = f'hcw_{R}'
                            for j in js:
                                c = j * 4 + q
                                a = acc[j]
                                an = f'acc{j}'
                                ts('dve', a[:], P_[:, j, 0:TA], cw[:, c, 0:1], hcb_sb[:, c:c + 1], ALU.mult, ALU.add, [pn, cn, 'hcb_sb'], [an])
                                stt(a[:], P_[:, j, 1:TA + 1], cw[:, c, 1:2], a[:], ALU.mult, ALU.add, [pn, cn, an], [an])
                                if j == 0:
                                    stt(x0T[:, q, t0:t0 + TA], P_[:, j, 2:TA + 2], cw[:, c, 2:3], a[:], ALU.mult, ALU.add,
                                        [pn, cn, an], [f'x0T_{q}'])
                                else:
                                    stt(a[:], P_[:, j, 2:TA + 2], cw[:, c, 2:3], a[:], ALU.mult, ALU.add, [pn, cn, an], [an])
                            zs = q % 2
                            tt('dve', zst[zs][:], acc[1][:], acc[2][:], ALU.mult, ['acc1', 'acc2'], [f'zst{zs}'])
                            bank = q % 2
                            for s4 in range(4):
                                tr(ps[bank][:, s4 * 128:(s4 + 1) * 128], zst[zs][:, s4 * 128:(s4 + 1) * 128], [f'zst{zs}'], [PS[bank]])
                            cp('act', ZT[:, it * 4:(it + 1) * 4, q * 128:(q + 1) * 128],
                               ps[bank][:, :].rearrange("p (s c) -> p s c", c=128), [PS[bank]], ['ZT'])
                S.end_phase()

            with ExitStack() as esc:
                base = L + 1 - 128 * nT
                W = 128 * (nS + nT - 1)
                Wh = (W // 2 + 127) // 128 * 128 + 128
                kw = [[sb(esc, f"kw{b}_{h}", [128, Wh], BF16) for h in range(2)] for b in range(2)]
                yconv = sb(esc, "yconv", [128, 128, nT], BF16)
                yhst = [sb(esc, f"yhst{i}", [128, 512], BF16) for i in range(2)]

                def _half(dd):
                    off = L - 127 - 128 * dd - base
                    return 0 if off + 128 <= Wh else 1
                assert _half(0) == 0
                Ds = [0] + [d for d in range(-(nS - 1), nT) if d != 0 and _half(d) == 0] + \
                    [d for d in range(-(nS - 1), nT) if d != 0 and _half(d) == 1]
                nper = 512 // nT
                for q in range(4):
                    for cc in range(128):
                        cg = q * 128 + cc
                        b = cc % 2
                        for h in range(2):
                            m0 = 0 if h == 0 else W - Wh
                            dma(kw[b][h][:], AP(kscr, cg * 2 * Lmax + base + m0, [[1, 128], [1, Wh]]),
                                ['kscr'], [f'kw{b}_{h}'], f"kw{b}_{h}", eng=('sp' if h == 0 else 'pool'))
                        grp = cc // nper
                        bank = grp % 2
                        col0 = (cc % nper) * nT
                        for di, dd in enumerate(Ds):
                            off = L - 127 - 128 * dd - base
                            if off + 128 <= Wh:
                                h, o = 0, off
                            else:
                                h, o = 1, off - (W - Wh)
                            assert 0 <= o and o + 128 <= Wh
                            T0 = max(0, dd)
                            T1 = min(nT, nS + dd)
                            mm(ps[bank][:, col0 + T0:col0 + T1], kw[b][h][:, o:o + 128], ZT[:, T0 - dd:T1 - dd, cg],
                               di == 0, di == len(Ds) - 1, [f'kw{b}_{h}', 'ZT'], [PS[bank]])
                        if cc % nper == nper - 1:
                            c0 = grp * nper
                            cp('act' if grp % 2 == 0 else 'dve', yconv[:, c0:c0 + nper, :],
                               ps[bank][:, :].rearrange("p (c t) -> p c t", t=nT), [PS[bank]], ['yconv'])
                    for T4 in range(nT // 4):
                        bank = 2 + T4 % 2
                        for s4 in range(4):
                            T = T4 * 4 + s4
                            mm(ps[bank][:, s4 * 128:(s4 + 1) * 128], yconv[:, :, T], jbf[:], True, True, ['yconv', 'jbf'], [PS[bank]])
                        ys_ = T4 % 2
                        stt(yhst[ys_][:], ps[bank][:, :], rnorm[:, q:q + 1], x0T[:, q, T4 * 512:(T4 + 1) * 512],
                            ALU.mult, ALU.mult, [PS[bank], 'rnorm', f'x0T_{q}'], [f'yhst{ys_}'])
                        dma(ya_s.ap()[4 + q, :, T4 * 512:(T4 + 1) * 512], yhst[ys_][:], [f'yhst{ys_}'], ['ya_s'], f"yho{ys_}")
                S.end_phase()

            esr.close()
            with ExitStack() as ese:
                g1 = sb(ese, "g1", [128, D]); b1 = sb(ese, "b1", [128, D]); g2 = sb(ese, "g2", [128, D]); b2 = sb(ese, "b2", [128, D])
                for t, src, name in ((g1, ln1g, 'g1'), (b1, ln1b, 'b1'), (g2, ln2g, 'g2'), (b2, ln2b, 'b2')):
                    dma(t[:], AP(src, 0, [[0, 128], [1, D]]), [], [name], "c1")
                keysT = sb(ese, "keysT", [128, 16, 128], BF16)
                dma(keysT[:].rearrange("p a b -> p (a b)"), keysT_s.ap(), [], ['keysT'], "c2")
                wob = [sb(ese, f"wob{i}", [128, D], BF16) for i in range(3)]
                yat = [sb(ese, f"yat{i}", [128, 8, TT], BF16) for i in range(1)]
                xres = [sb(ese, f"xres{i}", [128, D]) for i in range(1)]
                hh1 = sb(ese, "hh1", [128, 2, D]); x1 = [sb(ese, f"x1_{i}", [128, 2, D]) for i in range(2)]
                hh2 = hh1
                x1T = [sb(ese, f"x1T{i}", [128, 8, TT], BF16) for i in range(2)]
                st6 = sb(ese, "st6", [128, 2, 6]); mv = sb(ese, "mv", [128, 2]); rstd = sb(ese, "rstd", [128, 1])
                wqb = [sb(ese, f"wqb{i}", [128, 8, 128], BF16) for i in range(3)]
                qT = sb(ese, "qT", [128, 16, TT], BF16)
                scr8 = sb(ese, "scr8", [128, 2048]); scw = sb(ese, "scw", [128, 128])
                sc = scr8
                top = sb(ese, "top", [128, 16, 16]); tidx = sb(ese, "tidx", [128, 16, 16], U32); tidf = sb(ese, "tidf", [128, 16, 16])
                cand = scr8[:].rearrange("p (h x) -> p h x", x=256); candw = sb(ese, "candw", [128, 256])
                best = sb(ese, "best", [128, 8, 16]); pos = sb(ese, "pos", [128, 8, 16], U32)
                pa_u = sb(ese, "pa_u", [128, 8, 16], U32); pb_u = sb(ese, "pb_u", [128, 8, 16], U32)
                pa_f = sb(ese, "pa_f", [128, 8, 16]); pb_f = sb(ese, "pb_f", [128, 8, 16])
                eq = scr8[:].rearrange("p (h a b) -> p h a b", a=16, b=16)
                lst = sb(ese, "lst", [128, 3, 128])
                nmax = sb(ese, "nmax", [128, 8]); ssum = sb(ese, "ssum", [128, 8]); rsum = sb(ese, "rsum", [128, 8])
                lstT = sb(ese, "lstT", [128, 3, TT], BF16)
                TB = 8
                Pm = [sb(ese, f"Pm{i}", [128, TB, 128], BF16) for i in range(2)]
                Qm = [sb(ese, f"Qm{i}", [128, TB, 128], BF16) for i in range(2)]
                Wt = sb(ese, "Wt", [128, 128, TT], BF16)
                ub = [sb(ese, f"ub{i}", [128, 8, 128], BF16) for i in range(6)]
                vbb = [sb(ese, f"vbb{i}", [128, D], BF16) for i in range(4)]
                G = [sb(ese, f"G{i}", [128, TT], BF16) for i in range(4)]
                cf = [sb(ese, f"cf{i}", [128, TT], BF16) for i in range(4)]
                wqc = [0]
                woc = [0]

                def layer_norm(src, sn, dst, dn, g, gn, b, bn):
                    for hf in range(2):
                        S.add('dve', (lambda hf=hf: nc.vector.bn_stats(out=st6[:, hf, :], in_=src[:, hf * 512:(hf + 1) * 512])), [sn], ['st6'])
                    S.add('dve', lambda: nc.vector.bn_aggr(out=mv[:], in_=st6[:].rearrange("p a b -> p (a b)")), ['st6'], ['mv'])
                    act(rstd[:], mv[:, 1:2], AF.Sqrt, ['mv'], ['rstd'], bias=float(LN_EPS), scale=1.0)
                    S.add('dve', lambda: nc.vector.reciprocal(out=rstd[:], in_=rstd[:]), ['rstd'], ['rstd'])
                    ts('dve', dst, src, mv[:, 0:1], rstd[:, 0:1], ALU.subtract, ALU.mult, [sn, 'mv', 'rstd'], [dn])
                    tt('pool', dst, dst, g[:], ALU.mult, [dn, gn], [dn])
                    tt('pool', dst, dst, b[:], ALU.add, [dn, bn], [dn])

                ntl = Lown // TT
                V = nc.vector

                def prep_stages(it):
                    t0 = it * TT
                    xb = it % 2
                    X1 = x1[xb]
                    X1T = x1T[xb]
                    xtn = f'x1T{xb}'
                    st = []

                    def s_load():
                        dma(yat[0][:], ya_s.ap()[:, :, t0:t0 + TT].rearrange("q p t -> p q t"), ['ya_s'], ['yat0'], "yat0")
                    st.append(s_load)
                    for hh in range(2):
                        def s_d(hh=hh):
                            dma(xres[0][:], xin.ap()[t0 + hh * 128:t0 + (hh + 1) * 128, :], [], ['xres0'], "xres0")
                            for ck in range(8):
                                ws = woc[0] % 3
                                woc[0] += 1
                                dma(wob[ws][:], wout_s.ap()[ck], [], [f'wob{ws}'], f"wob{ws}")
                                for dh in range(2):
                                    mm(ps[4 + dh][:, :], yat[0][:, ck, hh * 128:(hh + 1) * 128], wob[ws][:, dh * 512:(dh + 1) * 512],
                                       ck == 0, ck == 7, ['yat0', f'wob{ws}'], [PS[4 + dh]])
                            for dh in range(2):
                                stt(hh1[:, hh, dh * 512:(dh + 1) * 512], xres[0][:, dh * 512:(dh + 1) * 512], float(ALPHA), ps[4 + dh][:, :],
                                    ALU.mult, ALU.add, ['xres0', PS[4 + dh]], [f'hh1_{hh}'])
                        st.append(s_d)

                        def s_ln(hh=hh):
                            layer_norm(hh1[:, hh, :], f'hh1_{hh}', X1[:, hh, :], f'x1_{xb}_{hh}', g1, 'g1', b1, 'b1')
                        st.append(s_ln)

                        def s_tr(hh=hh):
                            for b4 in range(2):
                                bank = 4 + b4
                                for c4 in range(4):
                                    dk = b4 * 4 + c4
                                    tr(ps[bank][:, c4 * 128:(c4 + 1) * 128], X1[:, hh, dk * 128:(dk + 1) * 128], [f'x1_{xb}_{hh}'], [PS[bank]])
                                cp('act', X1T[:, b4 * 4:(b4 + 1) * 4, hh * 128:(hh + 1) * 128],
                                   ps[bank][:, :].rearrange("p (c t) -> p c t", t=128), [PS[bank]], [xtn])
                        st.append(s_tr)
                    for c4g in range(4):
                        def s_q(c4g=c4g):
                            for c4 in range(4):
                                cc = c4g * 4 + c4
                                s_ = wqc[0] % 3
                                wqc[0] += 1
                                dma(wqb[s_][:].rearrange("p a b -> p (a b)"), wq_s.ap()[cc], [], [f'wqb{s_}'], f"wqb{s_}")
                                bank = 4 + cc % 2
                                for dk in range(8):
                                    mm(ps[bank][:, 0:TT], wqb[s_][:, dk, :], X1T[:, dk, :], dk == 0, dk == 7, [f'wqb{s_}', xtn], [PS[bank]])
                                cp('act' if cc % 2 == 0 else 'dve', qT[:, cc, :], ps[bank][:, 0:TT], [PS[bank]], ['qT'])
                        st.append(s_q)
                    for hh in range(2):
                        def s_sc(hh=hh):
                            for c4g in range(4):
                                bank = 4 + c4g % 2
                                for c4 in range(4):
                                    cc = c4g * 4 + c4
                                    mm(ps[bank][:, c4 * 128:(c4 + 1) * 128], qT[:, cc, hh * 128:(hh + 1) * 128], keysT[:, cc, :], True, True,
                                       ['qT', 'keysT'], [PS[bank]])
                                cp('act' if c4g % 2 == 0 else 'dve', sc[:, c4g * 512:(c4g + 1) * 512], ps[bank][:, :], [PS[bank]], ['scr8'])
                        st.append(s_sc)
                        for gh in range(2):
                            def s_tk(gh=gh):
                                for g in range(gh * 8, gh * 8 + 8):
                                    sg = sc[:, g * 128:(g + 1) * 128]
                                    S.add('dve', (lambda g=g, sg=sg: V.max(out=top[:, g, 0:8], in_=sg)), ['scr8'], ['top'])
                                    S.add('dve', (lambda g=g, sg=sg: V.max_index(out=tidx[:, g, 0:8], in_max=top[:, g, 0:8], in_values=sg)), ['scr8', 'top'], ['tidx'])
                                    S.add('dve', (lambda g=g, sg=sg: V.match_replace(out=scw[:], in_to_replace=top[:, g, 0:8], in_values=sg, imm_value=-1e30)),
                                          ['scr8', 'top'], ['scw'])
                                    S.add('dve', (lambda g=g: V.max(out=top[:, g, 8:16], in_=scw[:])), ['scw'], ['top'])
                                    S.add('dve', (lambda g=g: V.max_index(out=tidx[:, g, 8:16], in_max=top[:, g, 8:16], in_values=scw[:])), ['scw', 'top'], ['tidx'])
                            st.append(s_tk)

                        def s_cand():
                            cp('dve', tidf[:], tidx[:], ['tidx'], ['tidf'])
                            topv = top[:].rearrange("p (h two) k -> p h two k", two=2)
                            tt('dve', cand[:].rearrange("p h (a b) -> p h a b", b=16),
                               topv[:, :, 0, :].unsqueeze(3).to_broadcast([128, 8, 16, 16]),
                               topv[:, :, 1, :].unsqueeze(2).to_broadcast([128, 8, 16, 16]), ALU.add, ['top'], ['scr8'])
                            for h in range(8):
                                ch = cand[:, h, :]
                                S.add('dve', (lambda h=h, ch=ch: V.max(out=best[:, h, 0:8], in_=ch)), ['scr8'], ['best'])
                                S.add('dve', (lambda h=h, ch=ch: V.max_index(out=pos[:, h, 0:8], in_max=best[:, h, 0:8], in_values=ch)), ['scr8', 'best'], ['pos'])
                                S.add('dve', (lambda h=h, ch=ch: V.match_replace(out=candw[:], in_to_replace=best[:, h, 0:8], in_values=ch, imm_value=-1e30)),
                                      ['scr8', 'best'], ['candw'])
                                S.add('dve', (lambda h=h: V.max(out=best[:, h, 8:16], in_=candw[:])), ['candw'], ['best'])
                                S.add('dve', (lambda h=h: V.max_index(out=pos[:, h, 8:16], in_max=best[:, h, 8:16], in_values=candw[:])), ['candw', 'best'], ['pos'])
                        st.append(s_cand)

                        def s_sm(hh=hh):
                            gv = lst[:, 2, :].rearrange("p (h k) -> p h k", k=16)
                            tt('dve', gv, best[:], best[:, :, 0:1].to_broadcast([128, 8, 16]), ALU.subtract, ['best'], ['lst_g'])
                            act(gv, gv, AF.Exp, ['lst_g'], ['lst_g'])
                            S.add('dve', lambda: V.tensor_reduce(out=ssum[:], in_=gv, axis=AX.X, op=ALU.add), ['lst_g'], ['ssum'])
                            S.add('dve', lambda: V.reciprocal(out=rsum[:], in_=ssum[:]), ['ssum'], ['rsum'])
                            tt('dve', gv, gv, rsum[:].unsqueeze(2).to_broadcast([128, 8, 16]), ALU.mult, ['lst_g', 'rsum'], ['lst_g'])
                            S.add('dve', lambda: V.tensor_single_scalar(out=pa_u[:], in_=pos[:], scalar=4, op=ALU.logical_shift_right), ['pos'], ['pa_u'])
                            S.add('dve', lambda: V.tensor_single_scalar(out=pb_u[:], in_=pos[:], scalar=15, op=ALU.bitwise_and), ['pos'], ['pb_u'])
                            cp('dve', pa_f[:], pa_u[:], ['pa_u'], ['pa_f'])
                            cp('dve', pb_f[:], pb_u[:], ['pb_u'], ['pb_f'])
                            tidv = tidf[:].rearrange("p (h two) k -> p h two k", two=2)
                            for wi, (pf, pn_) in enumerate(((pa_f, 'pa_f'), (pb_f, 'pb_f'))):
                                tt('dve', eq[:], pf[:].unsqueeze(3).to_broadcast([128, 8, 16, 16]),
                                   iota16[:].unsqueeze(1).unsqueeze(1).to_broadcast([128, 8, 16, 16]), ALU.is_equal, [pn_, 'iota16'], ['scr8'])
                                tt('dve', eq[:], eq[:], tidv[:, :, wi, :].unsqueeze(2).to_broadcast([128, 8, 16, 16]), ALU.mult, ['scr8', 'tidf'], ['scr8'])
                                S.add('dve', (lambda wi=wi: V.tensor_reduce(out=lst[:, wi, :].rearrange("p (h k) -> p h k", k=16), in_=eq[:], axis=AX.X, op=ALU.add)),
                                      ['scr8'], [f'lst_{wi}'])
                            for wi in range(3):
                                tr(ps[4][:, wi * 128:(wi + 1) * 128], lst[:, wi, :], ['lst_0', 'lst_1', 'lst_g'], [PS[4]])
                            cp('act', lstT[:, :, hh * 128:(hh + 1) * 128], ps[4][:, 0:384].rearrange("p (w t) -> p w t", t=128), [PS[4]], ['lstT'])
                        st.append(s_sm)
                    return st

                def e4(it):
                    for tb in range(TT // TB):
                        s = tb % 2
                        tsl = slice(tb * TB, (tb + 1) * TB)
                        iob = iota128b[:].unsqueeze(1).to_broadcast([128, TB, 128])
                        tt('dve', Pm[s][:], iob, lstT[:, 0, tsl].unsqueeze(2).to_broadcast([128, TB, 128]), ALU.is_equal,
                           ['iota128b', 'lstT'], [f'Pm{s}'])
                        tt('dve', Qm[s][:], iob, lstT[:, 1, tsl].unsqueeze(2).to_broadcast([128, TB, 128]), ALU.is_equal,
                           ['iota128b', 'lstT'], [f'Qm{s}'])
                        tt('pool', Qm[s][:], Qm[s][:], lstT[:, 2, tsl].unsqueeze(2).to_broadcast([128, TB, 128]), ALU.mult,
                           [f'Qm{s}', 'lstT'], [f'Qm{s}'])
                        for t4 in range(TB // 4):
                            bank = 4 + (tb * (TB // 4) + t4) % 2
                            for k4 in range(4):
                                tl = t4 * 4 + k4
                                mm(ps[bank][:, k4 * 128:(k4 + 1) * 128], Qm[s][:, tl, :], Pm[s][:, tl, :], True, True, [f'Qm{s}', f'Pm{s}'], [PS[bank]])
                            tg = tb * TB + t4 * 4
                            cp('act', Wt[:, :, tg:tg + 4], ps[bank][:, :].rearrange("p (t i) -> p i t", i=128), [PS[bank]], ['Wt'])

                def f_evac(it):
                    X1 = x1[it % 2]
                    for hh in range(2):
                        for dh in range(2):
                            bank = hh * 2 + dh
                            stt(hh2[:, hh, dh * 512:(dh + 1) * 512], X1[:, hh, dh * 512:(dh + 1) * 512], float(ALPHA), ps[bank][:, :],
                                ALU.mult, ALU.add, [f'x1_{it % 2}_{hh}', PS[bank]], [f'hh1_{hh}'])

                def f_stages(it):
                    t0 = it * TT
                    st = []
                    for hh in range(2):
                        def s_f(hh=hh):
                            layer_norm(hh2[:, hh, :], f'hh1_{hh}', hh2[:, hh, :], f'hh1_{hh}', g2, 'g2', b2, 'b2')
                            dma(yout.ap()[t0 + hh * 128:t0 + (hh + 1) * 128, :], hh2[:, hh, :], [f'hh1_{hh}'], [f'yout{R}'], f"osto{hh}")
                        st.append(s_f)
                    return st

                def load_u(i):
                    s = i % 6
                    dma(ub[s][:].rearrange("p a b -> p (a b)"), uT_s.ap()[i], [], [f'ub{s}'], f"ub{s}")

                def load_v(i):
                    s = i % 4
                    dma(vbb[s][:], v_s.ap()[i], [], [f'vbb{s}'], f"vbb{s}", eng='pool')

                def s_mm(it, i):
                    s = i % 6
                    bank = 6 + i % 2
                    c0 = 0
                    pn = PS[bank]
                    X1T = x1T[it % 2]
                    for dk in range(8):
                        mm(ps[bank][:, c0:c0 + TT], ub[s][:, dk, :], X1T[:, dk, :], dk == 0, dk == 7, [f'ub{s}', f'x1T{it % 2}'], [pn])
                    e = i % 4
                    act(G[e][:], ps[bank][:, c0:c0 + TT], AF.Gelu, [pn], [f'G{e}'])
                    tt('dve', cf[e][:], G[e][:], Wt[:, i, :], ALU.mult, [f'G{e}', 'Wt'], [f'cf{e}'])

                def o_mm(i):
                    s = i % 4
                    e = i % 4
                    for hh in range(2):
                        for dh in range(2):
                            bank = hh * 2 + dh
                            mm(ps[bank][:, :], cf[e][:, hh * 128:(hh + 1) * 128], vbb[s][:, dh * 512:(dh + 1) * 512], i == 0, i == 127,
                               [f'cf{e}', f'vbb{s}'], [PS[bank]])

                for stg_ in prep_stages(0):
                    stg_()
                e4(0)
                pend = []
                for it in range(ntl):
                    stages = list(pend)
                    if it + 1 < ntl:
                        stages += prep_stages(it + 1)
                    items = Sched.hoist(S.record(stages), 30)
                    nit = len(items)
                    wmap = {'dve': 4.0, 'pool': 3.0, 'act': 1.0, 'pe': 0.5, 'sp': 0.5}
                    cumw = []
                    acc_w = 0.0
                    for it_ in items:
                        acc_w += wmap.get(it_[0], 1.0)
                        cumw.append(acc_w)
                    totw = max(acc_w, 1.0)
                    for i in range(5):
                        load_u(i)
                    for i in range(3):
                        load_v(i)
                    s_mm(it, 0)
                    s_mm(it, 1)
                    done = 0
                    for i in range(128):
                        if i + 5 < 128:
                            load_u(i + 5)
                        if i + 3 < 128:
                            load_v(i + 3)
                        if i + 2 < 128:
                            s_mm(it, i + 2)
                        o_mm(i)
                        upto = done
                        while upto < nit and cumw[upto] <= (i + 1) * totw / 118.0:
                            upto += 1
                        if _DBG_NOINTER:
                            upto = done
                        S.replay(items[done:upto])
                        done = max(done, upto)
                    S.replay(items[done:])
                    f_evac(it)
                    if it + 1 < ntl:
                        e4(it + 1)
                    pend = f_stages(it)
                for stg_ in pend:
                    stg_()
                S.end_phase()

    run('S', xs, ys, Ls, Ls)
    run('P', xp, yp, Lp, LpO)
    es.close()
    return nc, S


def _feat_tables(L):
    n = 2 * L
    jp = np.arange(n)
    pos = np.abs(L - jp).astype(np.float64)
    pos[0] = 0.0
    t = pos / (L - 1)
    w = 2.0 * math.pi * pos / L
    f = np.linspace(1e-4, 15.0, 16)
    feat = np.concatenate([t[None], np.cos(f[:, None] * w[None]), -np.sin(f[:, None] * w[None]), np.ones((1, n))], 0)
    return feat.astype(np.float32), t[None].astype(np.float32)


_CACHE = {}


def _prep_inputs(Lp, Ls, x_prompt, x_sample, w_in, b_in, a_conv_w, h_conv_w, h_conv_b, hf_w1, hf_b1, hf_freq,
                 hf_w2, hf_b2, hf_w3, hy_decay, hy_bias, w_out, ln1_g, ln1_b, peer_wq, peer_keys,
                 peer_u, peer_v, ln2_g, ln2_b):
    c = np.ascontiguousarray
    f32 = np.float32
    featP, tposP = _feat_tables(Lp)
    featS, tposS = _feat_tables(Ls)
    acw = np.asarray(a_conv_w[0]); hcw = np.asarray(h_conv_w[0])

    def cwl(a, nch):
        return c(a.reshape(3, nch, 128).transpose(2, 1, 0))
    w3 = np.asarray(hf_w3[0]); w3d = [c(w3[:, :512]), c(w3[:, 512:])]
    dec = np.asarray(hy_decay[0]); decd = [c(dec[0:1]), c(dec[1:2])]
    shared = dict(
        w_in=c(np.asarray(w_in[0])), b_in_t=c(np.asarray(b_in[0]).reshape(24, 128).T),
        acw_s=cwl(acw, 4), hcw_s=cwl(hcw, 12), hcb=c(np.asarray(h_conv_b[0]).reshape(12, 128).T),
        w1aug=c(np.concatenate([np.asarray(hf_w1[0]), np.asarray(hf_b1[0])[None]], 0)),
        w2aug=c(np.concatenate([np.asarray(hf_w2[0]), np.asarray(hf_b2[0])[None]], 0)),
        freq=c(np.asarray(hf_freq[0]).reshape(64, 1)),
        w3pos_s=w3d[0], w3neg_s=w3d[1], w3zero=w3d[0], decpos_s=decd[0], decneg_s=decd[1],
        feat_p=featP, tpos_p=tposP, feat_s=featS, tpos_s=tposS,
        hyb=c(np.asarray(hy_bias[0]).reshape(4, 128).T), w_out=c(np.asarray(w_out[0])),
        ln1g=c(np.asarray(ln1_g[0])[None]), ln1b=c(np.asarray(ln1_b[0])[None]),
        ln2g=c(np.asarray(ln2_g[0])[None]), ln2b=c(np.asarray(ln2_b[0])[None]),
        wq=c(np.asarray(peer_wq[0])), keys=c(np.asarray(peer_keys[0]).reshape(2048, 128)),
        u=c(np.asarray(peer_u[0])), v=c(np.asarray(peer_v[0])),
        ident=np.eye(128, dtype=f32), jmat=c(np.eye(128, dtype=f32)[::-1]),
        iota128=c(np.tile(np.arange(128, dtype=f32)[None], (128, 1))),
        iota16=c(np.tile(np.arange(16, dtype=f32)[None], (128, 1))),
    )
    shared = {k: np.asarray(v, dtype=f32) for k, v in shared.items()}
    in_maps = []
    xpn = np.asarray(x_prompt); xsn = np.asarray(x_sample)
    for k in range(NCORES):
        b, half = k // 2, k % 2
        m = dict(shared)
        if half == 0:
            m['xp'] = c(xpn[b]); m['acw_p'] = shared['acw_s']; m['hcw_p'] = shared['hcw_s']
            m['w3pos_p'] = w3d[0]; m['w3neg_p'] = w3d[1]; m['decpos_p'] = decd[0]; m['decneg_p'] = decd[1]
        else:
            m['xp'] = c(xpn[b][::-1]); m['acw_p'] = cwl(acw[::-1], 4); m['hcw_p'] = cwl(hcw[::-1], 12)
            m['w3pos_p'] = w3d[1]; m['w3neg_p'] = w3d[0]; m['decpos_p'] = decd[1]; m['decneg_p'] = decd[0]
        m['xs'] = c(xsn[k])
        in_maps.append({kk: np.asarray(vv, dtype=f32) for kk, vv in m.items()})
    return in_maps


def kernel(**inputs):
    xpn = inputs['x_prompt']; xsn = inputs['x_sample']
    B, Lp, _ = xpn.shape
    Bs, Ls, _ = xsn.shape
    assert B == 4 and Bs == 8
    key = (Lp, Ls)
    if key not in _CACHE:
        _CACHE[key] = build_nc(Lp, Ls)[0]
    nc = _CACHE[key]
    in_maps = _prep_inputs(Lp, Ls, **inputs)
    res = run_bass_kernel_spmd(nc, in_maps, core_ids=list(range(NCORES)))
    yp = np.zeros((B, Lp, D), np.float32)
    ys = np.zeros((Bs, Ls, D), np.float32)
    h = Lp // 2
    for k in range(NCORES):
        r = res.results[k]
        b, half = k // 2, k % 2
        if half == 0:
            yp[b, :h] = r['yp']
        else:
            yp[b, h:] = r['yp'][::-1]
        ys[k] = r['ys']
    return (yp, ys)
```

```python
import math
from contextlib import ExitStack
import numpy as np
import concourse.bass as bass
import concourse.mybir as mybir
from concourse.bass_utils import run_bass_kernel_spmd
from concourse.bass_types import AP

F32 = mybir.dt.float32
BF16 = mybir.dt.bfloat16
U32 = mybir.dt.uint32
I32 = mybir.dt.int32
AF = mybir.ActivationFunctionType
ALU = mybir.AluOpType
AX = mybir.AxisListType

D = 1024
NCORES = 8
ALPHA = 2.0 ** 0.25
LN_EPS = 1e-5
TWO_PI = 6.283185
TA = 512
TT = 256
import os
_DBG_NOINTER = bool(os.environ.get('NOINTER'))


class Sched:
    EP = 30000
    DEP = 1800

    def __init__(self, nc, es):
        self.nc = nc
        self.es = es
        self.eng = {'pe': nc.tensor, 'act': nc.scalar, 'dve': nc.vector, 'pool': nc.gpsimd, 'sp': nc.sync}
        self.ops = []
        self.lastw = {}
        self.readers = {}
        self.lastkey = {}
        self.cnt = {e: 0 for e in self.eng}
        self.sems = {}
        self.ksem = {}
        self.free_sems = []
        self.waited = {}
        self.nsem = 0
        self.n_inst = 0
        self.capture = None

    def record(self, fns):
        self.capture = []
        for f in fns:
            f()
        out = self.capture
        self.capture = None
        return out

    @staticmethod
    def hoist(items, H):
        items = list(items)
        for idx in range(len(items)):
            eng, fn, r, w, key = items[idx]
            if key is None:
                continue
            toks = set(w) | set(r)
            lim = max(0, idx - H)
            pos = idx
            while pos > lim:
                pe, pf, pr, pw, pk = items[pos - 1]
                if pk is not None and (pk == key or pe == eng):
                    break
                if (set(pw) & toks) or (set(pr) & set(w)):
                    break
                pos -= 1
            if pos != idx:
                it_ = items.pop(idx)
                items.insert(pos, it_)
        return items

    def replay(self, items):
        for it in items:
            self.add(*it)

    def _sem(self, name):
        self.nsem += 1
        return self.es.enter_context(self.nc.semaphore(f"s{self.nsem}_{name}"))

    def add(self, eng, fn, r=(), w=(), key=None):
        if self.capture is not None:
            self.capture.append((eng, fn, tuple(r), tuple(w), key))
            return -1
        i = len(self.ops)
        deps = set()
        for t in r:
            if t in self.lastw:
                deps.add(self.lastw[t])
        for t in w:
            if t in self.lastw:
                deps.add(self.lastw[t])
            for o in self.readers.get(t, {}).values():
                deps.add(o)
        if key is not None and key in self.lastkey:
            deps.add(self.lastkey[key])
        rk = eng if key is None else ('dma', i)
        for t in r:
            self.readers.setdefault(t, {})[rk] = i
        for t in w:
            self.lastw[t] = i
            self.readers[t] = {}
        if key is not None:
            self.lastkey[key] = i
        self.ops.append(dict(eng=eng, fn=fn, deps=deps, key=key, marked=False, sem=None))
        return i

    def end_phase(self):
        last = {}
        dmas = []
        for i, op in enumerate(self.ops):
            if op['key'] is None:
                if op['fn'] is not None:
                    last[op['eng']] = i
            else:
                dmas.append(i)
        deps = set(last.values()) | set(dmas)
        for e in self.eng:
            self.ops.append(dict(eng=e, fn=None, deps=set(deps), key=None, marked=False, sem=None))
        self._emit()
        for k, sc_ in self.ksem.items():
            if sc_[1] < self.DEP:
                self.free_sems.append(sc_)
        self.ksem = {}
        self.ops = []
        self.lastw = {}
        self.readers = {}
        self.lastkey = {}

    def _emit(self):
        ops = self.ops
        for op in ops:
            best = {}
            keep = []
            for d in op['deps']:
                p = ops[d]
                if p['fn'] is None:
                    continue
                if p['key'] is not None:
                    keep.append(d)
                else:
                    if p['eng'] == 'pe' and op['eng'] == 'pe' and op['key'] is None:
                        continue
                    if p['eng'] not in best or best[p['eng']] < d:
                        best[p['eng']] = d
            keep += list(best.values())
            op['deps'] = keep
            for d in keep:
                ops[d]['marked'] = True
        for op in ops:
            e = self.eng[op['eng']]
            for d in sorted(op['deps']):
                sem, val = ops[d]['sem']
                k = (op['eng'], id(sem))
                if self.waited.get(k, 0) >= val:
                    continue
                e.wait_ge(sem, val)
                self.waited[k] = val
                self.n_inst += 1
            if op['fn'] is None:
                continue
            inst = op['fn']()
            self.n_inst += 1
            if op['key'] is not None:
                sc_ = self.ksem.get(op['key'])
                if sc_ is None or sc_[1] >= self.DEP:
                    sc_ = self.free_sems.pop() if self.free_sems else [self._sem("d"), 0]
                    self.ksem[op['key']] = sc_
                sc_[1] += 1
                inst.then_inc(sc_[0], 16)
                op['sem'] = (sc_[0], 16 * sc_[1])
            elif op['marked']:
                n = self.cnt[op['eng']]
                self.cnt[op['eng']] = n + 1
                sk = (op['eng'], n // self.EP)
                if sk not in self.sems:
                    self.sems[sk] = self._sem(op['eng'])
                sem = self.sems[sk]
                inst.then_inc(sem, 1)
                op['sem'] = (sem, n % self.EP + 1)


def _r3(x):
    return x


def build_nc(Lp, Ls, dbg=False):
    nc = bass.Bass("TRN2", target_bir_lowering=False)
    Lmax = max(Lp, Ls)
    LpO = Lp // 2

    def din(name, shape, dt=F32):
        return nc.dram_tensor(name, list(shape), dt, kind="ExternalInput")

    xp = din("xp", [Lp, D]); xs = din("xs", [Ls, D])
    w_in = din("w_in", [D, 3072]); b_in_t = din("b_in_t", [128, 24])
    acw = {'P': din("acw_p", [128, 4, 3]), 'S': din("acw_s", [128, 4, 3])}
    hcw = {'P': din("hcw_p", [128, 12, 3]), 'S': din("hcw_s", [128, 12, 3])}
    hcb = din("hcb", [128, 12])
    w1aug = din("w1aug", [34, 64]); w2aug = din("w2aug", [65, 64]); freq = din("freq", [64, 1])
    w3pos = {'P': din("w3pos_p", [64, 512]), 'S': din("w3pos_s", [64, 512])}
    w3neg = {'P': din("w3neg_p", [64, 512]), 'S': din("w3neg_s", [64, 512])}
    w3zero = din("w3zero", [64, 512])
    decpos = {'P': din("decpos_p", [1, 512]), 'S': din("decpos_s", [1, 512])}
    decneg = {'P': din("decneg_p", [1, 512]), 'S': din("decneg_s", [1, 512])}
    feat = {'P': din("feat_p", [34, 2 * Lp]), 'S': din("feat_s", [34, 2 * Ls])}
    tpos = {'P': din("tpos_p", [1, 2 * Lp]), 'S': din("tpos_s", [1, 2 * Ls])}
    hyb = din("hyb", [128, 4])
    w_out = din("w_out", [D, D])
    ln1g = din("ln1g", [1, D]); ln1b = din("ln1b", [1, D]); ln2g = din("ln2g", [1, D]); ln2b = din("ln2b", [1, D])
    wq = din("wq", [D, 2048]); keys = din("keys", [2048, 128])
    u_d = din("u", [16384, D]); v_d = din("v", [16384, D])
    ident_d = din("ident", [128, 128]); jmat_d = din("jmat", [128, 128])
    iota128_d = din("iota128", [128, 128]); iota16_d = din("iota16", [128, 16])

    yp = nc.dram_tensor("yp", [LpO, D], F32, kind="ExternalOutput")
    ys = nc.dram_tensor("ys", [Ls, D], F32, kind="ExternalOutput")

    def dscr(name, shape, dt=BF16):
        return nc.dram_tensor(name, list(shape), dt, kind="Internal")

    w_in_s = dscr("w_in_s", [24, 128, 1024])
    wq_s = dscr("wq_s", [16, 128, 1024])
    uT_s = dscr("uT_s", [128, 128, 1024])
    v_s = dscr("v_s", [128, 128, 1024])
    ya_s = dscr("ya_s", [8, 128, Lmax])
    wout_s = dscr("wout_s", [8, 128, 1024])
    keysT_s = dscr("keysT_s", [128, 2048])
    kscr = dscr("kscr", [512, 2 * Lmax])

    es = ExitStack()
    S = Sched(nc, es)

    _nm = [0]

    def sb(stack, name, shape, dt=F32):
        _nm[0] += 1
        return stack.enter_context(nc.sbuf_tensor(f"{name}_{_nm[0]}", list(shape), dt))

    ps = [es.enter_context(nc.psum_tensor(f"ps{i}", [128, 512], F32)) for i in range(8)]
    PS = [f"ps{i}" for i in range(8)]

    ident = sb(es, "ident", [128, 128]); jbf = sb(es, "jbf", [128, 128], BF16)
    iota128 = sb(es, "iota128", [128, 128]); iota16 = sb(es, "iota16", [128, 16])
    iota128b = sb(es, "iota128b", [128, 128], BF16)
    b_in_sb = sb(es, "b_in_sb", [128, 24])
    acw_sb = {r: sb(es, f"acw_{r}", [128, 4, 3]) for r in 'PS'}
    hcw_sb = {r: sb(es, f"hcw_{r}", [128, 12, 3]) for r in 'PS'}
    hcb_sb = sb(es, "hcb_sb", [128, 12]); hyb_sb = sb(es, "hyb_sb", [128, 4])
    w1_sb = sb(es, "w1_sb", [34, 64]); w2_sb = sb(es, "w2_sb", [65, 64]); fq_sb = sb(es, "fq_sb", [64, 1])
    dec_sb = {}
    for r in 'PS':
        dec_sb[('pos', r)] = sb(es, f"decpos_{r}", [1, 512])
        dec_sb[('neg', r)] = sb(es, f"decneg_{r}", [1, 512])
    rnorm = sb(es, "rnorm", [128, 4])
    tmpj = sb(es, "tmpj", [128, 128])

    def dma(out, in_, r, w, key, eng='sp', slow=False):
        if slow:
            S.add(eng, lambda: S.eng[eng].dma_start(out=out, in_=in_, allow_slow_non_contiguous=True), r, w, key)
        else:
            S.add(eng, lambda: S.eng[eng].dma_start(out=out, in_=in_), r, w, key)

    def mm(out, lhsT, rhs, start, stop, r, w):
        S.add('pe', lambda: nc.tensor.matmul(out, lhsT, rhs, start=start, stop=stop), r, w)

    def tr(out, in_, r, w):
        S.add('pe', lambda: nc.tensor.transpose(out, in_, ident[:]), r + ['ident'], w)

    def act(out, in_, func, r, w, **kw):
        S.add('act', lambda: nc.scalar.activation(out=out, in_=in_, func=func, **kw), r, w)

    def cp(eng, out, in_, r, w):
        if eng == 'act':
            S.add('act', lambda: nc.scalar.copy(out=out, in_=in_), r, w)
        else:
            S.add(eng, lambda: S.eng[eng].tensor_copy(out=out, in_=in_), r, w)

    def tt(eng, out, in0, in1, op, r, w):
        S.add(eng, lambda: S.eng[eng].tensor_tensor(out=out, in0=in0, in1=in1, op=op), r, w)

    def ts(eng, out, in0, s1, s2, op0, op1, r, w):
        if op1 is None:
            S.add(eng, lambda: S.eng[eng].tensor_scalar(out=out, in0=in0, scalar1=s1, scalar2=None, op0=op0), r, w)
        else:
            S.add(eng, lambda: S.eng[eng].tensor_scalar(out=out, in0=in0, scalar1=s1, scalar2=s2, op0=op0, op1=op1), r, w)

    def stt(out, in0, scalar, in1, op0, op1, r, w):
        S.add('dve', lambda: nc.vector.scalar_tensor_tensor(out=out, in0=in0, scalar=scalar, in1=in1, op0=op0, op1=op1), r, w)

    rr = [0]

    def rot(engs):
        rr[0] += 1
        return engs[rr[0] % len(engs)]

    def ld(t, src, name, key="c0"):
        dma(t[:], src, [], [name], key)

    ld(ident, ident_d.ap(), 'ident'); ld(tmpj, jmat_d.ap(), 'tmpj', "c1")
    ld(iota128, iota128_d.ap(), 'iota128', "c2"); ld(iota16, iota16_d.ap(), 'iota16', "c3")
    ld(b_in_sb, b_in_t.ap(), 'b_in_sb')
    for r in 'PS':
        ld(acw_sb[r], acw[r].ap(), f'acw_{r}', "c1"); ld(hcw_sb[r], hcw[r].ap(), f'hcw_{r}', "c2")
        ld(dec_sb[('pos', r)], decpos[r].ap(), f'decpos_{r}', "c1"); ld(dec_sb[('neg', r)], decneg[r].ap(), f'decneg_{r}', "c2")
    ld(hcb_sb, hcb.ap(), 'hcb_sb', "c3"); ld(hyb_sb, hyb.ap(), 'hyb_sb')
    ld(w1_sb, w1aug.ap(), 'w1_sb', "c1"); ld(w2_sb, w2aug.ap(), 'w2_sb', "c2"); ld(fq_sb, freq.ap(), 'fq_sb', "c3")
    cp('dve', jbf[:], tmpj[:], ['tmpj'], ['jbf'])
    cp('dve', iota128b[:], iota128[:], ['iota128'], ['iota128b'])
    for r in 'PS':
        for s in ('pos', 'neg'):
            t = dec_sb[(s, r)]
            nm = f'dec{s}_{r}'
            act(t[:], t[:], AF.Abs, [nm], [nm])
    ts('dve', fq_sb[:], fq_sb[:], 1.0 / (2.0 * math.pi), None, ALU.mult, None, ['fq_sb'], ['fq_sb'])

    with ExitStack() as es2:
        stg = [sb(es2, f"stg{i}", [128, 3072]) for i in range(2)]
        stgb = [sb(es2, f"stgb{i}", [128, 3072], BF16) for i in range(2)]
        ust = [sb(es2, f"ust{i}", [128, D]) for i in range(3)]
        utb = [sb(es2, f"utb{i}", [128, 8, 128], BF16) for i in range(2)]
        vst = [sb(es2, f"vst{i}", [128, D]) for i in range(3)]
        vb = [sb(es2, f"vb{i}", [128, D], BF16) for i in range(2)]
        n = 0
        for (src, dst, ncol, nm) in ((w_in, w_in_s, 3072, 'w_in_s'), (wq, wq_s, 2048, 'wq_s')):
            for dk in range(8):
                sl = n % 2
                n += 1
                dma(stg[sl][:, 0:ncol], src.ap()[dk * 128:(dk + 1) * 128, :], [], [f'stg{sl}'], f"stg{sl}")
                cp(rot(['act', 'dve']), stgb[sl][:, 0:ncol], stg[sl][:, 0:ncol], [f'stg{sl}'], [f'stgb{sl}'])
                dma(dst.ap()[:, :, dk * 128:(dk + 1) * 128].rearrange("m p c -> p m c"),
                    stgb[sl][:, 0:ncol].rearrange("p (m c) -> p m c", c=128),
                    [f'stgb{sl}'], [nm], f"stgo{sl}", slow=True)
        for ck in range(8):
            sl = n % 2
            n += 1
            dma(stg[sl][:, 0:D], w_out.ap()[ck * 128:(ck + 1) * 128, :], [], [f'stg{sl}'], f"stg{sl}")
            cp(rot(['act', 'dve']), stgb[sl][:, 0:D], stg[sl][:, 0:D], [f'stg{sl}'], [f'stgb{sl}'])
            dma(wout_s.ap()[ck], stgb[sl][:, 0:D], [f'stgb{sl}'], ['wout_s'], f"stgo{sl}")
        for cg in range(4):
            sl = n % 2
            n += 1
            dma(stg[sl][:, 0:512].rearrange("p (c k) -> p c k", k=128),
                keys.ap()[cg * 512:(cg + 1) * 512, :].rearrange("(c p) k -> p c k", p=128), [], [f'stg{sl}'], f"stg{sl}")
            for c4 in range(4):
                tr(ps[cg % 2][:, c4 * 128:(c4 + 1) * 128], stg[sl][:, c4 * 128:(c4 + 1) * 128], [f'stg{sl}'], [PS[cg % 2]])
            cp('dve', stgb[sl][:, 0:512], ps[cg % 2][:, :], [PS[cg % 2]], [f'stgb{sl}'])
            dma(keysT_s.ap()[:, cg * 512:(cg + 1) * 512], stgb[sl][:, 0:512], [f'stgb{sl}'], ['keysT_s'], f"stgo{sl}")
        for i in range(128):
            s3 = i % 3
            s2 = i % 2
            dma(ust[s3][:], u_d.ap()[i * 128:(i + 1) * 128, :], [], [f'ust{s3}'], f"ust{s3}")
            for b in range(2):
                bank = 2 + 2 * s2 + b
                for c4 in range(4):
                    dk = b * 4 + c4
                    tr(ps[bank][:, c4 * 128:(c4 + 1) * 128], ust[s3][:, dk * 128:(dk + 1) * 128], [f'ust{s3}'], [PS[bank]])
                cp('act' if b == 0 else 'dve', utb[s2][:, b * 4:(b + 1) * 4, :],
                   ps[bank][:, :].rearrange("p (c k) -> p c k", k=128), [PS[bank]], [f'utb{s2}_{b}'])
            dma(uT_s.ap()[i], utb[s2][:].rearrange("p a b -> p (a b)"), [f'utb{s2}_0', f'utb{s2}_1'], ['uT_s'], f"uto{s2}")
            dma(vst[s3][:], v_d.ap()[i * 128:(i + 1) * 128, :], [], [f'vst{s3}'], f"vst{s3}", eng='pool')
            cp('pool', vb[s2][:], vst[s3][:], [f'vst{s3}'], [f'vb{s2}'])
            dma(v_s.ap()[i], vb[s2][:], [f'vb{s2}'], ['v_s'], f"vo{s2}", eng='pool')
        S.end_phase()

    def run(R, xin, yout, L, Lown):
        nS = L // 128
        nT = Lown // 128
        ntile = L // TA
        ntile_own = Lown // TA
        if True:
            with ExitStack() as esa:
                w3_sb = {}
                w3_sb[('pos', R)] = sb(esa, f"w3pos_{R}", [64, 512])
                w3_sb[('neg', R)] = sb(esa, f"w3neg_{R}", [64, 512])
                w3z_sb = sb(esa, "w3z_sb", [64, 512])
                ld(w3_sb[('pos', R)], w3pos[R].ap(), f'w3pos_{R}', "c3"); ld(w3_sb[('neg', R)], w3neg[R].ap(), f'w3neg_{R}', "c0")
                ld(w3z_sb, w3zero.ap(), 'w3z_sb')
                ft = [sb(esa, f"ft{i}", [34, 512]) for i in range(2)]
                tp = [sb(esa, f"tp{i}", [1, 512]) for i in range(2)]
                va = [sb(esa, f"va{i}", [64, 512]) for i in range(2)]
                vi = [sb(esa, f"vi{i}", [64, 512], I32) for i in range(2)]
                h1 = [sb(esa, f"h1{i}", [65, 512]) for i in range(2)]
                h2 = [sb(esa, f"h2{i}", [64, 512]) for i in range(2)]
                ew = [sb(esa, f"ew{i}", [128, 512]) for i in range(2)]
                kf = [sb(esa, f"kf{i}", [128, 512]) for i in range(2)]
                kb = [sb(esa, f"kb{i}", [128, 512], BF16) for i in range(2)]
                nacc = sb(esa, "nacc", [128, 4, 2 * L // 512])
                k0 = sb(esa, "k0", [128, 4]); k0b = sb(esa, "k0b", [128, 4], BF16)
                nrm = sb(esa, "nrm", [128, 4])
                for i in range(2):
                    S.add('pool', (lambda i=i: nc.gpsimd.memset(h1[i][64:65, :], 1.0)), [], [f'h1one{i}'])
                S.add('pool', lambda: nc.gpsimd.memset(nacc[:], 0.0), [], ['nacc'])
                nft = 2 * L // 512

                def sin_layer(sl, psb, lhs, lhs_nm, rhs, rhs_nm, out, out_nm):
                    va_, vi_ = va[sl], vi[sl]
                    vn, vin = f'va{sl}', f'vi{sl}'
                    mm(ps[psb][0:64, :], lhs, rhs, True, True, [lhs_nm] + rhs_nm, [PS[psb]])
                    ts('dve', va_[:], ps[psb][0:64, :], fq_sb[:, 0:1], 8.5, ALU.mult, ALU.add, [PS[psb], 'fq_sb'], [vn])
                    cp('dve', vi_[:], va_[:], [vn], [vin])
                    stt(va_[:], va_[:], -0.5, vi_[:], ALU.add, ALU.subtract, [vn, vin], [vn])
                    stt(va_[:], va_[:], -0.5, va_[:], ALU.is_lt, ALU.add, [vn], [vn])
                    act(out, va_[:], AF.Sin, [vn], out_nm, scale=TWO_PI)

                def L1(jt):
                    sl = jt % 2
                    j0 = jt * 512
                    dma(ft[sl][:], feat[R].ap()[:, j0:j0 + 512], [], [f'ft{sl}'], f"ft{sl}")
                    dma(tp[sl][:], tpos[R].ap()[:, j0:j0 + 512], [], [f'tp{sl}'], f"tp{sl}")
                    sin_layer(sl, 0 if sl == 0 else 6, w1_sb[:], 'w1_sb', ft[sl][:], [f'ft{sl}'], h1[sl][0:64, :], [f'h1{sl}'])

                def L2(jt):
                    sl = jt % 2
                    sin_layer(sl, 1 if sl == 0 else 7, w2_sb[:], 'w2_sb', h1[sl][:], [f'h1{sl}', f'h1one{sl}'], h2[sl][:], [f'h2{sl}'])

                def L3(jt, qs):
                    sl = jt % 2
                    j0 = jt * 512
                    dirn = 'pos' if j0 < L else 'neg'
                    H2 = h2[sl]
                    hn = f'h2{sl}'
                    for q in qs:
                        e = (jt * 4 + q) % 2
                        pa = 2 + 2 * e
                        pb_ = 3 + 2 * e
                        mm(ps[pa][:, :], w3_sb[(dirn, R)][:, q * 128:(q + 1) * 128], H2[:], True, True,
                           [f'w3{dirn}_{R}', hn], [PS[pa]])
                        mm(ps[pb_][:, :], dec_sb[(dirn, R)][:, q * 128:(q + 1) * 128], tp[sl][:], True, True,
                           [f'dec{dirn}_{R}', f'tp{sl}'], [PS[pb_]])
                        act(ew[e][:], ps[pb_][:, :], AF.Exp, [PS[pb_]], [f'ew{e}'], scale=-1.0)
                        tt('dve', kf[e][:], ps[pa][:, :], ew[e][:], ALU.mult, [PS[pa], f'ew{e}'], [f'kf{e}'])
                        if jt == 0:
                            S.add('dve', (lambda e=e: nc.vector.memset(kf[e][:, 0:1], 0.0)), [f'kf{e}'], [f'kf{e}'])
                        if j0 == L:
                            mm(ps[pb_][:, 0:1], w3z_sb[:, q * 128:(q + 1) * 128], H2[:, 0:1], True, True,
                               ['w3z_sb', hn], [PS[pb_]])
                            cp('dve', kf[e][:, 0:1], ps[pb_][:, 0:1], [PS[pb_], f'kf{e}'], [f'kf{e}'])
                            cp('dve', k0[:, q:q + 1], ps[pb_][:, 0:1], [PS[pb_]], ['k0'])
                        S.add('dve', (lambda e=e, q=q, jt=jt: nc.vector.tensor_reduce(
                            out=nacc[:, q, jt:jt + 1], in_=kf[e][:], axis=AX.X, op=ALU.add, apply_absolute_value=True)),
                            [f'kf{e}'], ['nacc'])
                        cp('pool', kb[e][:], kf[e][:], [f'kf{e}'], [f'kb{e}'])
                        dma(kscr.ap()[q * 128:(q + 1) * 128, j0:j0 + 512], kb[e][:], [f'kb{e}'], [f'kscr_{q}_{jt}'], f"kbo{e}")

                L1(0); L2(0)
                for jt in range(nft):
                    if jt + 1 < nft:
                        L1(jt + 1)
                    L3(jt, [0, 1])
                    if jt + 1 < nft:
                        L2(jt + 1)
                    L3(jt, [2, 3])
                S.add('dve', lambda: nc.vector.tensor_reduce(out=nrm[:], in_=nacc[:], axis=AX.X, op=ALU.add), ['nacc'], ['nrm'])
                S.add('dve', lambda: nc.vector.reciprocal(out=rnorm[:], in_=nrm[:]), ['nrm'], ['rnorm'])
                tt('dve', nrm[:], nrm[:], hyb_sb[:], ALU.mult, ['nrm', 'hyb_sb'], ['nrm'])
                tt('dve', k0[:], k0[:], nrm[:], ALU.add, ['k0', 'nrm'], ['k0'])
                cp('dve', k0b[:], k0[:], ['k0'], ['k0b'])
                for q in range(4):
                    dma(kscr.ap()[q * 128:(q + 1) * 128, L:L + 1], k0b[:, q:q + 1], ['k0b', f'kscr_{q}_{L // 512}'],
                        [f'kscr_{q}_{L // 512}'], "k0o", slow=True)
                S.end_phase()
        with ExitStack() as esr:
            ZT = sb(esr, f"ZT{R}", [128, nS, 512], BF16)
            x0T = sb(esr, f"x0T{R}", [128, 4, Lown], BF16)
            with ExitStack() as esa:

                xst = [sb(esa, f"xst{i}", [128, 4, D]) for i in range(1)]
                xT = [sb(esa, f"xT{i}", [128, 8, TA], BF16) for i in range(2)]
                wb = [sb(esa, f"wb{i}", [128, 8, 128], BF16) for i in range(3)]
                pbf = [sb(esa, f"pbf{i}", [128, 3, TA + 2]) for i in range(2)]
                carry = sb(esa, "carry", [128, 8, 3, 2])
                tmm = sb(esa, "tmm", [128, TA + 2]); acc = [sb(esa, f"acc{i}", [128, TA]) for i in range(3)]
                yast = [sb(esa, f"yast{i}", [128, TA], BF16) for i in range(2)]
                zst = [sb(esa, f"zst{i}", [128, TA]) for i in range(2)]
                x0c = sb(esa, "x0c", [128, 8, 1]); x0cb = sb(esa, "x0cb", [128, 8, 1], BF16)
                wcnt = [0]

                def load_w(m):
                    s = wcnt[0] % 3
                    wcnt[0] += 1
                    dma(wb[s][:].rearrange("p a b -> p (a b)"), w_in_s.ap()[m], ['w_in_s'], [f'wb{s}'], f"wb{s}")
                    return s

                S.add('pool', lambda: nc.gpsimd.memset(carry[:], 0.0), [], ['carry'])
                dma(x0c[:], AP(xin, 0, [[1, 128], [128, 8], [1, 1]]), [], ['x0c'], "x0c", slow=True)
                cp('dve', x0cb[:], x0c[:], ['x0c'], ['x0cb'])
                for m in range(24):
                    s = load_w(m)
                    for dk in range(8):
                        mm(ps[7][:, m:m + 1], wb[s][:, dk, :], x0cb[:, dk, :], dk == 0, dk == 7, [f'wb{s}', 'x0cb'], [PS[7]])
                for half in range(2):
                    tt('dve', carry[:, half * 4:(half + 1) * 4, :, 1:2].rearrange("p g j o -> p j g o"),
                       ps[7][:, half * 12:(half + 1) * 12].rearrange("p (j g o) -> p j g o", j=3, o=1),
                       b_in_sb[:, half * 12:(half + 1) * 12].rearrange("p (j g o) -> p j g o", j=3, o=1),
                       ALU.add, [PS[7], 'b_in_sb', 'carry'], ['carry'])

                for it in range(ntile):
                    own = it < ntile_own
                    t0 = it * TA
                    xs_ = 0
                    xt_ = it % 2
                    nrow = min(TA, L - 1 - t0)
                    last = (it == ntile - 1)
                    if last:
                        S.add('pool', (lambda xs_=xs_: nc.gpsimd.memset(xst[xs_][:, 3, :], 0.0)), [f'xst{xs_}'], [f'xst{xs_}'])
                        dma(xst[xs_][:, 0:3, :], xin.ap()[t0 + 1:t0 + 385, :].rearrange("(s p) d -> p s d", p=128),
                            [f'xst{xs_}'], [f'xst{xs_}'], f"xst{xs_}")
                        dma(xst[xs_][0:127, 3, :], xin.ap()[t0 + 385:t0 + 512, :], [f'xst{xs_}'], [f'xst{xs_}'], f"xst{xs_}")
                    else:
                        dma(xst[xs_][:], xin.ap()[t0 + 1:t0 + 513, :].rearrange("(s p) d -> p s d", p=128),
                            [], [f'xst{xs_}'], f"xst{xs_}")
                    for dk in range(8):
                        bank = dk % 2
                        for s4 in range(4):
                            tr(ps[bank][:, s4 * 128:(s4 + 1) * 128], xst[xs_][:, s4, dk * 128:(dk + 1) * 128], [f'xst{xs_}'], [PS[bank]])
                        cp('act' if dk % 2 == 0 else 'dve', xT[xt_][:, dk, :], ps[bank][:, :], [PS[bank]], [f'xT{xt_}'])
                    groups = list(range(8)) if own else list(range(4, 8))
                    for g in groups:
                        q = g % 4
                        isA = g < 4
                        ms = [j * 4 + q for j in range(3)] if isA else [12 + j * 4 + q for j in range(3)]
                        js = [0, 1, 2] if (isA or own) else [1, 2]
                        pbs = g % 2
                        P_ = pbf[pbs]
                        pn = f'pbf{pbs}'
                        cp('pool', P_[:, :, 0:2], carry[:, g, :, :], ['carry', pn], [pn])
                        for j in js:
                            m = ms[j]
                            s = load_w(m)
                            bank = (2 if g % 2 == 0 else 5) + (j % 3)
                            for dk in range(8):
                                mm(ps[bank][:, :], wb[s][:, dk, :], xT[xt_][:, dk, :], dk == 0, dk == 7, [f'wb{s}', f'xT{xt_}'], [PS[bank]])
                            act(P_[:, j, 2:TA + 2], ps[bank][:, :], AF.Identity, [PS[bank], 'b_in_sb', pn], [pn],
                                bias=b_in_sb[:, m:m + 1], scale=1.0)
                        if last:
                            S.add('pool', (lambda P_=P_: nc.gpsimd.memset(P_[:, :, TA + 1:TA + 2], 0.0)), [pn], [pn])
                        cp('pool', carry[:, g, :, :], P_[:, :, TA:TA + 2], [pn, 'carry'], ['carry'])
                        if isA:
                            cw = acw_sb[R]
                            cn = f'acw_{R}'
                            tt('dve', tmm[:], P_[:, 1, :], P_[:, 2, :], ALU.mult, [pn], ['tmm'])
                            ts('dve', acc[0][:], tmm[:, 0:TA], cw[:, q, 0:1], None, ALU.mult, None, ['tmm', cn], ['acc0'])
                            stt(acc[0][:], tmm[:, 1:TA + 1], cw[:, q, 1:2], acc[0][:], ALU.mult, ALU.add, ['tmm', cn, 'acc0'], ['acc0'])
                            stt(acc[0][:], tmm[:, 2:TA + 2], cw[:, q, 2:3], acc[0][:], ALU.mult, ALU.add, ['tmm', cn, 'acc0'], ['acc0'])
                            ys_ = q % 2
                            tt('dve', yast[ys_][:], P_[:, 0, 1:TA + 1], acc[0][:], ALU.mult, [pn, 'acc0'], [f'yast{ys_}'])
                            dma(ya_s.ap()[q, :, t0:t0 + TA], yast[ys_][:], [f'yast{ys_}'], ['ya_s'], f"yao{ys_}")
                        else:
                            cw = hcw_sb[R]
                            cn = f'hcw_{R}'
                            for j in js:
                                c = j * 4 + q
                                a = acc[j]
                                an = f'acc{j}'
                                ts('dve', a[:], P_[:, j, 0:TA], cw[:, c, 0:1], hcb_sb[:, c:c + 1], ALU.mult, ALU.add, [pn, cn, 'hcb_sb'], [an])
                                stt(a[:], P_[:, j, 1:TA + 1], cw[:, c, 1:2], a[:], ALU.mult, ALU.add, [pn, cn, an], [an])
                                if j == 0:
                                    stt(x0T[:, q, t0:t0 + TA], P_[:, j, 2:TA + 2], cw[:, c, 2:3], a[:], ALU.mult, ALU.add,
                                        [pn, cn, an], [f'x0T_{q}'])
                                else:
                                    stt(a[:], P_[:, j, 2:TA + 2], cw[:, c, 2:3], a[:], ALU.mult, ALU.add, [pn, cn, an], [an])
                            zs = q % 2
                            tt('dve', zst[zs][:], acc[1][:], acc[2][:], ALU.mult, ['acc1', 'acc2'], [f'zst{zs}'])
                            bank = q % 2
                            for s4 in range(4):
                                tr(ps[bank][:, s4 * 128:(s4 + 1) * 128], zst[zs][:, s4 * 128:(s4 + 1) * 128], [f'zst{zs}'], [PS[bank]])
                            cp('act', ZT[:, it * 4:(it + 1) * 4, q * 128:(q + 1) * 128],
                               ps[bank][:, :].rearrange("p (s c) -> p s c", c=128), [PS[bank]], ['ZT'])
                S.end_phase()

            with ExitStack() as esc:
                base = L + 1 - 128 * nT
                W = 128 * (nS + nT - 1)
                Wh = (W // 2 + 127) // 128 * 128 + 128
                kw = [[sb(esc, f"kw{b}_{h}", [128, Wh], BF16) for h in range(2)] for b in range(2)]
                yconv = sb(esc, "yconv", [128, 128, nT], BF16)
                yhst = [sb(esc, f"yhst{i}", [128, 512], BF16) for i in range(2)]

                def _half(dd):
                    off = L - 127 - 128 * dd - base
                    return 0 if off + 128 <= Wh else 1
                assert _half(0) == 0
                Ds = [0] + [d for d in range(-(nS - 1), nT) if d != 0 and _half(d) == 0] + \
                    [d for d in range(-(nS - 1), nT) if d != 0 and _half(d) == 1]
                nper = 512 // nT
                for q in range(4):
                    for cc in range(128):
                        cg = q * 128 + cc
                        b = cc % 2
                        for h in range(2):
                            m0 = 0 if h == 0 else W - Wh
                            dma(kw[b][h][:], AP(kscr, cg * 2 * Lmax + base + m0, [[1, 128], [1, Wh]]),
                                ['kscr'], [f'kw{b}_{h}'], f"kw{b}_{h}", eng=('sp' if h == 0 else 'pool'))
                        grp = cc // nper
                        bank = grp % 2
                        col0 = (cc % nper) * nT
                        for di, dd in enumerate(Ds):
                            off = L - 127 - 128 * dd - base
                            if off + 128 <= Wh:
                                h, o = 0, off
                            else:
                                h, o = 1, off - (W - Wh)
                            assert 0 <= o and o + 128 <= Wh
                            T0 = max(0, dd)
                            T1 = min(nT, nS + dd)
                            mm(ps[bank][:, col0 + T0:col0 + T1], kw[b][h][:, o:o + 128], ZT[:, T0 - dd:T1 - dd, cg],
                               di == 0, di == len(Ds) - 1, [f'kw{b}_{h}', 'ZT'], [PS[bank]])
                        if cc % nper == nper - 1:
                            c0 = grp * nper
                            cp('act' if grp % 2 == 0 else 'dve', yconv[:, c0:c0 + nper, :],
                               ps[bank][:, :].rearrange("p (c t) -> p c t", t=nT), [PS[bank]], ['yconv'])
                    for T4 in range(nT // 4):
                        bank = 2 + T4 % 2
                        for s4 in range(4):
                            T = T4 * 4 + s4
                            mm(ps[bank][:, s4 * 128:(s4 + 1) * 128], yconv[:, :, T], jbf[:], True, True, ['yconv', 'jbf'], [PS[bank]])
                        ys_ = T4 % 2
                        stt(yhst[ys_][:], ps[bank][:, :], rnorm[:, q:q + 1], x0T[:, q, T4 * 512:(T4 + 1) * 512],
                            ALU.mult, ALU.mult, [PS[bank], 'rnorm', f'x0T_{q}'], [f'yhst{ys_}'])
                        dma(ya_s.ap()[4 + q, :, T4 * 512:(T4 + 1) * 512], yhst[ys_][:], [f'yhst{ys_}'], ['ya_s'], f"yho{ys_}")
                S.end_phase()

            esr.close()
            with ExitStack() as ese:
                g1 = sb(ese, "g1", [128, D]); b1 = sb(ese, "b1", [128, D]); g2 = sb(ese, "g2", [128, D]); b2 = sb(ese, "b2", [128, D])
                for t, src, name in ((g1, ln1g, 'g1'), (b1, ln1b, 'b1'), (g2, ln2g, 'g2'), (b2, ln2b, 'b2')):
                    dma(t[:], AP(src, 0, [[0, 128], [1, D]]), [], [name], "c1")
                keysT = sb(ese, "keysT", [128, 16, 128], BF16)
                dma(keysT[:].rearrange("p a b -> p (a b)"), keysT_s.ap(), [], ['keysT'], "c2")
                wob = [sb(ese, f"wob{i}", [128, D], BF16) for i in range(3)]
                yat = [sb(ese, f"yat{i}", [128, 8, TT], BF16) for i in range(1)]
                xres = [sb(ese, f"xres{i}", [128, D]) for i in range(1)]
                hh1 = sb(ese, "hh1", [128, 2, D]); x1 = [sb(ese, f"x1_{i}", [128, 2, D]) for i in range(2)]
                hh2 = hh1
                x1T = [sb(ese, f"x1T{i}", [128, 8, TT], BF16) for i in range(2)]
                st6 = sb(ese, "st6", [128, 2, 6]); mv = sb(ese, "mv", [128, 2]); rstd = sb(ese, "rstd", [128, 1])
                wqb = [sb(ese, f"wqb{i}", [128, 8, 128], BF16) for i in range(3)]
                qT = sb(ese, "qT", [128, 16, TT], BF16)
                scr8 = sb(ese, "scr8", [128, 2048]); scw = sb(ese, "scw", [128, 128])
                sc = scr8
                top = sb(ese, "top", [128, 16, 16]); tidx = sb(ese, "tidx", [128, 16, 16], U32); tidf = sb(ese, "tidf", [128, 16, 16])
                cand = scr8[:].rearrange("p (h x) -> p h x", x=256); candw = sb(ese, "candw", [128, 256])
                best = sb(ese, "best", [128, 8, 16]); pos = sb(ese, "pos", [128, 8, 16], U32)
                pa_u = sb(ese, "pa_u", [128, 8, 16], U32); pb_u = sb(ese, "pb_u", [128, 8, 16], U32)
                pa_f = sb(ese, "pa_f", [128, 8, 16]); pb_f = sb(ese, "pb_f", [128, 8, 16])
                eq = scr8[:].rearrange("p (h a b) -> p h a b", a=16, b=16)
                lst = sb(ese, "lst", [128, 3, 128])
                nmax = sb(ese, "nmax", [128, 8]); ssum = sb(ese, "ssum", [128, 8]); rsum = sb(ese, "rsum", [128, 8])
                lstT = sb(ese, "lstT", [128, 3, TT], BF16)
                TB = 8
                Pm = [sb(ese, f"Pm{i}", [128, TB, 128], BF16) for i in range(2)]
                Qm = [sb(ese, f"Qm{i}", [128, TB, 128], BF16) for i in range(2)]
                Wt = sb(ese, "Wt", [128, 128, TT], BF16)
                ub = [sb(ese, f"ub{i}", [128, 8, 128], BF16) for i in range(6)]
                vbb = [sb(ese, f"vbb{i}", [128, D], BF16) for i in range(4)]
                G = [sb(ese, f"G{i}", [128, TT], BF16) for i in range(4)]
                cf = [sb(ese, f"cf{i}", [128, TT], BF16) for i in range(4)]
                wqc = [0]
                woc = [0]

                def layer_norm(src, sn, dst, dn, g, gn, b, bn):
                    for hf in range(2):
                        S.add('dve', (lambda hf=hf: nc.vector.bn_stats(out=st6[:, hf, :], in_=src[:, hf * 512:(hf + 1) * 512])), [sn], ['st6'])
                    S.add('dve', lambda: nc.vector.bn_aggr(out=mv[:], in_=st6[:].rearrange("p a b -> p (a b)")), ['st6'], ['mv'])
                    act(rstd[:], mv[:, 1:2], AF.Sqrt, ['mv'], ['rstd'], bias=float(LN_EPS), scale=1.0)
                    S.add('dve', lambda: nc.vector.reciprocal(out=rstd[:], in_=rstd[:]), ['rstd'], ['rstd'])
                    ts('dve', dst, src, mv[:, 0:1], rstd[:, 0:1], ALU.subtract, ALU.mult, [sn, 'mv', 'rstd'], [dn])
                    tt('pool', dst, dst, g[:], ALU.mult, [dn, gn], [dn])
                    tt('pool', dst, dst, b[:], ALU.add, [dn, bn], [dn])

                ntl = Lown // TT
                V = nc.vector

                def prep_stages(it):
                    t0 = it * TT
                    xb = it % 2
                    X1 = x1[xb]
                    X1T = x1T[xb]
                    xtn = f'x1T{xb}'
                    st = []

                    def s_load():
                        dma(yat[0][:], ya_s.ap()[:, :, t0:t0 + TT].rearrange("q p t -> p q t"), ['ya_s'], ['yat0'], "yat0", eng='act')
                    st.append(s_load)
                    for hh in range(2):
                        def s_d(hh=hh):
                            dma(xres[0][:], xin.ap()[t0 + hh * 128:t0 + (hh + 1) * 128, :], [], ['xres0'], "xres0", eng='act')
                            for ck in range(8):
                                ws = woc[0] % 3
                                woc[0] += 1
                                dma(wob[ws][:], wout_s.ap()[ck], [], [f'wob{ws}'], f"wob{ws}", eng='act')
                                for dh in range(2):
                                    mm(ps[4 + dh][:, :], yat[0][:, ck, hh * 128:(hh + 1) * 128], wob[ws][:, dh * 512:(dh + 1) * 512],
                                       ck == 0, ck == 7, ['yat0', f'wob{ws}'], [PS[4 + dh]])
                            for dh in range(2):
                                stt(hh1[:, hh, dh * 512:(dh + 1) * 512], xres[0][:, dh * 512:(dh + 1) * 512], float(ALPHA), ps[4 + dh][:, :],
                                    ALU.mult, ALU.add, ['xres0', PS[4 + dh]], [f'hh1_{hh}'])
                        st.append(s_d)

                        def s_ln(hh=hh):
                            layer_norm(hh1[:, hh, :], f'hh1_{hh}', X1[:, hh, :], f'x1_{xb}_{hh}', g1, 'g1', b1, 'b1')
                        st.append(s_ln)

                        def s_tr(hh=hh):
                            for b4 in range(2):
                                bank = 4 + b4
                                for c4 in range(4):
                                    dk = b4 * 4 + c4
                                    tr(ps[bank][:, c4 * 128:(c4 + 1) * 128], X1[:, hh, dk * 128:(dk + 1) * 128], [f'x1_{xb}_{hh}'], [PS[bank]])
                                cp('act', X1T[:, b4 * 4:(b4 + 1) * 4, hh * 128:(hh + 1) * 128],
                                   ps[bank][:, :].rearrange("p (c t) -> p c t", t=128), [PS[bank]], [xtn])
                        st.append(s_tr)
                    for c4g in range(4):
                        def s_q(c4g=c4g):
                            for c4 in range(4):
                                cc = c4g * 4 + c4
                                s_ = wqc[0] % 3
                                wqc[0] += 1
                                dma(wqb[s_][:].rearrange("p a b -> p (a b)"), wq_s.ap()[cc], [], [f'wqb{s_}'], f"wqb{s_}", eng='act')
                                bank = 4 + cc % 2
                                for dk in range(8):
                                    mm(ps[bank][:, 0:TT], wqb[s_][:, dk, :], X1T[:, dk, :], dk == 0, dk == 7, [f'wqb{s_}', xtn], [PS[bank]])
                                cp('act' if cc % 2 == 0 else 'dve', qT[:, cc, :], ps[bank][:, 0:TT], [PS[bank]], ['qT'])
                        st.append(s_q)
                    for hh in range(2):
                        def s_sc(hh=hh):
                            for c4g in range(4):
                                bank = 4 + c4g % 2
                                for c4 in range(4):
                                    cc = c4g * 4 + c4
                                    mm(ps[bank][:, c4 * 128:(c4 + 1) * 128], qT[:, cc, hh * 128:(hh + 1) * 128], keysT[:, cc, :], True, True,
                                       ['qT', 'keysT'], [PS[bank]])
                                cp('act' if c4g % 2 == 0 else 'dve', sc[:, c4g * 512:(c4g + 1) * 512], ps[bank][:, :], [PS[bank]], ['scr8'])
                        st.append(s_sc)
                        for gh in range(2):
                            def s_tk(gh=gh):
                                for g in range(gh * 8, gh * 8 + 8):
                                    sg = sc[:, g * 128:(g + 1) * 128]
                                    S.add('dve', (lambda g=g, sg=sg: V.max(out=top[:, g, 0:8], in_=sg)), ['scr8'], ['top'])
                                    S.add('dve', (lambda g=g, sg=sg: V.max_index(out=tidx[:, g, 0:8], in_max=top[:, g, 0:8], in_values=sg)), ['scr8', 'top'], ['tidx'])
                                    S.add('dve', (lambda g=g, sg=sg: V.match_replace(out=scw[:], in_to_replace=top[:, g, 0:8], in_values=sg, imm_value=-1e30)),
                                          ['scr8', 'top'], ['scw'])
                                    S.add('dve', (lambda g=g: V.max(out=top[:, g, 8:16], in_=scw[:])), ['scw'], ['top'])
                                    S.add('dve', (lambda g=g: V.max_index(out=tidx[:, g, 8:16], in_max=top[:, g, 8:16], in_values=scw[:])), ['scw', 'top'], ['tidx'])
                            st.append(s_tk)

                        def s_cand():
                            cp('dve', tidf[:], tidx[:], ['tidx'], ['tidf'])
                            topv = top[:].rearrange("p (h two) k -> p h two k", two=2)
                            tt('dve', cand[:].rearrange("p h (a b) -> p h a b", b=16),
                               topv[:, :, 0, :].unsqueeze(3).to_broadcast([128, 8, 16, 16]),
                               topv[:, :, 1, :].unsqueeze(2).to_broadcast([128, 8, 16, 16]), ALU.add, ['top'], ['scr8'])
                            for h in range(8):
                                ch = cand[:, h, :]
                                S.add('dve', (lambda h=h, ch=ch: V.max(out=best[:, h, 0:8], in_=ch)), ['scr8'], ['best'])
                                S.add('dve', (lambda h=h, ch=ch: V.max_index(out=pos[:, h, 0:8], in_max=best[:, h, 0:8], in_values=ch)), ['scr8', 'best'], ['pos'])
                                S.add('dve', (lambda h=h, ch=ch: V.match_replace(out=candw[:], in_to_replace=best[:, h, 0:8], in_values=ch, imm_value=-1e30)),
                                      ['scr8', 'best'], ['candw'])
                                S.add('dve', (lambda h=h: V.max(out=best[:, h, 8:16], in_=candw[:])), ['candw'], ['best'])
                                S.add('dve', (lambda h=h: V.max_index(out=pos[:, h, 8:16], in_max=best[:, h, 8:16], in_values=candw[:])), ['candw', 'best'], ['pos'])
                        st.append(s_cand)

                        def s_sm(hh=hh):
                            gv = lst[:, 2, :].rearrange("p (h k) -> p h k", k=16)
                            tt('dve', gv, best[:], best[:, :, 0:1].to_broadcast([128, 8, 16]), ALU.subtract, ['best'], ['lst_g'])
                            act(gv, gv, AF.Exp, ['lst_g'], ['lst_g'])
                            S.add('dve', lambda: V.tensor_reduce(out=ssum[:], in_=gv, axis=AX.X, op=ALU.add), ['lst_g'], ['ssum'])
                            S.add('dve', lambda: V.reciprocal(out=rsum[:], in_=ssum[:]), ['ssum'], ['rsum'])
                            tt('dve', gv, gv, rsum[:].unsqueeze(2).to_broadcast([128, 8, 16]), ALU.mult, ['lst_g', 'rsum'], ['lst_g'])
                            S.add('dve', lambda: V.tensor_single_scalar(out=pa_u[:], in_=pos[:], scalar=4, op=ALU.logical_shift_right), ['pos'], ['pa_u'])
                            S.add('dve', lambda: V.tensor_single_scalar(out=pb_u[:], in_=pos[:], scalar=15, op=ALU.bitwise_and), ['pos'], ['pb_u'])
                            cp('dve', pa_f[:], pa_u[:], ['pa_u'], ['pa_f'])
                            cp('dve', pb_f[:], pb_u[:], ['pb_u'], ['pb_f'])
                            tidv = tidf[:].rearrange("p (h two) k -> p h two k", two=2)
                            for wi, (pf, pn_) in enumerate(((pa_f, 'pa_f'), (pb_f, 'pb_f'))):
                                tt('dve', eq[:], pf[:].unsqueeze(3).to_broadcast([128, 8, 16, 16]),
                                   iota16[:].unsqueeze(1).unsqueeze(1).to_broadcast([128, 8, 16, 16]), ALU.is_equal, [pn_, 'iota16'], ['scr8'])
                                tt('dve', eq[:], eq[:], tidv[:, :, wi, :].unsqueeze(2).to_broadcast([128, 8, 16, 16]), ALU.mult, ['scr8', 'tidf'], ['scr8'])
                                S.add('dve', (lambda wi=wi: V.tensor_reduce(out=lst[:, wi, :].rearrange("p (h k) -> p h k", k=16), in_=eq[:], axis=AX.X, op=ALU.add)),
                                      ['scr8'], [f'lst_{wi}'])
                            for wi in range(3):
                                tr(ps[4][:, wi * 128:(wi + 1) * 128], lst[:, wi, :], ['lst_0', 'lst_1', 'lst_g'], [PS[4]])
                            cp('act', lstT[:, :, hh * 128:(hh + 1) * 128], ps[4][:, 0:384].rearrange("p (w t) -> p w t", t=128), [PS[4]], ['lstT'])
                        st.append(s_sm)
                    return st

                def e4(it):
                    for tb in range(TT // TB):
                        s = tb % 2
                        tsl = slice(tb * TB, (tb + 1) * TB)
                        iob = iota128b[:].unsqueeze(1).to_broadcast([128, TB, 128])
                        tt('dve', Pm[s][:], iob, lstT[:, 0, tsl].unsqueeze(2).to_broadcast([128, TB, 128]), ALU.is_equal,
                           ['iota128b', 'lstT'], [f'Pm{s}'])
                        tt('dve', Qm[s][:], iob, lstT[:, 1, tsl].unsqueeze(2).to_broadcast([128, TB, 128]), ALU.is_equal,
                           ['iota128b', 'lstT'], [f'Qm{s}'])
                        tt('pool', Qm[s][:], Qm[s][:], lstT[:, 2, tsl].unsqueeze(2).to_broadcast([128, TB, 128]), ALU.mult,
                           [f'Qm{s}', 'lstT'], [f'Qm{s}'])
                        for t4 in range(TB // 4):
                            bank = 4 + (tb * (TB // 4) + t4) % 2
                            for k4 in range(4):
                                tl = t4 * 4 + k4
                                mm(ps[bank][:, k4 * 128:(k4 + 1) * 128], Qm[s][:, tl, :], Pm[s][:, tl, :], True, True, [f'Qm{s}', f'Pm{s}'], [PS[bank]])
                            tg = tb * TB + t4 * 4
                            cp('act', Wt[:, :, tg:tg + 4], ps[bank][:, :].rearrange("p (t i) -> p i t", i=128), [PS[bank]], ['Wt'])

                def f_evac(it):
                    X1 = x1[it % 2]
                    for hh in range(2):
                        for dh in range(2):
                            bank = hh * 2 + dh
                            stt(hh2[:, hh, dh * 512:(dh + 1) * 512], X1[:, hh, dh * 512:(dh + 1) * 512], float(ALPHA), ps[bank][:, :],
                                ALU.mult, ALU.add, [f'x1_{it % 2}_{hh}', PS[bank]], [f'hh1_{hh}'])

                def f_stages(it):
                    t0 = it * TT
                    st = []
                    for hh in range(2):
                        def s_f(hh=hh):
                            layer_norm(hh2[:, hh, :], f'hh1_{hh}', hh2[:, hh, :], f'hh1_{hh}', g2, 'g2', b2, 'b2')
                            dma(yout.ap()[t0 + hh * 128:t0 + (hh + 1) * 128, :], hh2[:, hh, :], [f'hh1_{hh}'], [f'yout{R}'], f"osto{hh}")
                        st.append(s_f)
                    return st

                def load_u(i):
                    s = i % 6
                    dma(ub[s][:].rearrange("p a b -> p (a b)"), uT_s.ap()[i], [], [f'ub{s}'], f"ub{s}")

                def load_v(i):
                    s = i % 4
                    dma(vbb[s][:], v_s.ap()[i], [], [f'vbb{s}'], f"vbb{s}", eng='pool')

                def s_mm(it, i):
                    s = i % 6
                    bank = 6 + i % 2
                    c0 = 0
                    pn = PS[bank]
                    X1T = x1T[it % 2]
                    for dk in range(8):
                        mm(ps[bank][:, c0:c0 + TT], ub[s][:, dk, :], X1T[:, dk, :], dk == 0, dk == 7, [f'ub{s}', f'x1T{it % 2}'], [pn])
                    e = i % 4
                    act(G[e][:], ps[bank][:, c0:c0 + TT], AF.Gelu, [pn], [f'G{e}'])
                    tt('dve', cf[e][:], G[e][:], Wt[:, i, :], ALU.mult, [f'G{e}', 'Wt'], [f'cf{e}'])

                def o_mm(i):
                    s = i % 4
                    e = i % 4
                    for hh in range(2):
                        for dh in range(2):
                            bank = hh * 2 + dh
                            mm(ps[bank][:, :], cf[e][:, hh * 128:(hh + 1) * 128], vbb[s][:, dh * 512:(dh + 1) * 512], i == 0, i == 127,
                               [f'cf{e}', f'vbb{s}'], [PS[bank]])

                for stg_ in prep_stages(0):
                    stg_()
                e4(0)
                pend = []
                for it in range(ntl):
                    stages = list(pend)
                    if it + 1 < ntl:
                        stages += prep_stages(it + 1)
                    items = Sched.hoist(S.record(stages), 30)
                    nit = len(items)
                    wmap = {'dve': 4.0, 'pool': 3.0, 'act': 1.0, 'pe': 1.5, 'sp': 0.5}
                    cumw = []
                    acc_w = 0.0
                    for it_ in items:
                        acc_w += wmap.get(it_[0], 1.0)
                        cumw.append(acc_w)
                    totw = max(acc_w, 1.0)
                    for i in range(5):
                        load_u(i)
                    for i in range(3):
                        load_v(i)
                    s_mm(it, 0)
                    s_mm(it, 1)
                    done = 0
                    for i in range(128):
                        if i + 5 < 128:
                            load_u(i + 5)
                        if i + 3 < 128:
                            load_v(i + 3)
                        if i + 2 < 128:
                            s_mm(it, i + 2)
                        o_mm(i)
                        upto = done
                        while upto < nit and cumw[upto] <= (i + 1) * totw / 118.0:
                            upto += 1
                        if _DBG_NOINTER:
                            upto = done
                        S.replay(items[done:upto])
                        done = max(done, upto)
                    S.replay(items[done:])
                    f_evac(it)
                    if it + 1 < ntl:
                        e4(it + 1)
                    pend = f_stages(it)
                for stg_ in pend:
                    stg_()
                S.end_phase()

    run('S', xs, ys, Ls, Ls)
    run('P', xp, yp, Lp, LpO)
    es.close()
    return nc, S


def _feat_tables(L):
    n = 2 * L
    jp = np.arange(n)
    pos = np.abs(L - jp).astype(np.float64)
    pos[0] = 0.0
    t = pos / (L - 1)
    w = 2.0 * math.pi * pos / L
    f = np.linspace(1e-4, 15.0, 16)
    feat = np.concatenate([t[None], np.cos(f[:, None] * w[None]), -np.sin(f[:, None] * w[None]), np.ones((1, n))], 0)
    return feat.astype(np.float32), t[None].astype(np.float32)


_CACHE = {}


def _prep_inputs(Lp, Ls, x_prompt, x_sample, w_in, b_in, a_conv_w, h_conv_w, h_conv_b, hf_w1, hf_b1, hf_freq,
                 hf_w2, hf_b2, hf_w3, hy_decay, hy_bias, w_out, ln1_g, ln1_b, peer_wq, peer_keys,
                 peer_u, peer_v, ln2_g, ln2_b):
    c = np.ascontiguousarray
    f32 = np.float32
    featP, tposP = _feat_tables(Lp)
    featS, tposS = _feat_tables(Ls)
    acw = np.asarray(a_conv_w[0]); hcw = np.asarray(h_conv_w[0])

    def cwl(a, nch):
        return c(a.reshape(3, nch, 128).transpose(2, 1, 0))
    w3 = np.asarray(hf_w3[0]); w3d = [c(w3[:, :512]), c(w3[:, 512:])]
    dec = np.asarray(hy_decay[0]); decd = [c(dec[0:1]), c(dec[1:2])]
    shared = dict(
        w_in=c(np.asarray(w_in[0])), b_in_t=c(np.asarray(b_in[0]).reshape(24, 128).T),
        acw_s=cwl(acw, 4), hcw_s=cwl(hcw, 12), hcb=c(np.asarray(h_conv_b[0]).reshape(12, 128).T),
        w1aug=c(np.concatenate([np.asarray(hf_w1[0]), np.asarray(hf_b1[0])[None]], 0)),
        w2aug=c(np.concatenate([np.asarray(hf_w2[0]), np.asarray(hf_b2[0])[None]], 0)),
        freq=c(np.asarray(hf_freq[0]).reshape(64, 1)),
        w3pos_s=w3d[0], w3neg_s=w3d[1], w3zero=w3d[0], decpos_s=decd[0], decneg_s=decd[1],
        feat_p=featP, tpos_p=tposP, feat_s=featS, tpos_s=tposS,
        hyb=c(np.asarray(hy_bias[0]).reshape(4, 128).T), w_out=c(np.asarray(w_out[0])),
        ln1g=c(np.asarray(ln1_g[0])[None]), ln1b=c(np.asarray(ln1_b[0])[None]),
        ln2g=c(np.asarray(ln2_g[0])[None]), ln2b=c(np.asarray(ln2_b[0])[None]),
        wq=c(np.asarray(peer_wq[0])), keys=c(np.asarray(peer_keys[0]).reshape(2048, 128)),
        u=c(np.asarray(peer_u[0])), v=c(np.asarray(peer_v[0])),
        ident=np.eye(128, dtype=f32), jmat=c(np.eye(128, dtype=f32)[::-1]),
        iota128=c(np.tile(np.arange(128, dtype=f32)[None], (128, 1))),
        iota16=c(np.tile(np.arange(16, dtype=f32)[None], (128, 1))),
    )
    shared = {k: np.asarray(v, dtype=f32) for k, v in shared.items()}
    in_maps = []
    xpn = np.asarray(x_prompt); xsn = np.asarray(x_sample)
    for k in range(NCORES):
        b, half = k // 2, k % 2
        m = dict(shared)
        if half == 0:
            m['xp'] = c(xpn[b]); m['acw_p'] = shared['acw_s']; m['hcw_p'] = shared['hcw_s']
            m['w3pos_p'] = w3d[0]; m['w3neg_p'] = w3d[1]; m['decpos_p'] = decd[0]; m['decneg_p'] = decd[1]
        else:
            m['xp'] = c(xpn[b][::-1]); m['acw_p'] = cwl(acw[::-1], 4); m['hcw_p'] = cwl(hcw[::-1], 12)
            m['w3pos_p'] = w3d[1]; m['w3neg_p'] = w3d[0]; m['decpos_p'] = decd[1]; m['decneg_p'] = decd[0]
        m['xs'] = c(xsn[k])
        in_maps.append({kk: np.asarray(vv, dtype=f32) for kk, vv in m.items()})
    return in_maps


def kernel(**inputs):
    xpn = inputs['x_prompt']; xsn = inputs['x_sample']
    B, Lp, _ = xpn.shape
    Bs, Ls, _ = xsn.shape
    assert B == 4 and Bs == 8
    key = (Lp, Ls)
    if key not in _CACHE:
        _CACHE[key] = build_nc(Lp, Ls)[0]
    nc = _CACHE[key]
    in_maps = _prep_inputs(Lp, Ls, **inputs)
    res = run_bass_kernel_spmd(nc, in_maps, core_ids=list(range(NCORES)))
    yp = np.zeros((B, Lp, D), np.float32)
    ys = np.zeros((Bs, Ls, D), np.float32)
    h = Lp // 2
    for k in range(NCORES):
        r = res.results[k]
        b, half = k // 2, k % 2
        if half == 0:
            yp[b, :h] = r['yp']
        else:
            yp[b, h:] = r['yp'][::-1]
        ys[k] = r['ys']
    return (yp, ys)
```

```python
import math
from contextlib import ExitStack
import numpy as np
import concourse.bass as bass
import concourse.mybir as mybir
from concourse.bass_utils import run_bass_kernel_spmd
from concourse.bass_types import AP

F32 = mybir.dt.float32
BF16 = mybir.dt.bfloat16
U32 = mybir.dt.uint32
I32 = mybir.dt.int32
AF = mybir.ActivationFunctionType
ALU = mybir.AluOpType
AX = mybir.AxisListType

D = 1024
NCORES = 8
ALPHA = 2.0 ** 0.25
LN_EPS = 1e-5
TWO_PI = 6.283185
TA = 512
TT = 256
import os
_DBG_NOINTER = bool(os.environ.get('NOINTER'))


class Sched:
    EP = 30000
    DEP = 1800

    def __init__(self, nc, es):
        self.nc = nc
        self.es = es
        self.eng = {'pe': nc.tensor, 'act': nc.scalar, 'dve': nc.vector, 'pool': nc.gpsimd, 'sp': nc.sync}
        self.ops = []
        self.lastw = {}
        self.readers = {}
        self.lastkey = {}
        self.cnt = {e: 0 for e in self.eng}
        self.sems = {}
        self.ksem = {}
        self.free_sems = {'sw': [], 'hw': []}
        self.waited = {}
        self.nsem = 0
        self.n_inst = 0
        self.capture = None

    def record(self, fns):
        self.capture = []
        for f in fns:
            f()
        out = self.capture
        self.capture = None
        return out

    @staticmethod
    def hoist(items, H):
        items = list(items)
        for idx in range(len(items)):
            eng, fn, r, w, key = items[idx]
            if key is None:
                continue
            toks = set(w) | set(r)
            lim = max(0, idx - H)
            pos = idx
            while pos > lim:
                pe, pf, pr, pw, pk = items[pos - 1]
                if pk is not None and (pk == key or pe == eng):
                    break
                if (set(pw) & toks) or (set(pr) & set(w)):
                    break
                pos -= 1
            if pos != idx:
                it_ = items.pop(idx)
                items.insert(pos, it_)
        return items

    def replay(self, items):
        for it in items:
            self.add(*it)

    def _sem(self, name):
        self.nsem += 1
        return self.es.enter_context(self.nc.semaphore(f"s{self.nsem}_{name}"))

    def add(self, eng, fn, r=(), w=(), key=None):
        if self.capture is not None:
            self.capture.append((eng, fn, tuple(r), tuple(w), key))
            return -1
        i = len(self.ops)
        deps = set()
        for t in r:
            if t in self.lastw:
                deps.add(self.lastw[t])
        for t in w:
            if t in self.lastw:
                deps.add(self.lastw[t])
            for o in self.readers.get(t, {}).values():
                deps.add(o)
        if key is not None and key in self.lastkey:
            deps.add(self.lastkey[key])
        rk = eng if key is None else ('dma', i)
        for t in r:
            self.readers.setdefault(t, {})[rk] = i
        for t in w:
            self.lastw[t] = i
            self.readers[t] = {}
        if key is not None:
            self.lastkey[key] = i
        self.ops.append(dict(eng=eng, fn=fn, deps=deps, key=key, marked=False, sem=None))
        return i

    def end_phase(self):
        last = {}
        dmas = []
        for i, op in enumerate(self.ops):
            if op['key'] is None:
                if op['fn'] is not None:
                    last[op['eng']] = i
            else:
                dmas.append(i)
        deps = set(last.values()) | set(dmas)
        for e in self.eng:
            self.ops.append(dict(eng=e, fn=None, deps=set(deps), key=None, marked=False, sem=None))
        self._emit()
        for k, sc_ in self.ksem.items():
            if sc_[1] < self.DEP:
                self.free_sems[sc_[2]].append(sc_)
        self.ksem = {}
        self.ops = []
        self.lastw = {}
        self.readers = {}
        self.lastkey = {}

    def _emit(self):
        ops = self.ops
        for op in ops:
            best = {}
            keep = []
            for d in op['deps']:
                p = ops[d]
                if p['fn'] is None:
                    continue
                if p['key'] is not None:
                    keep.append(d)
                else:
                    if p['eng'] == 'pe' and op['eng'] == 'pe' and op['key'] is None:
                        continue
                    if p['eng'] not in best or best[p['eng']] < d:
                        best[p['eng']] = d
            keep += list(best.values())
            op['deps'] = keep
            for d in keep:
                ops[d]['marked'] = True
        for op in ops:
            e = self.eng[op['eng']]
            for d in sorted(op['deps']):
                sem, val = ops[d]['sem']
                k = (op['eng'], id(sem))
                if self.waited.get(k, 0) >= val:
                    continue
                e.wait_ge(sem, val)
                self.waited[k] = val
                self.n_inst += 1
            if op['fn'] is None:
                continue
            inst = op['fn']()
            self.n_inst += 1
            if op['key'] is not None:
                sc_ = self.ksem.get(op['key'])
                if sc_ is None or sc_[1] >= self.DEP:
                    cls = 'sw' if op['eng'] == 'pool' else 'hw'
                    sc_ = self.free_sems[cls].pop() if self.free_sems[cls] else [self._sem("d"), 0, cls]
                    self.ksem[op['key']] = sc_
                sc_[1] += 1
                inst.then_inc(sc_[0], 16)
                op['sem'] = (sc_[0], 16 * sc_[1])
            elif op['marked']:
                n = self.cnt[op['eng']]
                self.cnt[op['eng']] = n + 1
                sk = (op['eng'], n // self.EP)
                if sk not in self.sems:
                    self.sems[sk] = self._sem(op['eng'])
                sem = self.sems[sk]
                inst.then_inc(sem, 1)
                op['sem'] = (sem, n % self.EP + 1)


def _r3(x):
    return x


def build_nc(Lp, Ls, dbg=False):
    nc = bass.Bass("TRN2", target_bir_lowering=False)
    Lmax = max(Lp, Ls)
    LpO = Lp // 2

    def din(name, shape, dt=F32):
        return nc.dram_tensor(name, list(shape), dt, kind="ExternalInput")

    xp = din("xp", [Lp, D]); xs = din("xs", [Ls, D])
    w_in = din("w_in", [D, 3072]); b_in_t = din("b_in_t", [128, 24])
    acw = {'P': din("acw_p", [128, 4, 3]), 'S': din("acw_s", [128, 4, 3])}
    hcw = {'P': din("hcw_p", [128, 12, 3]), 'S': din("hcw_s", [128, 12, 3])}
    hcb = din("hcb", [128, 12])
    w1aug = din("w1aug", [34, 64]); w2aug = din("w2aug", [65, 64]); freq = din("freq", [64, 1])
    w3pos = {'P': din("w3pos_p", [64, 512]), 'S': din("w3pos_s", [64, 512])}
    w3neg = {'P': din("w3neg_p", [64, 512]), 'S': din("w3neg_s", [64, 512])}
    w3zero = din("w3zero", [64, 512])
    decpos = {'P': din("decpos_p", [1, 512]), 'S': din("decpos_s", [1, 512])}
    decneg = {'P': din("decneg_p", [1, 512]), 'S': din("decneg_s", [1, 512])}
    feat = {'P': din("feat_p", [34, 2 * Lp]), 'S': din("feat_s", [34, 2 * Ls])}
    tpos = {'P': din("tpos_p", [1, 2 * Lp]), 'S': din("tpos_s", [1, 2 * Ls])}
    hyb = din("hyb", [128, 4])
    w_out = din("w_out", [D, D])
    ln1g = din("ln1g", [1, D]); ln1b = din("ln1b", [1, D]); ln2g = din("ln2g", [1, D]); ln2b = din("ln2b", [1, D])
    wq = din("wq", [D, 2048]); keys = din("keys", [2048, 128])
    u_d = din("u", [16384, D]); v_d = din("v", [16384, D])
    ident_d = din("ident", [128, 128]); jmat_d = din("jmat", [128, 128])
    iota128_d = din("iota128", [128, 128]); iota16_d = din("iota16", [128, 16])

    yp = nc.dram_tensor("yp", [LpO, D], F32, kind="ExternalOutput")
    ys = nc.dram_tensor("ys", [Ls, D], F32, kind="ExternalOutput")

    def dscr(name, shape, dt=BF16):
        return nc.dram_tensor(name, list(shape), dt, kind="Internal")

    w_in_s = dscr("w_in_s", [24, 128, 1024])
    wq_s = dscr("wq_s", [16, 128, 1024])
    uT_s = dscr("uT_s", [128, 128, 1024])
    v_s = dscr("v_s", [128, 128, 1024])
    ya_s = dscr("ya_s", [8, 128, Lmax])
    wout_s = dscr("wout_s", [8, 128, 1024])
    keysT_s = dscr("keysT_s", [128, 2048])
    kscr = dscr("kscr", [512, 2 * Lmax])

    es = ExitStack()
    S = Sched(nc, es)

    _nm = [0]

    def sb(stack, name, shape, dt=F32):
        _nm[0] += 1
        return stack.enter_context(nc.sbuf_tensor(f"{name}_{_nm[0]}", list(shape), dt))

    ps = [es.enter_context(nc.psum_tensor(f"ps{i}", [128, 512], F32)) for i in range(8)]
    PS = [f"ps{i}" for i in range(8)]

    ident = sb(es, "ident", [128, 128]); jbf = sb(es, "jbf", [128, 128], BF16)
    iota128 = sb(es, "iota128", [128, 128]); iota16 = sb(es, "iota16", [128, 16])
    iota128b = sb(es, "iota128b", [128, 128], BF16)
    b_in_sb = sb(es, "b_in_sb", [128, 24])
    acw_sb = {r: sb(es, f"acw_{r}", [128, 4, 3]) for r in 'PS'}
    hcw_sb = {r: sb(es, f"hcw_{r}", [128, 12, 3]) for r in 'PS'}
    hcb_sb = sb(es, "hcb_sb", [128, 12]); hyb_sb = sb(es, "hyb_sb", [128, 4])
    w1_sb = sb(es, "w1_sb", [34, 64]); w2_sb = sb(es, "w2_sb", [65, 64]); fq_sb = sb(es, "fq_sb", [64, 1])
    dec_sb = {}
    for r in 'PS':
        dec_sb[('pos', r)] = sb(es, f"decpos_{r}", [1, 512])
        dec_sb[('neg', r)] = sb(es, f"decneg_{r}", [1, 512])
    rnorm = sb(es, "rnorm", [128, 4])
    tmpj = sb(es, "tmpj", [128, 128])

    def dma(out, in_, r, w, key, eng='sp', slow=False):
        if slow:
            S.add(eng, lambda: S.eng[eng].dma_start(out=out, in_=in_, allow_slow_non_contiguous=True), r, w, key)
        else:
            S.add(eng, lambda: S.eng[eng].dma_start(out=out, in_=in_), r, w, key)

    def mm(out, lhsT, rhs, start, stop, r, w):
        S.add('pe', lambda: nc.tensor.matmul(out, lhsT, rhs, start=start, stop=stop), r, w)

    def tr(out, in_, r, w):
        S.add('pe', lambda: nc.tensor.transpose(out, in_, ident[:]), r + ['ident'], w)

    def act(out, in_, func, r, w, **kw):
        S.add('act', lambda: nc.scalar.activation(out=out, in_=in_, func=func, **kw), r, w)

    def cp(eng, out, in_, r, w):
        if eng == 'act':
            S.add('act', lambda: nc.scalar.copy(out=out, in_=in_), r, w)
        else:
            S.add(eng, lambda: S.eng[eng].tensor_copy(out=out, in_=in_), r, w)

    def tt(eng, out, in0, in1, op, r, w):
        S.add(eng, lambda: S.eng[eng].tensor_tensor(out=out, in0=in0, in1=in1, op=op), r, w)

    def ts(eng, out, in0, s1, s2, op0, op1, r, w):
        if op1 is None:
            S.add(eng, lambda: S.eng[eng].tensor_scalar(out=out, in0=in0, scalar1=s1, scalar2=None, op0=op0), r, w)
        else:
            S.add(eng, lambda: S.eng[eng].tensor_scalar(out=out, in0=in0, scalar1=s1, scalar2=s2, op0=op0, op1=op1), r, w)

    def stt(out, in0, scalar, in1, op0, op1, r, w):
        S.add('dve', lambda: nc.vector.scalar_tensor_tensor(out=out, in0=in0, scalar=scalar, in1=in1, op0=op0, op1=op1), r, w)

    rr = [0]

    def rot(engs):
        rr[0] += 1
        return engs[rr[0] % len(engs)]

    def ld(t, src, name, key="c0"):
        dma(t[:], src, [], [name], key)

    ld(ident, ident_d.ap(), 'ident'); ld(tmpj, jmat_d.ap(), 'tmpj', "c1")
    ld(iota128, iota128_d.ap(), 'iota128', "c2"); ld(iota16, iota16_d.ap(), 'iota16', "c3")
    ld(b_in_sb, b_in_t.ap(), 'b_in_sb')
    for r in 'PS':
        ld(acw_sb[r], acw[r].ap(), f'acw_{r}', "c1"); ld(hcw_sb[r], hcw[r].ap(), f'hcw_{r}', "c2")
        ld(dec_sb[('pos', r)], decpos[r].ap(), f'decpos_{r}', "c1"); ld(dec_sb[('neg', r)], decneg[r].ap(), f'decneg_{r}', "c2")
    ld(hcb_sb, hcb.ap(), 'hcb_sb', "c3"); ld(hyb_sb, hyb.ap(), 'hyb_sb')
    ld(w1_sb, w1aug.ap(), 'w1_sb', "c1"); ld(w2_sb, w2aug.ap(), 'w2_sb', "c2"); ld(fq_sb, freq.ap(), 'fq_sb', "c3")
    cp('dve', jbf[:], tmpj[:], ['tmpj'], ['jbf'])
    cp('dve', iota128b[:], iota128[:], ['iota128'], ['iota128b'])
    for r in 'PS':
        for s in ('pos', 'neg'):
            t = dec_sb[(s, r)]
            nm = f'dec{s}_{r}'
            act(t[:], t[:], AF.Abs, [nm], [nm])
    ts('dve', fq_sb[:], fq_sb[:], 1.0 / (2.0 * math.pi), None, ALU.mult, None, ['fq_sb'], ['fq_sb'])

    with ExitStack() as es2:
        stg = [sb(es2, f"stg{i}", [128, 3072]) for i in range(2)]
        stgb = [sb(es2, f"stgb{i}", [128, 3072], BF16) for i in range(2)]
        ust = [sb(es2, f"ust{i}", [128, D]) for i in range(3)]
        utb = [sb(es2, f"utb{i}", [128, 8, 128], BF16) for i in range(2)]
        vst = [sb(es2, f"vst{i}", [128, D]) for i in range(3)]
        vb = [sb(es2, f"vb{i}", [128, D], BF16) for i in range(2)]
        n = 0
        for (src, dst, ncol, nm) in ((w_in, w_in_s, 3072, 'w_in_s'), (wq, wq_s, 2048, 'wq_s')):
            for dk in range(8):
                sl = n % 2
                n += 1
                dma(stg[sl][:, 0:ncol], src.ap()[dk * 128:(dk + 1) * 128, :], [], [f'stg{sl}'], f"stg{sl}")
                cp(rot(['act', 'dve']), stgb[sl][:, 0:ncol], stg[sl][:, 0:ncol], [f'stg{sl}'], [f'stgb{sl}'])
                dma(dst.ap()[:, :, dk * 128:(dk + 1) * 128].rearrange("m p c -> p m c"),
                    stgb[sl][:, 0:ncol].rearrange("p (m c) -> p m c", c=128),
                    [f'stgb{sl}'], [nm], f"stgo{sl}", slow=True)
        for ck in range(8):
            sl = n % 2
            n += 1
            dma(stg[sl][:, 0:D], w_out.ap()[ck * 128:(ck + 1) * 128, :], [], [f'stg{sl}'], f"stg{sl}")
            cp(rot(['act', 'dve']), stgb[sl][:, 0:D], stg[sl][:, 0:D], [f'stg{sl}'], [f'stgb{sl}'])
            dma(wout_s.ap()[ck], stgb[sl][:, 0:D], [f'stgb{sl}'], ['wout_s'], f"stgo{sl}")
        for cg in range(4):
            sl = n % 2
            n += 1
            dma(stg[sl][:, 0:512].rearrange("p (c k) -> p c k", k=128),
                keys.ap()[cg * 512:(cg + 1) * 512, :].rearrange("(c p) k -> p c k", p=128), [], [f'stg{sl}'], f"stg{sl}")
            for c4 in range(4):
                tr(ps[cg % 2][:, c4 * 128:(c4 + 1) * 128], stg[sl][:, c4 * 128:(c4 + 1) * 128], [f'stg{sl}'], [PS[cg % 2]])
            cp('dve', stgb[sl][:, 0:512], ps[cg % 2][:, :], [PS[cg % 2]], [f'stgb{sl}'])
            dma(keysT_s.ap()[:, cg * 512:(cg + 1) * 512], stgb[sl][:, 0:512], [f'stgb{sl}'], ['keysT_s'], f"stgo{sl}")
        for i in range(128):
            s3 = i % 3
            s2 = i % 2
            dma(ust[s3][:], u_d.ap()[i * 128:(i + 1) * 128, :], [], [f'ust{s3}'], f"ust{s3}")
            for b in range(2):
                bank = 2 + 2 * s2 + b
                for c4 in range(4):
                    dk = b * 4 + c4
                    tr(ps[bank][:, c4 * 128:(c4 + 1) * 128], ust[s3][:, dk * 128:(dk + 1) * 128], [f'ust{s3}'], [PS[bank]])
                cp('act' if b == 0 else 'dve', utb[s2][:, b * 4:(b + 1) * 4, :],
                   ps[bank][:, :].rearrange("p (c k) -> p c k", k=128), [PS[bank]], [f'utb{s2}_{b}'])
            dma(uT_s.ap()[i], utb[s2][:].rearrange("p a b -> p (a b)"), [f'utb{s2}_0', f'utb{s2}_1'], ['uT_s'], f"uto{s2}")
            dma(vst[s3][:], v_d.ap()[i * 128:(i + 1) * 128, :], [], [f'vst{s3}'], f"vst{s3}", eng='pool')
            cp('pool', vb[s2][:], vst[s3][:], [f'vst{s3}'], [f'vb{s2}'])
            dma(v_s.ap()[i], vb[s2][:], [f'vb{s2}'], ['v_s'], f"vo{s2}", eng='pool')
        S.end_phase()

    def run(R, xin, yout, L, Lown):
        nS = L // 128
        nT = Lown // 128
        ntile = L // TA
        ntile_own = Lown // TA
        if True:
            with ExitStack() as esa:
                w3_sb = {}
                w3_sb[('pos', R)] = sb(esa, f"w3pos_{R}", [64, 512])
                w3_sb[('neg', R)] = sb(esa, f"w3neg_{R}", [64, 512])
                w3z_sb = sb(esa, "w3z_sb", [64, 512])
                ld(w3_sb[('pos', R)], w3pos[R].ap(), f'w3pos_{R}', "c3"); ld(w3_sb[('neg', R)], w3neg[R].ap(), f'w3neg_{R}', "c0")
                ld(w3z_sb, w3zero.ap(), 'w3z_sb')
                ft = [sb(esa, f"ft{i}", [34, 512]) for i in range(2)]
                tp = [sb(esa, f"tp{i}", [1, 512]) for i in range(2)]
                va = [sb(esa, f"va{i}", [64, 512]) for i in range(2)]
                vi = [sb(esa, f"vi{i}", [64, 512], I32) for i in range(2)]
                h1 = [sb(esa, f"h1{i}", [65, 512]) for i in range(2)]
                h2 = [sb(esa, f"h2{i}", [64, 512]) for i in range(2)]
                ew = [sb(esa, f"ew{i}", [128, 512]) for i in range(2)]
                kf = [sb(esa, f"kf{i}", [128, 512]) for i in range(2)]
                kb = [sb(esa, f"kb{i}", [128, 512], BF16) for i in range(2)]
                nacc = sb(esa, "nacc", [128, 4, 2 * L // 512])
                k0 = sb(esa, "k0", [128, 4]); k0b = sb(esa, "k0b", [128, 4], BF16)
                nrm = sb(esa, "nrm", [128, 4])
                for i in range(2):
                    S.add('pool', (lambda i=i: nc.gpsimd.memset(h1[i][64:65, :], 1.0)), [], [f'h1one{i}'])
                S.add('pool', lambda: nc.gpsimd.memset(nacc[:], 0.0), [], ['nacc'])
                nft = 2 * L // 512

                def sin_layer(sl, psb, lhs, lhs_nm, rhs, rhs_nm, out, out_nm):
                    va_, vi_ = va[sl], vi[sl]
                    vn, vin = f'va{sl}', f'vi{sl}'
                    mm(ps[psb][0:64, :], lhs, rhs, True, True, [lhs_nm] + rhs_nm, [PS[psb]])
                    ts('dve', va_[:], ps[psb][0:64, :], fq_sb[:, 0:1], 8.5, ALU.mult, ALU.add, [PS[psb], 'fq_sb'], [vn])
                    cp('dve', vi_[:], va_[:], [vn], [vin])
                    stt(va_[:], va_[:], -0.5, vi_[:], ALU.add, ALU.subtract, [vn, vin], [vn])
                    stt(va_[:], va_[:], -0.5, va_[:], ALU.is_lt, ALU.add, [vn], [vn])
                    act(out, va_[:], AF.Sin, [vn], out_nm, scale=TWO_PI)

                def L1(jt):
                    sl = jt % 2
                    j0 = jt * 512
                    dma(ft[sl][:], feat[R].ap()[:, j0:j0 + 512], [], [f'ft{sl}'], f"ft{sl}")
                    dma(tp[sl][:], tpos[R].ap()[:, j0:j0 + 512], [], [f'tp{sl}'], f"tp{sl}")
                    sin_layer(sl, 0 if sl == 0 else 6, w1_sb[:], 'w1_sb', ft[sl][:], [f'ft{sl}'], h1[sl][0:64, :], [f'h1{sl}'])

                def L2(jt):
                    sl = jt % 2
                    sin_layer(sl, 1 if sl == 0 else 7, w2_sb[:], 'w2_sb', h1[sl][:], [f'h1{sl}', f'h1one{sl}'], h2[sl][:], [f'h2{sl}'])

                def L3(jt, qs):
                    sl = jt % 2
                    j0 = jt * 512
                    dirn = 'pos' if j0 < L else 'neg'
                    H2 = h2[sl]
                    hn = f'h2{sl}'
                    for q in qs:
                        e = (jt * 4 + q) % 2
                        pa = 2 + 2 * e
                        pb_ = 3 + 2 * e
                        mm(ps[pa][:, :], w3_sb[(dirn, R)][:, q * 128:(q + 1) * 128], H2[:], True, True,
                           [f'w3{dirn}_{R}', hn], [PS[pa]])
                        mm(ps[pb_][:, :], dec_sb[(dirn, R)][:, q * 128:(q + 1) * 128], tp[sl][:], True, True,
                           [f'dec{dirn}_{R}', f'tp{sl}'], [PS[pb_]])
                        act(ew[e][:], ps[pb_][:, :], AF.Exp, [PS[pb_]], [f'ew{e}'], scale=-1.0)
                        tt('dve', kf[e][:], ps[pa][:, :], ew[e][:], ALU.mult, [PS[pa], f'ew{e}'], [f'kf{e}'])
                        if jt == 0:
                            S.add('dve', (lambda e=e: nc.vector.memset(kf[e][:, 0:1], 0.0)), [f'kf{e}'], [f'kf{e}'])
                        if j0 == L:
                            mm(ps[pb_][:, 0:1], w3z_sb[:, q * 128:(q + 1) * 128], H2[:, 0:1], True, True,
                               ['w3z_sb', hn], [PS[pb_]])
                            cp('dve', kf[e][:, 0:1], ps[pb_][:, 0:1], [PS[pb_], f'kf{e}'], [f'kf{e}'])
                            cp('dve', k0[:, q:q + 1], ps[pb_][:, 0:1], [PS[pb_]], ['k0'])
                        S.add('dve', (lambda e=e, q=q, jt=jt: nc.vector.tensor_reduce(
                            out=nacc[:, q, jt:jt + 1], in_=kf[e][:], axis=AX.X, op=ALU.add, apply_absolute_value=True)),
                            [f'kf{e}'], ['nacc'])
                        cp('pool', kb[e][:], kf[e][:], [f'kf{e}'], [f'kb{e}'])
                        dma(kscr.ap()[q * 128:(q + 1) * 128, j0:j0 + 512], kb[e][:], [f'kb{e}'], [f'kscr_{q}_{jt}'], f"kbo{e}")

                L1(0); L2(0)
                for jt in range(nft):
                    if jt + 1 < nft:
                        L1(jt + 1)
                    L3(jt, [0, 1])
                    if jt + 1 < nft:
                        L2(jt + 1)
                    L3(jt, [2, 3])
                S.add('dve', lambda: nc.vector.tensor_reduce(out=nrm[:], in_=nacc[:], axis=AX.X, op=ALU.add), ['nacc'], ['nrm'])
                S.add('dve', lambda: nc.vector.reciprocal(out=rnorm[:], in_=nrm[:]), ['nrm'], ['rnorm'])
                tt('dve', nrm[:], nrm[:], hyb_sb[:], ALU.mult, ['nrm', 'hyb_sb'], ['nrm'])
                tt('dve', k0[:], k0[:], nrm[:], ALU.add, ['k0', 'nrm'], ['k0'])
                cp('dve', k0b[:], k0[:], ['k0'], ['k0b'])
                for q in range(4):
                    dma(kscr.ap()[q * 128:(q + 1) * 128, L:L + 1], k0b[:, q:q + 1], ['k0b', f'kscr_{q}_{L // 512}'],
                        [f'kscr_{q}_{L // 512}'], "k0o", slow=True)
                S.end_phase()
        with ExitStack() as esr:
            ZT = sb(esr, f"ZT{R}", [128, nS, 512], BF16)
            x0T = sb(esr, f"x0T{R}", [128, 4, Lown], BF16)
            with ExitStack() as esa:

                xst = [sb(esa, f"xst{i}", [128, 4, D]) for i in range(1)]
                xT = [sb(esa, f"xT{i}", [128, 8, TA], BF16) for i in range(2)]
                wb = [sb(esa, f"wb{i}", [128, 8, 128], BF16) for i in range(3)]
                pbf = [sb(esa, f"pbf{i}", [128, 3, TA + 2]) for i in range(2)]
                carry = sb(esa, "carry", [128, 8, 3, 2])
                tmm = sb(esa, "tmm", [128, TA + 2]); acc = [sb(esa, f"acc{i}", [128, TA]) for i in range(3)]
                yast = [sb(esa, f"yast{i}", [128, TA], BF16) for i in range(2)]
                zst = [sb(esa, f"zst{i}", [128, TA]) for i in range(2)]
                x0c = sb(esa, "x0c", [128, 8, 1]); x0cb = sb(esa, "x0cb", [128, 8, 1], BF16)
                wcnt = [0]

                def load_w(m):
                    s = wcnt[0] % 3
                    wcnt[0] += 1
                    dma(wb[s][:].rearrange("p a b -> p (a b)"), w_in_s.ap()[m], ['w_in_s'], [f'wb{s}'], f"wb{s}")
                    return s

                S.add('pool', lambda: nc.gpsimd.memset(carry[:], 0.0), [], ['carry'])
                dma(x0c[:], AP(xin, 0, [[1, 128], [128, 8], [1, 1]]), [], ['x0c'], "x0c", slow=True)
                cp('dve', x0cb[:], x0c[:], ['x0c'], ['x0cb'])
                for m in range(24):
                    s = load_w(m)
                    for dk in range(8):
                        mm(ps[7][:, m:m + 1], wb[s][:, dk, :], x0cb[:, dk, :], dk == 0, dk == 7, [f'wb{s}', 'x0cb'], [PS[7]])
                for half in range(2):
                    tt('dve', carry[:, half * 4:(half + 1) * 4, :, 1:2].rearrange("p g j o -> p j g o"),
                       ps[7][:, half * 12:(half + 1) * 12].rearrange("p (j g o) -> p j g o", j=3, o=1),
                       b_in_sb[:, half * 12:(half + 1) * 12].rearrange("p (j g o) -> p j g o", j=3, o=1),
                       ALU.add, [PS[7], 'b_in_sb', 'carry'], ['carry'])

                for it in range(ntile):
                    own = it < ntile_own
                    t0 = it * TA
                    xs_ = 0
                    xt_ = it % 2
                    nrow = min(TA, L - 1 - t0)
                    last = (it == ntile - 1)
                    if last:
                        S.add('pool', (lambda xs_=xs_: nc.gpsimd.memset(xst[xs_][:, 3, :], 0.0)), [f'xst{xs_}'], [f'xst{xs_}'])
                        dma(xst[xs_][:, 0:3, :], xin.ap()[t0 + 1:t0 + 385, :].rearrange("(s p) d -> p s d", p=128),
                            [f'xst{xs_}'], [f'xst{xs_}'], f"xst{xs_}")
                        dma(xst[xs_][0:127, 3, :], xin.ap()[t0 + 385:t0 + 512, :], [f'xst{xs_}'], [f'xst{xs_}'], f"xst{xs_}")
                    else:
                        dma(xst[xs_][:], xin.ap()[t0 + 1:t0 + 513, :].rearrange("(s p) d -> p s d", p=128),
                            [], [f'xst{xs_}'], f"xst{xs_}")
                    for dk in range(8):
                        bank = dk % 2
                        for s4 in range(4):
                            tr(ps[bank][:, s4 * 128:(s4 + 1) * 128], xst[xs_][:, s4, dk * 128:(dk + 1) * 128], [f'xst{xs_}'], [PS[bank]])
                        cp('act' if dk % 2 == 0 else 'dve', xT[xt_][:, dk, :], ps[bank][:, :], [PS[bank]], [f'xT{xt_}'])
                    groups = list(range(8)) if own else list(range(4, 8))
                    for g in groups:
                        q = g % 4
                        isA = g < 4
                        ms = [j * 4 + q for j in range(3)] if isA else [12 + j * 4 + q for j in range(3)]
                        js = [0, 1, 2] if (isA or own) else [1, 2]
                        pbs = g % 2
                        P_ = pbf[pbs]
                        pn = f'pbf{pbs}'
                        cp('pool', P_[:, :, 0:2], carry[:, g, :, :], ['carry', pn], [pn])
                        for j in js:
                            m = ms[j]
                            s = load_w(m)
                            bank = (2 if g % 2 == 0 else 5) + (j % 3)
                            for dk in range(8):
                                mm(ps[bank][:, :], wb[s][:, dk, :], xT[xt_][:, dk, :], dk == 0, dk == 7, [f'wb{s}', f'xT{xt_}'], [PS[bank]])
                            act(P_[:, j, 2:TA + 2], ps[bank][:, :], AF.Identity, [PS[bank], 'b_in_sb', pn], [pn],
                                bias=b_in_sb[:, m:m + 1], scale=1.0)
                        if last:
                            S.add('pool', (lambda P_=P_: nc.gpsimd.memset(P_[:, :, TA + 1:TA + 2], 0.0)), [pn], [pn])
                        cp('pool', carry[:, g, :, :], P_[:, :, TA:TA + 2], [pn, 'carry'], ['carry'])
                        if isA:
                            cw = acw_sb[R]
                            cn = f'acw_{R}'
                            tt('dve', tmm[:], P_[:, 1, :], P_[:, 2, :], ALU.mult, [pn], ['tmm'])
                            ts('dve', acc[0][:], tmm[:, 0:TA], cw[:, q, 0:1], None, ALU.mult, None, ['tmm', cn], ['acc0'])
                            stt(acc[0][:], tmm[:, 1:TA + 1], cw[:, q, 1:2], acc[0][:], ALU.mult, ALU.add, ['tmm', cn, 'acc0'], ['acc0'])
                            stt(acc[0][:], tmm[:, 2:TA + 2], cw[:, q, 2:3], acc[0][:], ALU.mult, ALU.add, ['tmm', cn, 'acc0'], ['acc0'])
                            ys_ = q % 2
                            tt('dve', yast[ys_][:], P_[:, 0, 1:TA + 1], acc[0][:], ALU.mult, [pn, 'acc0'], [f'yast{ys_}'])
                            dma(ya_s.ap()[q, :, t0:t0 + TA], yast[ys_][:], [f'yast{ys_}'], ['ya_s'], f"yao{ys_}")
                        else:
                            cw = hcw_sb[R]
                            cn = f'hcw_{R}'
                            for j in js:
                                c = j * 4 + q
                                a = acc[j]
                                an = f'acc{j}'
                                ts('dve', a[:], P_[:, j, 0:TA], cw[:, c, 0:1], hcb_sb[:, c:c + 1], ALU.mult, ALU.add, [pn, cn, 'hcb_sb'], [an])
                                stt(a[:], P_[:, j, 1:TA + 1], cw[:, c, 1:2], a[:], ALU.mult, ALU.add, [pn, cn, an], [an])
                                if j == 0:
                                    stt(x0T[:, q, t0:t0 + TA], P_[:, j, 2:TA + 2], cw[:, c, 2:3], a[:], ALU.mult, ALU.add,
                                        [pn, cn, an], [f'x0T_{q}'])
                                else:
                                    stt(a[:], P_[:, j, 2:TA + 2], cw[:, c, 2:3], a[:], ALU.mult, ALU.add, [pn, cn, an], [an])
                            zs = q % 2
                            tt('dve', zst[zs][:], acc[1][:], acc[2][:], ALU.mult, ['acc1', 'acc2'], [f'zst{zs}'])
                            bank = q % 2
                            for s4 in range(4):
                                tr(ps[bank][:, s4 * 128:(s4 + 1) * 128], zst[zs][:, s4 * 128:(s4 + 1) * 128], [f'zst{zs}'], [PS[bank]])
                            cp('act', ZT[:, it * 4:(it + 1) * 4, q * 128:(q + 1) * 128],
                               ps[bank][:, :].rearrange("p (s c) -> p s c", c=128), [PS[bank]], ['ZT'])
                S.end_phase()

            with ExitStack() as esc:
                base = L + 1 - 128 * nT
                W = 128 * (nS + nT - 1)
                Wh = (W // 2 + 127) // 128 * 128 + 128
                kw = [[sb(esc, f"kw{b}_{h}", [128, Wh], BF16) for h in range(2)] for b in range(2)]
                yconv = sb(esc, "yconv", [128, 128, nT], BF16)
                yhst = [sb(esc, f"yhst{i}", [128, 512], BF16) for i in range(2)]

                def _half(dd):
                    off = L - 127 - 128 * dd - base
                    return 0 if off + 128 <= Wh else 1
                assert _half(0) == 0
                Ds = [0] + [d for d in range(-(nS - 1), nT) if d != 0 and _half(d) == 0] + \
                    [d for d in range(-(nS - 1), nT) if d != 0 and _half(d) == 1]
                nper = 512 // nT
                for q in range(4):
                    for cc in range(128):
                        cg = q * 128 + cc
                        b = cc % 2
                        for h in range(2):
                            m0 = 0 if h == 0 else W - Wh
                            dma(kw[b][h][:], AP(kscr, cg * 2 * Lmax + base + m0, [[1, 128], [1, Wh]]),
                                ['kscr'], [f'kw{b}_{h}'], f"kw{b}_{h}", eng=('sp' if h == 0 else 'pool'))
                        grp = cc // nper
                        bank = grp % 2
                        col0 = (cc % nper) * nT
                        for di, dd in enumerate(Ds):
                            off = L - 127 - 128 * dd - base
                            if off + 128 <= Wh:
                                h, o = 0, off
                            else:
                                h, o = 1, off - (W - Wh)
                            assert 0 <= o and o + 128 <= Wh
                            T0 = max(0, dd)
                            T1 = min(nT, nS + dd)
                            mm(ps[bank][:, col0 + T0:col0 + T1], kw[b][h][:, o:o + 128], ZT[:, T0 - dd:T1 - dd, cg],
                               di == 0, di == len(Ds) - 1, [f'kw{b}_{h}', 'ZT'], [PS[bank]])
                        if cc % nper == nper - 1:
                            c0 = grp * nper
                            cp('act' if grp % 2 == 0 else 'dve', yconv[:, c0:c0 + nper, :],
                               ps[bank][:, :].rearrange("p (c t) -> p c t", t=nT), [PS[bank]], ['yconv'])
                    for T4 in range(nT // 4):
                        bank = 2 + T4 % 2
                        for s4 in range(4):
                            T = T4 * 4 + s4
                            mm(ps[bank][:, s4 * 128:(s4 + 1) * 128], yconv[:, :, T], jbf[:], True, True, ['yconv', 'jbf'], [PS[bank]])
                        ys_ = T4 % 2
                        stt(yhst[ys_][:], ps[bank][:, :], rnorm[:, q:q + 1], x0T[:, q, T4 * 512:(T4 + 1) * 512],
                            ALU.mult, ALU.mult, [PS[bank], 'rnorm', f'x0T_{q}'], [f'yhst{ys_}'])
                        dma(ya_s.ap()[4 + q, :, T4 * 512:(T4 + 1) * 512], yhst[ys_][:], [f'yhst{ys_}'], ['ya_s'], f"yho{ys_}")
                S.end_phase()

            esr.close()
            with ExitStack() as ese:
                g1 = sb(ese, "g1", [128, D]); b1 = sb(ese, "b1", [128, D]); g2 = sb(ese, "g2", [128, D]); b2 = sb(ese, "b2", [128, D])
                for t, src, name in ((g1, ln1g, 'g1'), (b1, ln1b, 'b1'), (g2, ln2g, 'g2'), (b2, ln2b, 'b2')):
                    dma(t[:], AP(src, 0, [[0, 128], [1, D]]), [], [name], "c1")
                keysT = sb(ese, "keysT", [128, 16, 128], BF16)
                dma(keysT[:].rearrange("p a b -> p (a b)"), keysT_s.ap(), [], ['keysT'], "c2")
                wob = [sb(ese, f"wob{i}", [128, D], BF16) for i in range(3)]
                yat = [sb(ese, f"yat{i}", [128, 8, TT], BF16) for i in range(1)]
                xres = [sb(ese, f"xres{i}", [128, D]) for i in range(1)]
                hh1 = sb(ese, "hh1", [128, 2, D]); x1 = [sb(ese, f"x1_{i}", [128, 2, D]) for i in range(2)]
                hh2 = hh1
                x1T = [sb(ese, f"x1T{i}", [128, 8, TT], BF16) for i in range(2)]
                st6 = sb(ese, "st6", [128, 2, 6]); mv = sb(ese, "mv", [128, 2]); rstd = sb(ese, "rstd", [128, 1])
                wqb = [sb(ese, f"wqb{i}", [128, 8, 128], BF16) for i in range(3)]
                qT = sb(ese, "qT", [128, 16, TT], BF16)
                scr8 = sb(ese, "scr8", [128, 2048]); scw = sb(ese, "scw", [128, 128])
                sc = scr8
                top = sb(ese, "top", [128, 16, 16]); tidx = sb(ese, "tidx", [128, 16, 16], U32); tidf = sb(ese, "tidf", [128, 16, 16])
                cand = scr8[:].rearrange("p (h x) -> p h x", x=256); candw = sb(ese, "candw", [128, 256])
                best = sb(ese, "best", [128, 8, 16]); pos = sb(ese, "pos", [128, 8, 16], U32)
                pa_u = sb(ese, "pa_u", [128, 8, 16], U32); pb_u = sb(ese, "pb_u", [128, 8, 16], U32)
                pa_f = sb(ese, "pa_f", [128, 8, 16]); pb_f = sb(ese, "pb_f", [128, 8, 16])
                eq = scr8[:].rearrange("p (h a b) -> p h a b", a=16, b=16)
                lst = sb(ese, "lst", [128, 3, 128])
                nmax = sb(ese, "nmax", [128, 8]); ssum = sb(ese, "ssum", [128, 8]); rsum = sb(ese, "rsum", [128, 8])
                lstT = sb(ese, "lstT", [128, 3, TT], BF16)
                TB = 8
                Pm = [sb(ese, f"Pm{i}", [128, TB, 128], BF16) for i in range(2)]
                Qm = [sb(ese, f"Qm{i}", [128, TB, 128], BF16) for i in range(2)]
                Wt = sb(ese, "Wt", [128, 128, TT], BF16)
                ub = [sb(ese, f"ub{i}", [128, 8, 128], BF16) for i in range(6)]
                vbb = [sb(ese, f"vbb{i}", [128, D], BF16) for i in range(4)]
                G = [sb(ese, f"G{i}", [128, TT], BF16) for i in range(4)]
                cf = [sb(ese, f"cf{i}", [128, TT], BF16) for i in range(4)]
                wqc = [0]
                woc = [0]

                def layer_norm(src, sn, dst, dn, g, gn, b, bn):
                    for hf in range(2):
                        S.add('dve', (lambda hf=hf: nc.vector.bn_stats(out=st6[:, hf, :], in_=src[:, hf * 512:(hf + 1) * 512])), [sn], ['st6'])
                    S.add('dve', lambda: nc.vector.bn_aggr(out=mv[:], in_=st6[:].rearrange("p a b -> p (a b)")), ['st6'], ['mv'])
                    act(rstd[:], mv[:, 1:2], AF.Sqrt, ['mv'], ['rstd'], bias=float(LN_EPS), scale=1.0)
                    S.add('dve', lambda: nc.vector.reciprocal(out=rstd[:], in_=rstd[:]), ['rstd'], ['rstd'])
                    ts('dve', dst, src, mv[:, 0:1], rstd[:, 0:1], ALU.subtract, ALU.mult, [sn, 'mv', 'rstd'], [dn])
                    tt('pool', dst, dst, g[:], ALU.mult, [dn, gn], [dn])
                    tt('pool', dst, dst, b[:], ALU.add, [dn, bn], [dn])

                ntl = Lown // TT
                V = nc.vector

                def prep_stages(it):
                    t0 = it * TT
                    xb = it % 2
                    X1 = x1[xb]
                    X1T = x1T[xb]
                    xtn = f'x1T{xb}'
                    st = []

                    def s_load():
                        dma(yat[0][:], ya_s.ap()[:, :, t0:t0 + TT].rearrange("q p t -> p q t"), ['ya_s'], ['yat0'], "yat0", eng='act')
                    st.append(s_load)
                    for hh in range(2):
                        def s_d(hh=hh):
                            dma(xres[0][:], xin.ap()[t0 + hh * 128:t0 + (hh + 1) * 128, :], [], ['xres0'], "xres0", eng='act')
                            for ck in range(8):
                                ws = woc[0] % 3
                                woc[0] += 1
                                dma(wob[ws][:], wout_s.ap()[ck], [], [f'wob{ws}'], f"wob{ws}", eng='act')
                                for dh in range(2):
                                    mm(ps[4 + dh][:, :], yat[0][:, ck, hh * 128:(hh + 1) * 128], wob[ws][:, dh * 512:(dh + 1) * 512],
                                       ck == 0, ck == 7, ['yat0', f'wob{ws}'], [PS[4 + dh]])
                            for dh in range(2):
                                stt(hh1[:, hh, dh * 512:(dh + 1) * 512], xres[0][:, dh * 512:(dh + 1) * 512], float(ALPHA), ps[4 + dh][:, :],
                                    ALU.mult, ALU.add, ['xres0', PS[4 + dh]], [f'hh1_{hh}'])
                        st.append(s_d)

                        def s_ln(hh=hh):
                            layer_norm(hh1[:, hh, :], f'hh1_{hh}', X1[:, hh, :], f'x1_{xb}_{hh}', g1, 'g1', b1, 'b1')
                        st.append(s_ln)

                        def s_tr(hh=hh):
                            for b4 in range(2):
                                bank = 4 + b4
                                for c4 in range(4):
                                    dk = b4 * 4 + c4
                                    tr(ps[bank][:, c4 * 128:(c4 + 1) * 128], X1[:, hh, dk * 128:(dk + 1) * 128], [f'x1_{xb}_{hh}'], [PS[bank]])
                                cp('act', X1T[:, b4 * 4:(b4 + 1) * 4, hh * 128:(hh + 1) * 128],
                                   ps[bank][:, :].rearrange("p (c t) -> p c t", t=128), [PS[bank]], [xtn])
                        st.append(s_tr)
                    for c4g in range(4):
                        def s_q(c4g=c4g):
                            for c4 in range(4):
                                cc = c4g * 4 + c4
                                s_ = wqc[0] % 3
                                wqc[0] += 1
                                dma(wqb[s_][:].rearrange("p a b -> p (a b)"), wq_s.ap()[cc], [], [f'wqb{s_}'], f"wqb{s_}", eng='act')
                                bank = 4 + cc % 2
                                for dk in range(8):
                                    mm(ps[bank][:, 0:TT], wqb[s_][:, dk, :], X1T[:, dk, :], dk == 0, dk == 7, [f'wqb{s_}', xtn], [PS[bank]])
                                cp('act' if cc % 2 == 0 else 'dve', qT[:, cc, :], ps[bank][:, 0:TT], [PS[bank]], ['qT'])
                        st.append(s_q)
                    for hh in range(2):
                        def s_sc(hh=hh):
                            for c4g in range(4):
                                bank = 4 + c4g % 2
                                for c4 in range(4):
                                    cc = c4g * 4 + c4
                                    mm(ps[bank][:, c4 * 128:(c4 + 1) * 128], qT[:, cc, hh * 128:(hh + 1) * 128], keysT[:, cc, :], True, True,
                                       ['qT', 'keysT'], [PS[bank]])
                                cp('act' if c4g % 2 == 0 else 'dve', sc[:, c4g * 512:(c4g + 1) * 512], ps[bank][:, :], [PS[bank]], ['scr8'])
                        st.append(s_sc)
                        for gh in range(2):
                            def s_tk(gh=gh):
                                for g in range(gh * 8, gh * 8 + 8):
                                    sg = sc[:, g * 128:(g + 1) * 128]
                                    S.add('dve', (lambda g=g, sg=sg: V.max(out=top[:, g, 0:8], in_=sg)), ['scr8'], ['top'])
                                    S.add('dve', (lambda g=g, sg=sg: V.max_index(out=tidx[:, g, 0:8], in_max=top[:, g, 0:8], in_values=sg)), ['scr8', 'top'], ['tidx'])
                                    S.add('dve', (lambda g=g, sg=sg: V.match_replace(out=scw[:], in_to_replace=top[:, g, 0:8], in_values=sg, imm_value=-1e30)),
                                          ['scr8', 'top'], ['scw'])
                                    S.add('dve', (lambda g=g: V.max(out=top[:, g, 8:16], in_=scw[:])), ['scw'], ['top'])
                                    S.add('dve', (lambda g=g: V.max_index(out=tidx[:, g, 8:16], in_max=top[:, g, 8:16], in_values=scw[:])), ['scw', 'top'], ['tidx'])
                            st.append(s_tk)

                        def s_cand():
                            cp('dve', tidf[:], tidx[:], ['tidx'], ['tidf'])
                            topv = top[:].rearrange("p (h two) k -> p h two k", two=2)
                            tt('dve', cand[:].rearrange("p h (a b) -> p h a b", b=16),
                               topv[:, :, 0, :].unsqueeze(3).to_broadcast([128, 8, 16, 16]),
                               topv[:, :, 1, :].unsqueeze(2).to_broadcast([128, 8, 16, 16]), ALU.add, ['top'], ['scr8'])
                            for h in range(8):
                                ch = cand[:, h, :]
                                S.add('dve', (lambda h=h, ch=ch: V.max(out=best[:, h, 0:8], in_=ch)), ['scr8'], ['best'])
                                S.add('dve', (lambda h=h, ch=ch: V.max_index(out=pos[:, h, 0:8], in_max=best[:, h, 0:8], in_values=ch)), ['scr8', 'best'], ['pos'])
                                S.add('dve', (lambda h=h, ch=ch: V.match_replace(out=candw[:], in_to_replace=best[:, h, 0:8], in_values=ch, imm_value=-1e30)),
                                      ['scr8', 'best'], ['candw'])
                                S.add('dve', (lambda h=h: V.max(out=best[:, h, 8:16], in_=candw[:])), ['candw'], ['best'])
                                S.add('dve', (lambda h=h: V.max_index(out=pos[:, h, 8:16], in_max=best[:, h, 8:16], in_values=candw[:])), ['candw', 'best'], ['pos'])
                        st.append(s_cand)

                        def s_sm(hh=hh):
                            gv = lst[:, 2, :].rearrange("p (h k) -> p h k", k=16)
                            tt('dve', gv, best[:], best[:, :, 0:1].to_broadcast([128, 8, 16]), ALU.subtract, ['best'], ['lst_g'])
                            act(gv, gv, AF.Exp, ['lst_g'], ['lst_g'])
                            S.add('dve', lambda: V.tensor_reduce(out=ssum[:], in_=gv, axis=AX.X, op=ALU.add), ['lst_g'], ['ssum'])
                            S.add('dve', lambda: V.reciprocal(out=rsum[:], in_=ssum[:]), ['ssum'], ['rsum'])
                            tt('dve', gv, gv, rsum[:].unsqueeze(2).to_broadcast([128, 8, 16]), ALU.mult, ['lst_g', 'rsum'], ['lst_g'])
                            S.add('dve', lambda: V.tensor_single_scalar(out=pa_u[:], in_=pos[:], scalar=4, op=ALU.logical_shift_right), ['pos'], ['pa_u'])
                            S.add('dve', lambda: V.tensor_single_scalar(out=pb_u[:], in_=pos[:], scalar=15, op=ALU.bitwise_and), ['pos'], ['pb_u'])
                            cp('dve', pa_f[:], pa_u[:], ['pa_u'], ['pa_f'])
                            cp('dve', pb_f[:], pb_u[:], ['pb_u'], ['pb_f'])
                            tidv = tidf[:].rearrange("p (h two) k -> p h two k", two=2)
                            for wi, (pf, pn_) in enumerate(((pa_f, 'pa_f'), (pb_f, 'pb_f'))):
                                tt('dve', eq[:], pf[:].unsqueeze(3).to_broadcast([128, 8, 16, 16]),
                                   iota16[:].unsqueeze(1).unsqueeze(1).to_broadcast([128, 8, 16, 16]), ALU.is_equal, [pn_, 'iota16'], ['scr8'])
                                tt('dve', eq[:], eq[:], tidv[:, :, wi, :].unsqueeze(2).to_broadcast([128, 8, 16, 16]), ALU.mult, ['scr8', 'tidf'], ['scr8'])
                                S.add('dve', (lambda wi=wi: V.tensor_reduce(out=lst[:, wi, :].rearrange("p (h k) -> p h k", k=16), in_=eq[:], axis=AX.X, op=ALU.add)),
                                      ['scr8'], [f'lst_{wi}'])
                            for wi in range(3):
                                tr(ps[4][:, wi * 128:(wi + 1) * 128], lst[:, wi, :], ['lst_0', 'lst_1', 'lst_g'], [PS[4]])
                            cp('act', lstT[:, :, hh * 128:(hh + 1) * 128], ps[4][:, 0:384].rearrange("p (w t) -> p w t", t=128), [PS[4]], ['lstT'])
                        st.append(s_sm)
                    return st

                def e4(it):
                    for tb in range(TT // TB):
                        s = tb % 2
                        tsl = slice(tb * TB, (tb + 1) * TB)
                        iob = iota128b[:].unsqueeze(1).to_broadcast([128, TB, 128])
                        tt('dve', Pm[s][:], iob, lstT[:, 0, tsl].unsqueeze(2).to_broadcast([128, TB, 128]), ALU.is_equal,
                           ['iota128b', 'lstT'], [f'Pm{s}'])
                        tt('dve', Qm[s][:], iob, lstT[:, 1, tsl].unsqueeze(2).to_broadcast([128, TB, 128]), ALU.is_equal,
                           ['iota128b', 'lstT'], [f'Qm{s}'])
                        tt('pool', Qm[s][:], Qm[s][:], lstT[:, 2, tsl].unsqueeze(2).to_broadcast([128, TB, 128]), ALU.mult,
                           [f'Qm{s}', 'lstT'], [f'Qm{s}'])
                        for t4 in range(TB // 4):
                            bank = 4 + (tb * (TB // 4) + t4) % 2
                            for k4 in range(4):
                                tl = t4 * 4 + k4
                                mm(ps[bank][:, k4 * 128:(k4 + 1) * 128], Qm[s][:, tl, :], Pm[s][:, tl, :], True, True, [f'Qm{s}', f'Pm{s}'], [PS[bank]])
                            tg = tb * TB + t4 * 4
                            cp('act', Wt[:, :, tg:tg + 4], ps[bank][:, :].rearrange("p (t i) -> p i t", i=128), [PS[bank]], ['Wt'])

                def f_evac(it):
                    X1 = x1[it % 2]
                    for hh in range(2):
                        for dh in range(2):
                            bank = hh * 2 + dh
                            stt(hh2[:, hh, dh * 512:(dh + 1) * 512], X1[:, hh, dh * 512:(dh + 1) * 512], float(ALPHA), ps[bank][:, :],
                                ALU.mult, ALU.add, [f'x1_{it % 2}_{hh}', PS[bank]], [f'hh1_{hh}'])

                def f_stages(it):
                    t0 = it * TT
                    st = []
                    for hh in range(2):
                        def s_f(hh=hh):
                            layer_norm(hh2[:, hh, :], f'hh1_{hh}', hh2[:, hh, :], f'hh1_{hh}', g2, 'g2', b2, 'b2')
                            dma(yout.ap()[t0 + hh * 128:t0 + (hh + 1) * 128, :], hh2[:, hh, :], [f'hh1_{hh}'], [f'yout{R}'], f"osto{hh}")
                        st.append(s_f)
                    return st

                def load_u(i):
                    s = i % 6
                    dma(ub[s][:].rearrange("p a b -> p (a b)"), uT_s.ap()[i], [], [f'ub{s}'], f"ub{s}")

                def load_v(i):
                    s = i % 4
                    dma(vbb[s][:], v_s.ap()[i], [], [f'vbb{s}'], f"vbb{s}", eng='pool')

                def s_mm(it, i):
                    s = i % 6
                    bank = 6 + i % 2
                    c0 = 0
                    pn = PS[bank]
                    X1T = x1T[it % 2]
                    for dk in range(8):
                        mm(ps[bank][:, c0:c0 + TT], ub[s][:, dk, :], X1T[:, dk, :], dk == 0, dk == 7, [f'ub{s}', f'x1T{it % 2}'], [pn])
                    e = i % 4
                    act(G[e][:], ps[bank][:, c0:c0 + TT], AF.Gelu, [pn], [f'G{e}'])
                    tt('dve', cf[e][:], G[e][:], Wt[:, i, :], ALU.mult, [f'G{e}', 'Wt'], [f'cf{e}'])

                def o_mm(i):
                    s = i % 4
                    e = i % 4
                    for hh in range(2):
                        for dh in range(2):
                            bank = hh * 2 + dh
                            mm(ps[bank][:, :], cf[e][:, hh * 128:(hh + 1) * 128], vbb[s][:, dh * 512:(dh + 1) * 512], i == 0, i == 127,
                               [f'cf{e}', f'vbb{s}'], [PS[bank]])

                for stg_ in prep_stages(0):
                    stg_()
                e4(0)
                pend = []
                for it in range(ntl):
                    stages = list(pend)
                    if it + 1 < ntl:
                        stages += prep_stages(it + 1)
                    items = Sched.hoist(S.record(stages), 30)
                    nit = len(items)
                    wmap = {'dve': 4.0, 'pool': 3.0, 'act': 1.0, 'pe': 1.5, 'sp': 0.5}
                    cumw = []
                    acc_w = 0.0
                    for it_ in items:
                        acc_w += wmap.get(it_[0], 1.0)
                        cumw.append(acc_w)
                    totw = max(acc_w, 1.0)
                    for i in range(5):
                        load_u(i)
                    for i in range(3):
                        load_v(i)
                    s_mm(it, 0)
                    s_mm(it, 1)
                    done = 0
                    for i in range(128):
                        if i + 5 < 128:
                            load_u(i + 5)
                        if i + 3 < 128:
                            load_v(i + 3)
                        if i + 2 < 128:
                            s_mm(it, i + 2)
                        o_mm(i)
                        upto = done
                        while upto < nit and cumw[upto] <= (i + 1) * totw / 118.0:
                            upto += 1
                        if _DBG_NOINTER:
                            upto = done
                        S.replay(items[done:upto])
                        done = max(done, upto)
                    S.replay(items[done:])
                    f_evac(it)
                    if it + 1 < ntl:
                        e4(it + 1)
                    pend = f_stages(it)
                for stg_ in pend:
                    stg_()
                S.end_phase()

    run('S', xs, ys, Ls, Ls)
    run('P', xp, yp, Lp, LpO)
    es.close()
    return nc, S


def _feat_tables(L):
    n = 2 * L
    jp = np.arange(n)
    pos = np.abs(L - jp).astype(np.float64)
    pos[0] = 0.0
    t = pos / (L - 1)
    w = 2.0 * math.pi * pos / L
    f = np.linspace(1e-4, 15.0, 16)
    feat = np.concatenate([t[None], np.cos(f[:, None] * w[None]), -np.sin(f[:, None] * w[None]), np.ones((1, n))], 0)
    return feat.astype(np.float32), t[None].astype(np.float32)


_CACHE = {}


def _prep_inputs(Lp, Ls, x_prompt, x_sample, w_in, b_in, a_conv_w, h_conv_w, h_conv_b, hf_w1, hf_b1, hf_freq,
                 hf_w2, hf_b2, hf_w3, hy_decay, hy_bias, w_out, ln1_g, ln1_b, peer_wq, peer_keys,
                 peer_u, peer_v, ln2_g, ln2_b):
    c = np.ascontiguousarray
    f32 = np.float32
    featP, tposP = _feat_tables(Lp)
    featS, tposS = _feat_tables(Ls)
    acw = np.asarray(a_conv_w[0]); hcw = np.asarray(h_conv_w[0])

    def cwl(a, nch):
        return c(a.reshape(3, nch, 128).transpose(2, 1, 0))
    w3 = np.asarray(hf_w3[0]); w3d = [c(w3[:, :512]), c(w3[:, 512:])]
    dec = np.asarray(hy_decay[0]); decd = [c(dec[0:1]), c(dec[1:2])]
    shared = dict(
        w_in=c(np.asarray(w_in[0])), b_in_t=c(np.asarray(b_in[0]).reshape(24, 128).T),
        acw_s=cwl(acw, 4), hcw_s=cwl(hcw, 12), hcb=c(np.asarray(h_conv_b[0]).reshape(12, 128).T),
        w1aug=c(np.concatenate([np.asarray(hf_w1[0]), np.asarray(hf_b1[0])[None]], 0)),
        w2aug=c(np.concatenate([np.asarray(hf_w2[0]), np.asarray(hf_b2[0])[None]], 0)),
        freq=c(np.asarray(hf_freq[0]).reshape(64, 1)),
        w3pos_s=w3d[0], w3neg_s=w3d[1], w3zero=w3d[0], decpos_s=decd[0], decneg_s=decd[1],
        feat_p=featP, tpos_p=tposP, feat_s=featS, tpos_s=tposS,
        hyb=c(np.asarray(hy_bias[0]).reshape(4, 128).T), w_out=c(np.asarray(w_out[0])),
        ln1g=c(np.asarray(ln1_g[0])[None]), ln1b=c(np.asarray(ln1_b[0])[None]),
        ln2g=c(np.asarray(ln2_g[0])[None]), ln2b=c(np.asarray(ln2_b[0])[None]),
        wq=c(np.asarray(peer_wq[0])), keys=c(np.asarray(peer_keys[0]).reshape(2048, 128)),
        u=c(np.asarray(peer_u[0])), v=c(np.asarray(peer_v[0])),
        ident=np.eye(128, dtype=f32), jmat=c(np.eye(128, dtype=f32)[::-1]),
        iota128=c(np.tile(np.arange(128, dtype=f32)[None], (128, 1))),
        iota16=c(np.tile(np.arange(16, dtype=f32)[None], (128, 1))),
    )
    shared = {k: np.asarray(v, dtype=f32) for k, v in shared.items()}
    in_maps = []
    xpn = np.asarray(x_prompt); xsn = np.asarray(x_sample)
    for k in range(NCORES):
        b, half = k // 2, k % 2
        m = dict(shared)
        if half == 0:
            m['xp'] = c(xpn[b]); m['acw_p'] = shared['acw_s']; m['hcw_p'] = shared['hcw_s']
            m['w3pos_p'] = w3d[0]; m['w3neg_p'] = w3d[1]; m['decpos_p'] = decd[0]; m['decneg_p'] = decd[1]
        else:
            m['xp'] = c(xpn[b][::-1]); m['acw_p'] = cwl(acw[::-1], 4); m['hcw_p'] = cwl(hcw[::-1], 12)
            m['w3pos_p'] = w3d[1]; m['w3neg_p'] = w3d[0]; m['decpos_p'] = decd[1]; m['decneg_p'] = decd[0]
        m['xs'] = c(xsn[k])
        in_maps.append({kk: np.asarray(vv, dtype=f32) for kk, vv in m.items()})
    return in_maps


def kernel(**inputs):
    xpn = inputs['x_prompt']; xsn = inputs['x_sample']
    B, Lp, _ = xpn.shape
    Bs, Ls, _ = xsn.shape
    assert B == 4 and Bs == 8
    key = (Lp, Ls)
    if key not in _CACHE:
        _CACHE[key] = build_nc(Lp, Ls)[0]
    nc = _CACHE[key]
    in_maps = _prep_inputs(Lp, Ls, **inputs)
    res = run_bass_kernel_spmd(nc, in_maps, core_ids=list(range(NCORES)))
    yp = np.zeros((B, Lp, D), np.float32)
    ys = np.zeros((Bs, Ls, D), np.float32)
    h = Lp // 2
    for k in range(NCORES):
        r = res.results[k]
        b, half = k // 2, k % 2
        if half == 0:
            yp[b, :h] = r['yp']
        else:
            yp[b, h:] = r['yp'][::-1]
        ys[k] = r['ys']
    return (yp, ys)
```

```python
import math
from contextlib import ExitStack
import numpy as np
import concourse.bass as bass
import concourse.mybir as mybir
from concourse.bass_utils import run_bass_kernel_spmd
from concourse.bass_types import AP

F32 = mybir.dt.float32
BF16 = mybir.dt.bfloat16
U32 = mybir.dt.uint32
I32 = mybir.dt.int32
AF = mybir.ActivationFunctionType
ALU = mybir.AluOpType
AX = mybir.AxisListType

D = 1024
NCORES = 8
ALPHA = 2.0 ** 0.25
LN_EPS = 1e-5
TWO_PI = 6.283185
TA = 512
TT = 256
import os
_DBG_NOINTER = bool(os.environ.get('NOINTER'))


class Sched:
    EP = 30000
    DEP = 1800

    def __init__(self, nc, es):
        self.nc = nc
        self.es = es
        self.eng = {'pe': nc.tensor, 'act': nc.scalar, 'dve': nc.vector, 'pool': nc.gpsimd, 'sp': nc.sync}
        self.ops = []
        self.lastw = {}
        self.readers = {}
        self.lastkey = {}
        self.cnt = {e: 0 for e in self.eng}
        self.sems = {}
        self.ksem = {}
        self.free_sems = {'sw': [], 'hw': []}
        self.waited = {}
        self.nsem = 0
        self.n_inst = 0
        self.capture = None

    def record(self, fns):
        self.capture = []
        for f in fns:
            f()
        out = self.capture
        self.capture = None
        return out

    @staticmethod
    def hoist(items, H):
        items = list(items)
        for idx in range(len(items)):
            eng, fn, r, w, key = items[idx]
            if key is None:
                continue
            toks = set(w) | set(r)
            lim = max(0, idx - H)
            pos = idx
            while pos > lim:
                pe, pf, pr, pw, pk = items[pos - 1]
                if pk is not None and (pk == key or pe == eng):
                    break
                if (set(pw) & toks) or (set(pr) & set(w)):
                    break
                pos -= 1
            if pos != idx:
                it_ = items.pop(idx)
                items.insert(pos, it_)
        return items

    def replay(self, items):
        for it in items:
            self.add(*it)

    def _sem(self, name):
        self.nsem += 1
        return self.es.enter_context(self.nc.semaphore(f"s{self.nsem}_{name}"))

    def add(self, eng, fn, r=(), w=(), key=None):
        if self.capture is not None:
            self.capture.append((eng, fn, tuple(r), tuple(w), key))
            return -1
        i = len(self.ops)
        deps = set()
        for t in r:
            if t in self.lastw:
                deps.add(self.lastw[t])
        for t in w:
            if t in self.lastw:
                deps.add(self.lastw[t])
            for o in self.readers.get(t, {}).values():
                deps.add(o)
        if key is not None and key in self.lastkey:
            deps.add(self.lastkey[key])
        rk = eng if key is None else ('dma', i)
        for t in r:
            self.readers.setdefault(t, {})[rk] = i
        for t in w:
            self.lastw[t] = i
            self.readers[t] = {}
        if key is not None:
            self.lastkey[key] = i
        self.ops.append(dict(eng=eng, fn=fn, deps=deps, key=key, marked=False, sem=None))
        return i

    def end_phase(self):
        last = {}
        dmas = []
        for i, op in enumerate(self.ops):
            if op['key'] is None:
                if op['fn'] is not None:
                    last[op['eng']] = i
            else:
                dmas.append(i)
        deps = set(last.values()) | set(dmas)
        for e in self.eng:
            self.ops.append(dict(eng=e, fn=None, deps=set(deps), key=None, marked=False, sem=None))
        self._emit()
        for k, sc_ in self.ksem.items():
            if sc_[1] < self.DEP:
                self.free_sems[sc_[2]].append(sc_)
        self.ksem = {}
        self.ops = []
        self.lastw = {}
        self.readers = {}
        self.lastkey = {}

    def _emit(self):
        ops = self.ops
        for op in ops:
            best = {}
            keep = []
            for d in op['deps']:
                p = ops[d]
                if p['fn'] is None:
                    continue
                if p['key'] is not None:
                    keep.append(d)
                else:
                    if p['eng'] == 'pe' and op['eng'] == 'pe' and op['key'] is None:
                        continue
                    if p['eng'] not in best or best[p['eng']] < d:
                        best[p['eng']] = d
            keep += list(best.values())
            op['deps'] = keep
            for d in keep:
                ops[d]['marked'] = True
        for op in ops:
            e = self.eng[op['eng']]
            for d in sorted(op['deps']):
                sem, val = ops[d]['sem']
                k = (op['eng'], id(sem))
                if self.waited.get(k, 0) >= val:
                    continue
                e.wait_ge(sem, val)
                self.waited[k] = val
                self.n_inst += 1
            if op['fn'] is None:
                continue
            inst = op['fn']()
            self.n_inst += 1
            if op['key'] is not None:
                sc_ = self.ksem.get(op['key'])
                if sc_ is None or sc_[1] >= self.DEP:
                    cls = 'sw' if op['eng'] == 'pool' else 'hw'
                    sc_ = self.free_sems[cls].pop() if self.free_sems[cls] else [self._sem("d"), 0, cls]
                    self.ksem[op['key']] = sc_
                sc_[1] += 1
                inst.then_inc(sc_[0], 16)
                op['sem'] = (sc_[0], 16 * sc_[1])
            elif op['marked']:
                n = self.cnt[op['eng']]
                self.cnt[op['eng']] = n + 1
                sk = (op['eng'], n // self.EP)
                if sk not in self.sems:
                    self.sems[sk] = self._sem(op['eng'])
                sem = self.sems[sk]
                inst.then_inc(sem, 1)
                op['sem'] = (sem, n % self.EP + 1)


def _r3(x):
    return x


def build_nc(Lp, Ls, dbg=False):
    nc = bass.Bass("TRN2", target_bir_lowering=False)
    Lmax = max(Lp, Ls)
    LpO = Lp // 2

    def din(name, shape, dt=F32):
        return nc.dram_tensor(name, list(shape), dt, kind="ExternalInput")

    xp = din("xp", [Lp, D]); xs = din("xs", [Ls, D])
    w_in = din("w_in", [D, 3072]); b_in_t = din("b_in_t", [128, 24])
    acw = {'P': din("acw_p", [128, 4, 3]), 'S': din("acw_s", [128, 4, 3])}
    hcw = {'P': din("hcw_p", [128, 12, 3]), 'S': din("hcw_s", [128, 12, 3])}
    hcb = din("hcb", [128, 12])
    w1aug = din("w1aug", [34, 64]); w2aug = din("w2aug", [65, 64]); freq = din("freq", [64, 1])
    w3pos = {'P': din("w3pos_p", [64, 512]), 'S': din("w3pos_s", [64, 512])}
    w3neg = {'P': din("w3neg_p", [64, 512]), 'S': din("w3neg_s", [64, 512])}
    w3zero = din("w3zero", [64, 512])
    decpos = {'P': din("decpos_p", [1, 512]), 'S': din("decpos_s", [1, 512])}
    decneg = {'P': din("decneg_p", [1, 512]), 'S': din("decneg_s", [1, 512])}
    feat = {'P': din("feat_p", [34, 2 * Lp]), 'S': din("feat_s", [34, 2 * Ls])}
    tpos = {'P': din("tpos_p", [1, 2 * Lp]), 'S': din("tpos_s", [1, 2 * Ls])}
    hyb = din("hyb", [128, 4])
    w_out = din("w_out", [D, D])
    ln1g = din("ln1g", [1, D]); ln1b = din("ln1b", [1, D]); ln2g = din("ln2g", [1, D]); ln2b = din("ln2b", [1, D])
    wq = din("wq", [D, 2048]); keys = din("keys", [2048, 128])
    u_d = din("u", [16384, D]); v_d = din("v", [16384, D])
    ident_d = din("ident", [128, 128]); jmat_d = din("jmat", [128, 128])
    iota128_d = din("iota128", [128, 128]); iota16_d = din("iota16", [128, 16])

    yp = nc.dram_tensor("yp", [LpO, D], F32, kind="ExternalOutput")
    ys = nc.dram_tensor("ys", [Ls, D], F32, kind="ExternalOutput")

    def dscr(name, shape, dt=BF16):
        return nc.dram_tensor(name, list(shape), dt, kind="Internal")

    w_in_s = dscr("w_in_s", [24, 128, 1024])
    wq_s = dscr("wq_s", [16, 128, 1024])
    uT_s = dscr("uT_s", [128, 128, 1024])
    v_s = dscr("v_s", [128, 128, 1024])
    ya_s = dscr("ya_s", [8, 128, Lmax])
    wout_s = dscr("wout_s", [8, 128, 1024])
    keysT_s = dscr("keysT_s", [128, 2048])
    kscr = dscr("kscr", [512, 2 * Lmax])

    es = ExitStack()
    S = Sched(nc, es)

    _nm = [0]

    def sb(stack, name, shape, dt=F32):
        _nm[0] += 1
        return stack.enter_context(nc.sbuf_tensor(f"{name}_{_nm[0]}", list(shape), dt))

    ps = [es.enter_context(nc.psum_tensor(f"ps{i}", [128, 512], F32)) for i in range(8)]
    PS = [f"ps{i}" for i in range(8)]

    ident = sb(es, "ident", [128, 128]); jbf = sb(es, "jbf", [128, 128], BF16)
    iota128 = sb(es, "iota128", [128, 128]); iota16 = sb(es, "iota16", [128, 16])
    iota128b = sb(es, "iota128b", [128, 128], BF16)
    b_in_sb = sb(es, "b_in_sb", [128, 24])
    acw_sb = {r: sb(es, f"acw_{r}", [128, 4, 3]) for r in 'PS'}
    hcw_sb = {r: sb(es, f"hcw_{r}", [128, 12, 3]) for r in 'PS'}
    hcb_sb = sb(es, "hcb_sb", [128, 12]); hyb_sb = sb(es, "hyb_sb", [128, 4])
    w1_sb = sb(es, "w1_sb", [34, 64]); w2_sb = sb(es, "w2_sb", [65, 64]); fq_sb = sb(es, "fq_sb", [64, 1])
    dec_sb = {}
    for r in 'PS':
        dec_sb[('pos', r)] = sb(es, f"decpos_{r}", [1, 512])
        dec_sb[('neg', r)] = sb(es, f"decneg_{r}", [1, 512])
    rnorm = sb(es, "rnorm", [128, 4])
    tmpj = sb(es, "tmpj", [128, 128])

    def dma(out, in_, r, w, key, eng='sp', slow=False):
        if slow:
            S.add(eng, lambda: S.eng[eng].dma_start(out=out, in_=in_, allow_slow_non_contiguous=True), r, w, key)
        else:
            S.add(eng, lambda: S.eng[eng].dma_start(out=out, in_=in_), r, w, key)

    def mm(out, lhsT, rhs, start, stop, r, w):
        S.add('pe', lambda: nc.tensor.matmul(out, lhsT, rhs, start=start, stop=stop), r, w)

    def tr(out, in_, r, w):
        S.add('pe', lambda: nc.tensor.transpose(out, in_, ident[:]), r + ['ident'], w)

    def act(out, in_, func, r, w, **kw):
        S.add('act', lambda: nc.scalar.activation(out=out, in_=in_, func=func, **kw), r, w)

    def cp(eng, out, in_, r, w):
        if eng == 'act':
            S.add('act', lambda: nc.scalar.copy(out=out, in_=in_), r, w)
        else:
            S.add(eng, lambda: S.eng[eng].tensor_copy(out=out, in_=in_), r, w)

    def tt(eng, out, in0, in1, op, r, w):
        S.add(eng, lambda: S.eng[eng].tensor_tensor(out=out, in0=in0, in1=in1, op=op), r, w)

    def ts(eng, out, in0, s1, s2, op0, op1, r, w):
        if op1 is None:
            S.add(eng, lambda: S.eng[eng].tensor_scalar(out=out, in0=in0, scalar1=s1, scalar2=None, op0=op0), r, w)
        else:
            S.add(eng, lambda: S.eng[eng].tensor_scalar(out=out, in0=in0, scalar1=s1, scalar2=s2, op0=op0, op1=op1), r, w)

    def stt(out, in0, scalar, in1, op0, op1, r, w):
        S.add('dve', lambda: nc.vector.scalar_tensor_tensor(out=out, in0=in0, scalar=scalar, in1=in1, op0=op0, op1=op1), r, w)

    rr = [0]

    def rot(engs):
        rr[0] += 1
        return engs[rr[0] % len(engs)]

    def ld(t, src, name, key="c0"):
        dma(t[:], src, [], [name], key)

    ld(ident, ident_d.ap(), 'ident'); ld(tmpj, jmat_d.ap(), 'tmpj', "c1")
    ld(iota128, iota128_d.ap(), 'iota128', "c2"); ld(iota16, iota16_d.ap(), 'iota16', "c3")
    ld(b_in_sb, b_in_t.ap(), 'b_in_sb')
    for r in 'PS':
        ld(acw_sb[r], acw[r].ap(), f'acw_{r}', "c1"); ld(hcw_sb[r], hcw[r].ap(), f'hcw_{r}', "c2")
        ld(dec_sb[('pos', r)], decpos[r].ap(), f'decpos_{r}', "c1"); ld(dec_sb[('neg', r)], decneg[r].ap(), f'decneg_{r}', "c2")
    ld(hcb_sb, hcb.ap(), 'hcb_sb', "c3"); ld(hyb_sb, hyb.ap(), 'hyb_sb')
    ld(w1_sb, w1aug.ap(), 'w1_sb', "c1"); ld(w2_sb, w2aug.ap(), 'w2_sb', "c2"); ld(fq_sb, freq.ap(), 'fq_sb', "c3")
    cp('dve', jbf[:], tmpj[:], ['tmpj'], ['jbf'])
    cp('dve', iota128b[:], iota128[:], ['iota128'], ['iota128b'])
    for r in 'PS':
        for s in ('pos', 'neg'):
            t = dec_sb[(s, r)]
            nm = f'dec{s}_{r}'
            act(t[:], t[:], AF.Abs, [nm], [nm])
    ts('dve', fq_sb[:], fq_sb[:], 1.0 / (2.0 * math.pi), None, ALU.mult, None, ['fq_sb'], ['fq_sb'])

    with ExitStack() as es2:
        stg = [sb(es2, f"stg{i}", [128, 3072]) for i in range(2)]
        stgb = [sb(es2, f"stgb{i}", [128, 3072], BF16) for i in range(2)]
        ust = [sb(es2, f"ust{i}", [128, D]) for i in range(3)]
        utb = [sb(es2, f"utb{i}", [128, 8, 128], BF16) for i in range(2)]
        vst = [sb(es2, f"vst{i}", [128, D]) for i in range(3)]
        vb = [sb(es2, f"vb{i}", [128, D], BF16) for i in range(2)]
        n = 0
        for (src, dst, ncol, nm) in ((w_in, w_in_s, 3072, 'w_in_s'), (wq, wq_s, 2048, 'wq_s')):
            for dk in range(8):
                sl = n % 2
                n += 1
                dma(stg[sl][:, 0:ncol], src.ap()[dk * 128:(dk + 1) * 128, :], [], [f'stg{sl}'], f"stg{sl}")
                cp(rot(['act', 'dve']), stgb[sl][:, 0:ncol], stg[sl][:, 0:ncol], [f'stg{sl}'], [f'stgb{sl}'])
                dma(dst.ap()[:, :, dk * 128:(dk + 1) * 128].rearrange("m p c -> p m c"),
                    stgb[sl][:, 0:ncol].rearrange("p (m c) -> p m c", c=128),
                    [f'stgb{sl}'], [nm], f"stgo{sl}", slow=True)
        for ck in range(8):
            sl = n % 2
            n += 1
            dma(stg[sl][:, 0:D], w_out.ap()[ck * 128:(ck + 1) * 128, :], [], [f'stg{sl}'], f"stg{sl}")
            cp(rot(['act', 'dve']), stgb[sl][:, 0:D], stg[sl][:, 0:D], [f'stg{sl}'], [f'stgb{sl}'])
            dma(wout_s.ap()[ck], stgb[sl][:, 0:D], [f'stgb{sl}'], ['wout_s'], f"stgo{sl}")
        for cg in range(4):
            sl = n % 2
            n += 1
            dma(stg[sl][:, 0:512].rearrange("p (c k) -> p c k", k=128),
                keys.ap()[cg * 512:(cg + 1) * 512, :].rearrange("(c p) k -> p c k", p=128), [], [f'stg{sl}'], f"stg{sl}")
            for c4 in range(4):
                tr(ps[cg % 2][:, c4 * 128:(c4 + 1) * 128], stg[sl][:, c4 * 128:(c4 + 1) * 128], [f'stg{sl}'], [PS[cg % 2]])
            cp('dve', stgb[sl][:, 0:512], ps[cg % 2][:, :], [PS[cg % 2]], [f'stgb{sl}'])
            dma(keysT_s.ap()[:, cg * 512:(cg + 1) * 512], stgb[sl][:, 0:512], [f'stgb{sl}'], ['keysT_s'], f"stgo{sl}")
        for i in range(128):
            s3 = i % 3
            s2 = i % 2
            dma(ust[s3][:], u_d.ap()[i * 128:(i + 1) * 128, :], [], [f'ust{s3}'], f"ust{s3}")
            for b in range(2):
                bank = 2 + 2 * s2 + b
                for c4 in range(4):
                    dk = b * 4 + c4
                    tr(ps[bank][:, c4 * 128:(c4 + 1) * 128], ust[s3][:, dk * 128:(dk + 1) * 128], [f'ust{s3}'], [PS[bank]])
                cp('act' if b == 0 else 'dve', utb[s2][:, b * 4:(b + 1) * 4, :],
                   ps[bank][:, :].rearrange("p (c k) -> p c k", k=128), [PS[bank]], [f'utb{s2}_{b}'])
            dma(uT_s.ap()[i], utb[s2][:].rearrange("p a b -> p (a b)"), [f'utb{s2}_0', f'utb{s2}_1'], ['uT_s'], f"uto{s2}")
            dma(vst[s3][:], v_d.ap()[i * 128:(i + 1) * 128, :], [], [f'vst{s3}'], f"vst{s3}", eng='pool')
            cp('pool', vb[s2][:], vst[s3][:], [f'vst{s3}'], [f'vb{s2}'])
            dma(v_s.ap()[i], vb[s2][:], [f'vb{s2}'], ['v_s'], f"vo{s2}", eng='pool')
        S.end_phase()

    def run(R, xin, yout, L, Lown):
        nS = L // 128
        nT = Lown // 128
        ntile = L // TA
        ntile_own = Lown // TA
        if True:
            with ExitStack() as esa:
                w3_sb = {}
                w3_sb[('pos', R)] = sb(esa, f"w3pos_{R}", [64, 512])
                w3_sb[('neg', R)] = sb(esa, f"w3neg_{R}", [64, 512])
                w3z_sb = sb(esa, "w3z_sb", [64, 512])
                ld(w3_sb[('pos', R)], w3pos[R].ap(), f'w3pos_{R}', "c3"); ld(w3_sb[('neg', R)], w3neg[R].ap(), f'w3neg_{R}', "c0")
                ld(w3z_sb, w3zero.ap(), 'w3z_sb')
                ft = [sb(esa, f"ft{i}", [34, 512]) for i in range(2)]
                tp = [sb(esa, f"tp{i}", [1, 512]) for i in range(2)]
                va = [sb(esa, f"va{i}", [64, 512]) for i in range(2)]
                vi = [sb(esa, f"vi{i}", [64, 512], I32) for i in range(2)]
                h1 = [sb(esa, f"h1{i}", [65, 512]) for i in range(2)]
                h2 = [sb(esa, f"h2{i}", [64, 512]) for i in range(2)]
                ew = [sb(esa, f"ew{i}", [128, 512]) for i in range(2)]
                kf = [sb(esa, f"kf{i}", [128, 512]) for i in range(2)]
                kb = [sb(esa, f"kb{i}", [128, 512], BF16) for i in range(2)]
                nacc = sb(esa, "nacc", [128, 4, 2 * L // 512])
                k0 = sb(esa, "k0", [128, 4]); k0b = sb(esa, "k0b", [128, 4], BF16)
                nrm = sb(esa, "nrm", [128, 4])
                for i in range(2):
                    S.add('pool', (lambda i=i: nc.gpsimd.memset(h1[i][64:65, :], 1.0)), [], [f'h1one{i}'])
                S.add('pool', lambda: nc.gpsimd.memset(nacc[:], 0.0), [], ['nacc'])
                nft = 2 * L // 512

                def sin_layer(sl, psb, lhs, lhs_nm, rhs, rhs_nm, out, out_nm):
                    va_, vi_ = va[sl], vi[sl]
                    vn, vin = f'va{sl}', f'vi{sl}'
                    mm(ps[psb][0:64, :], lhs, rhs, True, True, [lhs_nm] + rhs_nm, [PS[psb]])
                    ts('dve', va_[:], ps[psb][0:64, :], fq_sb[:, 0:1], 8.5, ALU.mult, ALU.add, [PS[psb], 'fq_sb'], [vn])
                    cp('dve', vi_[:], va_[:], [vn], [vin])
                    stt(va_[:], va_[:], -0.5, vi_[:], ALU.add, ALU.subtract, [vn, vin], [vn])
                    stt(va_[:], va_[:], -0.5, va_[:], ALU.is_lt, ALU.add, [vn], [vn])
                    act(out, va_[:], AF.Sin, [vn], out_nm, scale=TWO_PI)

                def L1(jt):
                    sl = jt % 2
                    j0 = jt * 512
                    dma(ft[sl][:], feat[R].ap()[:, j0:j0 + 512], [], [f'ft{sl}'], f"ft{sl}")
                    dma(tp[sl][:], tpos[R].ap()[:, j0:j0 + 512], [], [f'tp{sl}'], f"tp{sl}")
                    sin_layer(sl, 0 if sl == 0 else 6, w1_sb[:], 'w1_sb', ft[sl][:], [f'ft{sl}'], h1[sl][0:64, :], [f'h1{sl}'])

                def L2(jt):
                    sl = jt % 2
                    sin_layer(sl, 1 if sl == 0 else 7, w2_sb[:], 'w2_sb', h1[sl][:], [f'h1{sl}', f'h1one{sl}'], h2[sl][:], [f'h2{sl}'])

                def L3(jt, qs):
                    sl = jt % 2
                    j0 = jt * 512
                    dirn = 'pos' if j0 < L else 'neg'
                    H2 = h2[sl]
                    hn = f'h2{sl}'
                    for q in qs:
                        e = (jt * 4 + q) % 2
                        pa = 2 + 2 * e
                        pb_ = 3 + 2 * e
                        mm(ps[pa][:, :], w3_sb[(dirn, R)][:, q * 128:(q + 1) * 128], H2[:], True, True,
                           [f'w3{dirn}_{R}', hn], [PS[pa]])
                        mm(ps[pb_][:, :], dec_sb[(dirn, R)][:, q * 128:(q + 1) * 128], tp[sl][:], True, True,
                           [f'dec{dirn}_{R}', f'tp{sl}'], [PS[pb_]])
                        act(ew[e][:], ps[pb_][:, :], AF.Exp, [PS[pb_]], [f'ew{e}'], scale=-1.0)
                        tt('dve', kf[e][:], ps[pa][:, :], ew[e][:], ALU.mult, [PS[pa], f'ew{e}'], [f'kf{e}'])
                        if jt == 0:
                            S.add('dve', (lambda e=e: nc.vector.memset(kf[e][:, 0:1], 0.0)), [f'kf{e}'], [f'kf{e}'])
                        if j0 == L:
                            mm(ps[pb_][:, 0:1], w3z_sb[:, q * 128:(q + 1) * 128], H2[:, 0:1], True, True,
                               ['w3z_sb', hn], [PS[pb_]])
                            cp('dve', kf[e][:, 0:1], ps[pb_][:, 0:1], [PS[pb_], f'kf{e}'], [f'kf{e}'])
                            cp('dve', k0[:, q:q + 1], ps[pb_][:, 0:1], [PS[pb_]], ['k0'])
                        S.add('dve', (lambda e=e, q=q, jt=jt: nc.vector.tensor_reduce(
                            out=nacc[:, q, jt:jt + 1], in_=kf[e][:], axis=AX.X, op=ALU.add, apply_absolute_value=True)),
                            [f'kf{e}'], ['nacc'])
                        cp('pool', kb[e][:], kf[e][:], [f'kf{e}'], [f'kb{e}'])
                        dma(kscr.ap()[q * 128:(q + 1) * 128, j0:j0 + 512], kb[e][:], [f'kb{e}'], [f'kscr_{q}_{jt}'], f"kbo{e}")

                L1(0); L2(0)
                for jt in range(nft):
                    if jt + 1 < nft:
                        L1(jt + 1)
                    L3(jt, [0, 1])
                    if jt + 1 < nft:
                        L2(jt + 1)
                    L3(jt, [2, 3])
                S.add('dve', lambda: nc.vector.tensor_reduce(out=nrm[:], in_=nacc[:], axis=AX.X, op=ALU.add), ['nacc'], ['nrm'])
                S.add('dve', lambda: nc.vector.reciprocal(out=rnorm[:], in_=nrm[:]), ['nrm'], ['rnorm'])
                tt('dve', nrm[:], nrm[:], hyb_sb[:], ALU.mult, ['nrm', 'hyb_sb'], ['nrm'])
                tt('dve', k0[:], k0[:], nrm[:], ALU.add, ['k0', 'nrm'], ['k0'])
                cp('dve', k0b[:], k0[:], ['k0'], ['k0b'])
                for q in range(4):
                    dma(kscr.ap()[q * 128:(q + 1) * 128, L:L + 1], k0b[:, q:q + 1], ['k0b', f'kscr_{q}_{L // 512}'],
                        [f'kscr_{q}_{L // 512}'], "k0o", slow=True)
                S.end_phase()
        with ExitStack() as esr:
            ZT = sb(esr, f"ZT{R}", [128, nS, 512], BF16)
            x0T = sb(esr, f"x0T{R}", [128, 4, Lown], BF16)
            with ExitStack() as esa:

                xst = [sb(esa, f"xst{i}", [128, 4, D]) for i in range(2)]
                xT = [sb(esa, f"xT{i}", [128, 8, TA], BF16) for i in range(2)]
                wb = [sb(esa, f"wb{i}", [128, 8, 128], BF16) for i in range(4)]
                pbf = [sb(esa, f"pbf{i}", [128, 3, TA + 2]) for i in range(2)]
                carry = sb(esa, "carry", [128, 8, 3, 2])
                tmm = sb(esa, "tmm", [128, TA + 2]); acc = [sb(esa, f"acc{i}", [128, TA]) for i in range(3)]
                yast = [sb(esa, f"yast{i}", [128, TA], BF16) for i in range(2)]
                zst = [sb(esa, f"zst{i}", [128, TA]) for i in range(2)]
                x0c = sb(esa, "x0c", [128, 8, 1]); x0cb = sb(esa, "x0cb", [128, 8, 1], BF16)
                wcnt = [0]

                def load_w(m):
                    s = wcnt[0] % 4
                    wcnt[0] += 1
                    dma(wb[s][:].rearrange("p a b -> p (a b)"), w_in_s.ap()[m], ['w_in_s'], [f'wb{s}'], f"wb{s}")
                    return s

                S.add('pool', lambda: nc.gpsimd.memset(carry[:], 0.0), [], ['carry'])
                dma(x0c[:], AP(xin, 0, [[1, 128], [128, 8], [1, 1]]), [], ['x0c'], "x0c", slow=True)
                cp('dve', x0cb[:], x0c[:], ['x0c'], ['x0cb'])
                for m in range(24):
                    s = load_w(m)
                    for dk in range(8):
                        mm(ps[7][:, m:m + 1], wb[s][:, dk, :], x0cb[:, dk, :], dk == 0, dk == 7, [f'wb{s}', 'x0cb'], [PS[7]])
                for half in range(2):
                    tt('dve', carry[:, half * 4:(half + 1) * 4, :, 1:2].rearrange("p g j o -> p j g o"),
                       ps[7][:, half * 12:(half + 1) * 12].rearrange("p (j g o) -> p j g o", j=3, o=1),
                       b_in_sb[:, half * 12:(half + 1) * 12].rearrange("p (j g o) -> p j g o", j=3, o=1),
                       ALU.add, [PS[7], 'b_in_sb', 'carry'], ['carry'])

                for it in range(ntile):
                    own = it < ntile_own
                    t0 = it * TA
                    xs_ = it % 2
                    xt_ = it % 2
                    nrow = min(TA, L - 1 - t0)
                    last = (it == ntile - 1)
                    if last:
                        S.add('pool', (lambda xs_=xs_: nc.gpsimd.memset(xst[xs_][:, 3, :], 0.0)), [f'xst{xs_}'], [f'xst{xs_}'])
                        dma(xst[xs_][:, 0:3, :], xin.ap()[t0 + 1:t0 + 385, :].rearrange("(s p) d -> p s d", p=128),
                            [f'xst{xs_}'], [f'xst{xs_}'], f"xst{xs_}")
                        dma(xst[xs_][0:127, 3, :], xin.ap()[t0 + 385:t0 + 512, :], [f'xst{xs_}'], [f'xst{xs_}'], f"xst{xs_}")
                    else:
                        dma(xst[xs_][:], xin.ap()[t0 + 1:t0 + 513, :].rearrange("(s p) d -> p s d", p=128),
                            [], [f'xst{xs_}'], f"xst{xs_}")
                    for dk in range(8):
                        bank = dk % 2
                        for s4 in range(4):
                            tr(ps[bank][:, s4 * 128:(s4 + 1) * 128], xst[xs_][:, s4, dk * 128:(dk + 1) * 128], [f'xst{xs_}'], [PS[bank]])
                        cp('act' if dk % 2 == 0 else 'dve', xT[xt_][:, dk, :], ps[bank][:, :], [PS[bank]], [f'xT{xt_}'])
                    groups = list(range(8)) if own else list(range(4, 8))
                    for g in groups:
                        q = g % 4
                        isA = g < 4
                        ms = [j * 4 + q for j in range(3)] if isA else [12 + j * 4 + q for j in range(3)]
                        js = [0, 1, 2] if (isA or own) else [1, 2]
                        pbs = g % 2
                        P_ = pbf[pbs]
                        pn = f'pbf{pbs}'
                        cp('pool', P_[:, :, 0:2], carry[:, g, :, :], ['carry', pn], [pn])
                        for j in js:
                            m = ms[j]
                            s = load_w(m)
                            bank = (2 if g % 2 == 0 else 5) + (j % 3)
                            for dk in range(8):
                                mm(ps[bank][:, :], wb[s][:, dk, :], xT[xt_][:, dk, :], dk == 0, dk == 7, [f'wb{s}', f'xT{xt_}'], [PS[bank]])
                            act(P_[:, j, 2:TA + 2], ps[bank][:, :], AF.Identity, [PS[bank], 'b_in_sb', pn], [pn],
                                bias=b_in_sb[:, m:m + 1], scale=1.0)
                        if last:
                            S.add('pool', (lambda P_=P_: nc.gpsimd.memset(P_[:, :, TA + 1:TA + 2], 0.0)), [pn], [pn])
                        cp('pool', carry[:, g, :, :], P_[:, :, TA:TA + 2], [pn, 'carry'], ['carry'])
                        if isA:
                            cw = acw_sb[R]
                            cn = f'acw_{R}'
                            tt('dve', tmm[:], P_[:, 1, :], P_[:, 2, :], ALU.mult, [pn], ['tmm'])
                            ts('dve', acc[0][:], tmm[:, 0:TA], cw[:, q, 0:1], None, ALU.mult, None, ['tmm', cn], ['acc0'])
                            stt(acc[0][:], tmm[:, 1:TA + 1], cw[:, q, 1:2], acc[0][:], ALU.mult, ALU.add, ['tmm', cn, 'acc0'], ['acc0'])
                            stt(acc[0][:], tmm[:, 2:TA + 2], cw[:, q, 2:3], acc[0][:], ALU.mult, ALU.add, ['tmm', cn, 'acc0'], ['acc0'])
                            ys_ = q % 2
                            tt('dve', yast[ys_][:], P_[:, 0, 1:TA + 1], acc[0][:], ALU.mult, [pn, 'acc0'], [f'yast{ys_}'])
                            dma(ya_s.ap()[q, :, t0:t0 + TA], yast[ys_][:], [f'yast{ys_}'], ['ya_s'], f"yao{ys_}")
                        else:
                            cw = hcw_sb[R]
                            cn = f'hcw_{R}'
                            for j in js:
                                c = j * 4 + q
                                a = acc[j]
                                an = f'acc{j}'
                                ts('dve', a[:], P_[:, j, 0:TA], cw[:, c, 0:1], hcb_sb[:, c:c + 1], ALU.mult, ALU.add, [pn, cn, 'hcb_sb'], [an])
                                stt(a[:], P_[:, j, 1:TA + 1], cw[:, c, 1:2], a[:], ALU.mult, ALU.add, [pn, cn, an], [an])
                                if j == 0:
                                    stt(x0T[:, q, t0:t0 + TA], P_[:, j, 2:TA + 2], cw[:, c, 2:3], a[:], ALU.mult, ALU.add,
                                        [pn, cn, an], [f'x0T_{q}'])
                                else:
                                    stt(a[:], P_[:, j, 2:TA + 2], cw[:, c, 2:3], a[:], ALU.mult, ALU.add, [pn, cn, an], [an])
                            zs = q % 2
                            tt('dve', zst[zs][:], acc[1][:], acc[2][:], ALU.mult, ['acc1', 'acc2'], [f'zst{zs}'])
                            bank = q % 2
                            for s4 in range(4):
                                tr(ps[bank][:, s4 * 128:(s4 + 1) * 128], zst[zs][:, s4 * 128:(s4 + 1) * 128], [f'zst{zs}'], [PS[bank]])
                            cp('act', ZT[:, it * 4:(it + 1) * 4, q * 128:(q + 1) * 128],
                               ps[bank][:, :].rearrange("p (s c) -> p s c", c=128), [PS[bank]], ['ZT'])
                S.end_phase()

            with ExitStack() as esc:
                base = L + 1 - 128 * nT
                W = 128 * (nS + nT - 1)
                Wh = (W // 2 + 127) // 128 * 128 + 128
                kw = [[sb(esc, f"kw{b}_{h}", [128, Wh], BF16) for h in range(2)] for b in range(2)]
                yconv = sb(esc, "yconv", [128, 128, nT], BF16)
                yhst = [sb(esc, f"yhst{i}", [128, 512], BF16) for i in range(2)]

                def _half(dd):
                    off = L - 127 - 128 * dd - base
                    return 0 if off + 128 <= Wh else 1
                assert _half(0) == 0
                Ds = [0] + [d for d in range(-(nS - 1), nT) if d != 0 and _half(d) == 0] + \
                    [d for d in range(-(nS - 1), nT) if d != 0 and _half(d) == 1]
                nper = 512 // nT
                for q in range(4):
                    for cc in range(128):
                        cg = q * 128 + cc
                        b = cc % 2
                        for h in range(2):
                            m0 = 0 if h == 0 else W - Wh
                            dma(kw[b][h][:], AP(kscr, cg * 2 * Lmax + base + m0, [[1, 128], [1, Wh]]),
                                ['kscr'], [f'kw{b}_{h}'], f"kw{b}_{h}", eng=('sp' if h == 0 else 'pool'))
                        grp = cc // nper
                        bank = grp % 2
                        col0 = (cc % nper) * nT
                        for di, dd in enumerate(Ds):
                            off = L - 127 - 128 * dd - base
                            if off + 128 <= Wh:
                                h, o = 0, off
                            else:
                                h, o = 1, off - (W - Wh)
                            assert 0 <= o and o + 128 <= Wh
                            T0 = max(0, dd)
                            T1 = min(nT, nS + dd)
                            mm(ps[bank][:, col0 + T0:col0 + T1], kw[b][h][:, o:o + 128], ZT[:, T0 - dd:T1 - dd, cg],
                               di == 0, di == len(Ds) - 1, [f'kw{b}_{h}', 'ZT'], [PS[bank]])
                        if cc % nper == nper - 1:
                            c0 = grp * nper
                            cp('act' if grp % 2 == 0 else 'dve', yconv[:, c0:c0 + nper, :],
                               ps[bank][:, :].rearrange("p (c t) -> p c t", t=nT), [PS[bank]], ['yconv'])
                    for T4 in range(nT // 4):
                        bank = 2 + T4 % 2
                        for s4 in range(4):
                            T = T4 * 4 + s4
                            mm(ps[bank][:, s4 * 128:(s4 + 1) * 128], yconv[:, :, T], jbf[:], True, True, ['yconv', 'jbf'], [PS[bank]])
                        ys_ = T4 % 2
                        stt(yhst[ys_][:], ps[bank][:, :], rnorm[:, q:q + 1], x0T[:, q, T4 * 512:(T4 + 1) * 512],
                            ALU.mult, ALU.mult, [PS[bank], 'rnorm', f'x0T_{q}'], [f'yhst{ys_}'])
                        dma(ya_s.ap()[4 + q, :, T4 * 512:(T4 + 1) * 512], yhst[ys_][:], [f'yhst{ys_}'], ['ya_s'], f"yho{ys_}")
                S.end_phase()

            esr.close()
            with ExitStack() as ese:
                g1 = sb(ese, "g1", [128, D]); b1 = sb(ese, "b1", [128, D]); g2 = sb(ese, "g2", [128, D]); b2 = sb(ese, "b2", [128, D])
                for t, src, name in ((g1, ln1g, 'g1'), (b1, ln1b, 'b1'), (g2, ln2g, 'g2'), (b2, ln2b, 'b2')):
                    dma(t[:], AP(src, 0, [[0, 128], [1, D]]), [], [name], "c1")
                keysT = sb(ese, "keysT", [128, 16, 128], BF16)
                dma(keysT[:].rearrange("p a b -> p (a b)"), keysT_s.ap(), [], ['keysT'], "c2")
                wob = [sb(ese, f"wob{i}", [128, D], BF16) for i in range(3)]
                yat = [sb(ese, f"yat{i}", [128, 8, TT], BF16) for i in range(1)]
                xres = [sb(ese, f"xres{i}", [128, D]) for i in range(1)]
                hh1 = sb(ese, "hh1", [128, 2, D]); x1 = [sb(ese, f"x1_{i}", [128, 2, D]) for i in range(2)]
                hh2 = hh1
                x1T = [sb(ese, f"x1T{i}", [128, 8, TT], BF16) for i in range(2)]
                st6 = sb(ese, "st6", [128, 2, 6]); mv = sb(ese, "mv", [128, 2]); rstd = sb(ese, "rstd", [128, 1])
                wqb = [sb(ese, f"wqb{i}", [128, 8, 128], BF16) for i in range(3)]
                qT = sb(ese, "qT", [128, 16, TT], BF16)
                scr8 = sb(ese, "scr8", [128, 2048]); scw = sb(ese, "scw", [128, 128])
                sc = scr8
                top = sb(ese, "top", [128, 16, 16]); tidx = sb(ese, "tidx", [128, 16, 16], U32); tidf = sb(ese, "tidf", [128, 16, 16])
                cand = scr8[:].rearrange("p (h x) -> p h x", x=256); candw = sb(ese, "candw", [128, 256])
                best = sb(ese, "best", [128, 8, 16]); pos = sb(ese, "pos", [128, 8, 16], U32)
                pa_u = sb(ese, "pa_u", [128, 8, 16], U32); pb_u = sb(ese, "pb_u", [128, 8, 16], U32)
                pa_f = sb(ese, "pa_f", [128, 8, 16]); pb_f = sb(ese, "pb_f", [128, 8, 16])
                eq = scr8[:].rearrange("p (h a b) -> p h a b", a=16, b=16)
                lst = sb(ese, "lst", [128, 3, 128])
                nmax = sb(ese, "nmax", [128, 8]); ssum = sb(ese, "ssum", [128, 8]); rsum = sb(ese, "rsum", [128, 8])
                lstT = sb(ese, "lstT", [128, 3, TT], BF16)
                TB = 8
                Pm = [sb(ese, f"Pm{i}", [128, TB, 128], BF16) for i in range(2)]
                Qm = [sb(ese, f"Qm{i}", [128, TB, 128], BF16) for i in range(2)]
                Wt = sb(ese, "Wt", [128, 128, TT], BF16)
                ub = [sb(ese, f"ub{i}", [128, 8, 128], BF16) for i in range(6)]
                vbb = [sb(ese, f"vbb{i}", [128, D], BF16) for i in range(4)]
                G = [sb(ese, f"G{i}", [128, TT], BF16) for i in range(4)]
                cf = [sb(ese, f"cf{i}", [128, TT], BF16) for i in range(4)]
                wqc = [0]
                woc = [0]

                def layer_norm(src, sn, dst, dn, g, gn, b, bn):
                    for hf in range(2):
                        S.add('dve', (lambda hf=hf: nc.vector.bn_stats(out=st6[:, hf, :], in_=src[:, hf * 512:(hf + 1) * 512])), [sn], ['st6'])
                    S.add('dve', lambda: nc.vector.bn_aggr(out=mv[:], in_=st6[:].rearrange("p a b -> p (a b)")), ['st6'], ['mv'])
                    act(rstd[:], mv[:, 1:2], AF.Sqrt, ['mv'], ['rstd'], bias=float(LN_EPS), scale=1.0)
                    S.add('dve', lambda: nc.vector.reciprocal(out=rstd[:], in_=rstd[:]), ['rstd'], ['rstd'])
                    ts('dve', dst, src, mv[:, 0:1], rstd[:, 0:1], ALU.subtract, ALU.mult, [sn, 'mv', 'rstd'], [dn])
                    tt('pool', dst, dst, g[:], ALU.mult, [dn, gn], [dn])
                    tt('pool', dst, dst, b[:], ALU.add, [dn, bn], [dn])

                ntl = Lown // TT
                V = nc.vector

                def prep_stages(it):
                    t0 = it * TT
                    xb = it % 2
                    X1 = x1[xb]
                    X1T = x1T[xb]
                    xtn = f'x1T{xb}'
                    st = []

                    def s_load():
                        dma(yat[0][:], ya_s.ap()[:, :, t0:t0 + TT].rearrange("q p t -> p q t"), ['ya_s'], ['yat0'], "yat0", eng='act')
                    st.append(s_load)
                    for hh in range(2):
                        def s_d(hh=hh):
                            dma(xres[0][:], xin.ap()[t0 + hh * 128:t0 + (hh + 1) * 128, :], [], ['xres0'], "xres0", eng='act')
                            for ck in range(8):
                                ws = woc[0] % 3
                                woc[0] += 1
                                dma(wob[ws][:], wout_s.ap()[ck], [], [f'wob{ws}'], f"wob{ws}", eng='act')
                                for dh in range(2):
                                    mm(ps[4 + dh][:, :], yat[0][:, ck, hh * 128:(hh + 1) * 128], wob[ws][:, dh * 512:(dh + 1) * 512],
                                       ck == 0, ck == 7, ['yat0', f'wob{ws}'], [PS[4 + dh]])
                            for dh in range(2):
                                stt(hh1[:, hh, dh * 512:(dh + 1) * 512], xres[0][:, dh * 512:(dh + 1) * 512], float(ALPHA), ps[4 + dh][:, :],
                                    ALU.mult, ALU.add, ['xres0', PS[4 + dh]], [f'hh1_{hh}'])
                        st.append(s_d)

                        def s_ln(hh=hh):
                            layer_norm(hh1[:, hh, :], f'hh1_{hh}', X1[:, hh, :], f'x1_{xb}_{hh}', g1, 'g1', b1, 'b1')
                        st.append(s_ln)

                        def s_tr(hh=hh):
                            for b4 in range(2):
                                bank = 4 + b4
                                for c4 in range(4):
                                    dk = b4 * 4 + c4
                                    tr(ps[bank][:, c4 * 128:(c4 + 1) * 128], X1[:, hh, dk * 128:(dk + 1) * 128], [f'x1_{xb}_{hh}'], [PS[bank]])
                                cp('act', X1T[:, b4 * 4:(b4 + 1) * 4, hh * 128:(hh + 1) * 128],
                                   ps[bank][:, :].rearrange("p (c t) -> p c t", t=128), [PS[bank]], [xtn])
                        st.append(s_tr)
                    for c4g in range(4):
                        def s_q(c4g=c4g):
                            for c4 in range(4):
                                cc = c4g * 4 + c4
                                s_ = wqc[0] % 3
                                wqc[0] += 1
                                dma(wqb[s_][:].rearrange("p a b -> p (a b)"), wq_s.ap()[cc], [], [f'wqb{s_}'], f"wqb{s_}", eng='act')
                                bank = 4 + cc % 2
                                for dk in range(8):
                                    mm(ps[bank][:, 0:TT], wqb[s_][:, dk, :], X1T[:, dk, :], dk == 0, dk == 7, [f'wqb{s_}', xtn], [PS[bank]])
                                cp('act' if cc % 2 == 0 else 'dve', qT[:, cc, :], ps[bank][:, 0:TT], [PS[bank]], ['qT'])
                        st.append(s_q)
                    for hh in range(2):
                        def s_sc(hh=hh):
                            for c4g in range(4):
                                bank = 4 + c4g % 2
                                for c4 in range(4):
                                    cc = c4g * 4 + c4
                                    mm(ps[bank][:, c4 * 128:(c4 + 1) * 128], qT[:, cc, hh * 128:(hh + 1) * 128], keysT[:, cc, :], True, True,
                                       ['qT', 'keysT'], [PS[bank]])
                                cp('act' if c4g % 2 == 0 else 'dve', sc[:, c4g * 512:(c4g + 1) * 512], ps[bank][:, :], [PS[bank]], ['scr8'])
                        st.append(s_sc)
                        for gh in range(2):
                            def s_tk(gh=gh):
                                for g in range(gh * 8, gh * 8 + 8):
                                    sg = sc[:, g * 128:(g + 1) * 128]
                                    S.add('dve', (lambda g=g, sg=sg: V.max(out=top[:, g, 0:8], in_=sg)), ['scr8'], ['top'])
                                    S.add('dve', (lambda g=g, sg=sg: V.max_index(out=tidx[:, g, 0:8], in_max=top[:, g, 0:8], in_values=sg)), ['scr8', 'top'], ['tidx'])
                                    S.add('dve', (lambda g=g, sg=sg: V.match_replace(out=scw[:], in_to_replace=top[:, g, 0:8], in_values=sg, imm_value=-1e30)),
                                          ['scr8', 'top'], ['scw'])
                                    S.add('dve', (lambda g=g: V.max(out=top[:, g, 8:16], in_=scw[:])), ['scw'], ['top'])
                                    S.add('dve', (lambda g=g: V.max_index(out=tidx[:, g, 8:16], in_max=top[:, g, 8:16], in_values=scw[:])), ['scw', 'top'], ['tidx'])
                            st.append(s_tk)

                        def s_cand():
                            cp('dve', tidf[:], tidx[:], ['tidx'], ['tidf'])
                            topv = top[:].rearrange("p (h two) k -> p h two k", two=2)
                            tt('dve', cand[:].rearrange("p h (a b) -> p h a b", b=16),
                               topv[:, :, 0, :].unsqueeze(3).to_broadcast([128, 8, 16, 16]),
                               topv[:, :, 1, :].unsqueeze(2).to_broadcast([128, 8, 16, 16]), ALU.add, ['top'], ['scr8'])
                            for h in range(8):
                                ch = cand[:, h, :]
                                S.add('dve', (lambda h=h, ch=ch: V.max(out=best[:, h, 0:8], in_=ch)), ['scr8'], ['best'])
                                S.add('dve', (lambda h=h, ch=ch: V.max_index(out=pos[:, h, 0:8], in_max=best[:, h, 0:8], in_values=ch)), ['scr8', 'best'], ['pos'])
                                S.add('dve', (lambda h=h, ch=ch: V.match_replace(out=candw[:], in_to_replace=best[:, h, 0:8], in_values=ch, imm_value=-1e30)),
                                      ['scr8', 'best'], ['candw'])
                                S.add('dve', (lambda h=h: V.max(out=best[:, h, 8:16], in_=candw[:])), ['candw'], ['best'])
                                S.add('dve', (lambda h=h: V.max_index(out=pos[:, h, 8:16], in_max=best[:, h, 8:16], in_values=candw[:])), ['candw', 'best'], ['pos'])
                        st.append(s_cand)

                        def s_sm(hh=hh):
                            gv = lst[:, 2, :].rearrange("p (h k) -> p h k", k=16)
                            tt('dve', gv, best[:], best[:, :, 0:1].to_broadcast([128, 8, 16]), ALU.subtract, ['best'], ['lst_g'])
                            act(gv, gv, AF.Exp, ['lst_g'], ['lst_g'])
                            S.add('dve', lambda: V.tensor_reduce(out=ssum[:], in_=gv, axis=AX.X, op=ALU.add), ['lst_g'], ['ssum'])
                            S.add('dve', lambda: V.reciprocal(out=rsum[:], in_=ssum[:]), ['ssum'], ['rsum'])
                            tt('dve', gv, gv, rsum[:].unsqueeze(2).to_broadcast([128, 8, 16]), ALU.mult, ['lst_g', 'rsum'], ['lst_g'])
                            S.add('dve', lambda: V.tensor_single_scalar(out=pa_u[:], in_=pos[:], scalar=4, op=ALU.logical_shift_right), ['pos'], ['pa_u'])
                            S.add('dve', lambda: V.tensor_single_scalar(out=pb_u[:], in_=pos[:], scalar=15, op=ALU.bitwise_and), ['pos'], ['pb_u'])
                            cp('dve', pa_f[:], pa_u[:], ['pa_u'], ['pa_f'])
                            cp('dve', pb_f[:], pb_u[:], ['pb_u'], ['pb_f'])
                            tidv = tidf[:].rearrange("p (h two) k -> p h two k", two=2)
                            for wi, (pf, pn_) in enumerate(((pa_f, 'pa_f'), (pb_f, 'pb_f'))):
                                tt('dve', eq[:], pf[:].unsqueeze(3).to_broadcast([128, 8, 16, 16]),
                                   iota16[:].unsqueeze(1).unsqueeze(1).to_broadcast([128, 8, 16, 16]), ALU.is_equal, [pn_, 'iota16'], ['scr8'])
                                tt('dve', eq[:], eq[:], tidv[:, :, wi, :].unsqueeze(2).to_broadcast([128, 8, 16, 16]), ALU.mult, ['scr8', 'tidf'], ['scr8'])
                                S.add('dve', (lambda wi=wi: V.tensor_reduce(out=lst[:, wi, :].rearrange("p (h k) -> p h k", k=16), in_=eq[:], axis=AX.X, op=ALU.add)),
                                      ['scr8'], [f'lst_{wi}'])
                            for wi in range(3):
                                tr(ps[4][:, wi * 128:(wi + 1) * 128], lst[:, wi, :], ['lst_0', 'lst_1', 'lst_g'], [PS[4]])
                            cp('act', lstT[:, :, hh * 128:(hh + 1) * 128], ps[4][:, 0:384].rearrange("p (w t) -> p w t", t=128), [PS[4]], ['lstT'])
                        st.append(s_sm)
                    return st

                def e4(it):
                    for tb in range(TT // TB):
                        s = tb % 2
                        tsl = slice(tb * TB, (tb + 1) * TB)
                        iob = iota128b[:].unsqueeze(1).to_broadcast([128, TB, 128])
                        tt('dve', Pm[s][:], iob, lstT[:, 0, tsl].unsqueeze(2).to_broadcast([128, TB, 128]), ALU.is_equal,
                           ['iota128b', 'lstT'], [f'Pm{s}'])
                        tt('dve', Qm[s][:], iob, lstT[:, 1, tsl].unsqueeze(2).to_broadcast([128, TB, 128]), ALU.is_equal,
                           ['iota128b', 'lstT'], [f'Qm{s}'])
                        tt('pool', Qm[s][:], Qm[s][:], lstT[:, 2, tsl].unsqueeze(2).to_broadcast([128, TB, 128]), ALU.mult,
                           [f'Qm{s}', 'lstT'], [f'Qm{s}'])
                        for t4 in range(TB // 4):
                            bank = 4 + (tb * (TB // 4) + t4) % 2
                            for k4 in range(4):
                                tl = t4 * 4 + k4
                                mm(ps[bank][:, k4 * 128:(k4 + 1) * 128], Qm[s][:, tl, :], Pm[s][:, tl, :], True, True, [f'Qm{s}', f'Pm{s}'], [PS[bank]])
                            tg = tb * TB + t4 * 4
                            cp('act', Wt[:, :, tg:tg + 4], ps[bank][:, :].rearrange("p (t i) -> p i t", i=128), [PS[bank]], ['Wt'])

                def f_evac(it):
                    X1 = x1[it % 2]
                    for hh in range(2):
                        for dh in range(2):
                            bank = hh * 2 + dh
                            stt(hh2[:, hh, dh * 512:(dh + 1) * 512], X1[:, hh, dh * 512:(dh + 1) * 512], float(ALPHA), ps[bank][:, :],
                                ALU.mult, ALU.add, [f'x1_{it % 2}_{hh}', PS[bank]], [f'hh1_{hh}'])

                def f_stages(it):
                    t0 = it * TT
                    st = []
                    for hh in range(2):
                        def s_f(hh=hh):
                            layer_norm(hh2[:, hh, :], f'hh1_{hh}', hh2[:, hh, :], f'hh1_{hh}', g2, 'g2', b2, 'b2')
                            dma(yout.ap()[t0 + hh * 128:t0 + (hh + 1) * 128, :], hh2[:, hh, :], [f'hh1_{hh}'], [f'yout{R}'], f"osto{hh}")
                        st.append(s_f)
                    return st

                def load_u(i):
                    s = i % 6
                    dma(ub[s][:].rearrange("p a b -> p (a b)"), uT_s.ap()[i], [], [f'ub{s}'], f"ub{s}")

                def load_v(i):
                    s = i % 4
                    dma(vbb[s][:], v_s.ap()[i], [], [f'vbb{s}'], f"vbb{s}", eng='pool')

                def s_mm(it, i):
                    s = i % 6
                    bank = 6 + i % 2
                    c0 = 0
                    pn = PS[bank]
                    X1T = x1T[it % 2]
                    for dk in range(8):
                        mm(ps[bank][:, c0:c0 + TT], ub[s][:, dk, :], X1T[:, dk, :], dk == 0, dk == 7, [f'ub{s}', f'x1T{it % 2}'], [pn])
                    e = i % 4
                    act(G[e][:], ps[bank][:, c0:c0 + TT], AF.Gelu, [pn], [f'G{e}'])
                    tt('dve', cf[e][:], G[e][:], Wt[:, i, :], ALU.mult, [f'G{e}', 'Wt'], [f'cf{e}'])

                def o_mm(i):
                    s = i % 4
                    e = i % 4
                    for hh in range(2):
                        for dh in range(2):
                            bank = hh * 2 + dh
                            mm(ps[bank][:, :], cf[e][:, hh * 128:(hh + 1) * 128], vbb[s][:, dh * 512:(dh + 1) * 512], i == 0, i == 127,
                               [f'cf{e}', f'vbb{s}'], [PS[bank]])

                for stg_ in prep_stages(0):
                    stg_()
                e4(0)
                pend = []
                for it in range(ntl):
                    stages = list(pend)
                    if it + 1 < ntl:
                        stages += prep_stages(it + 1)
                    items = Sched.hoist(S.record(stages), 30)
                    nit = len(items)
                    wmap = {'dve': 4.0, 'pool': 3.0, 'act': 1.0, 'pe': 1.5, 'sp': 0.5}
                    cumw = []
                    acc_w = 0.0
                    for it_ in items:
                        acc_w += wmap.get(it_[0], 1.0)
                        cumw.append(acc_w)
                    totw = max(acc_w, 1.0)
                    for i in range(5):
                        load_u(i)
                    for i in range(3):
                        load_v(i)
                    s_mm(it, 0)
                    s_mm(it, 1)
                    done = 0
                    for i in range(128):
                        if i + 5 < 128:
                            load_u(i + 5)
                        if i + 3 < 128:
                            load_v(i + 3)
                        if i + 2 < 128:
                            s_mm(it, i + 2)
                        o_mm(i)
                        upto = done
                        while upto < nit and cumw[upto] <= (i + 1) * totw / 118.0:
                            upto += 1
                        if _DBG_NOINTER:
                            upto = done
                        S.replay(items[done:upto])
                        done = max(done, upto)
                    S.replay(items[done:])
                    f_evac(it)
                    if it + 1 < ntl:
                        e4(it + 1)
                    pend = f_stages(it)
                for stg_ in pend:
                    stg_()
                S.end_phase()

    run('S', xs, ys, Ls, Ls)
    run('P', xp, yp, Lp, LpO)
    es.close()
    return nc, S


def _feat_tables(L):
    n = 2 * L
    jp = np.arange(n)
    pos = np.abs(L - jp).astype(np.float64)
    pos[0] = 0.0
    t = pos / (L - 1)
    w = 2.0 * math.pi * pos / L
    f = np.linspace(1e-4, 15.0, 16)
    feat = np.concatenate([t[None], np.cos(f[:, None] * w[None]), -np.sin(f[:, None] * w[None]), np.ones((1, n))], 0)
    return feat.astype(np.float32), t[None].astype(np.float32)


_CACHE = {}


def _prep_inputs(Lp, Ls, x_prompt, x_sample, w_in, b_in, a_conv_w, h_conv_w, h_conv_b, hf_w1, hf_b1, hf_freq,
                 hf_w2, hf_b2, hf_w3, hy_decay, hy_bias, w_out, ln1_g, ln1_b, peer_wq, peer_keys,
                 peer_u, peer_v, ln2_g, ln2_b):
    c = np.ascontiguousarray
    f32 = np.float32
    featP, tposP = _feat_tables(Lp)
    featS, tposS = _feat_tables(Ls)
    acw = np.asarray(a_conv_w[0]); hcw = np.asarray(h_conv_w[0])

    def cwl(a, nch):
        return c(a.reshape(3, nch, 128).transpose(2, 1, 0))
    w3 = np.asarray(hf_w3[0]); w3d = [c(w3[:, :512]), c(w3[:, 512:])]
    dec = np.asarray(hy_decay[0]); decd = [c(dec[0:1]), c(dec[1:2])]
    shared = dict(
        w_in=c(np.asarray(w_in[0])), b_in_t=c(np.asarray(b_in[0]).reshape(24, 128).T),
        acw_s=cwl(acw, 4), hcw_s=cwl(hcw, 12), hcb=c(np.asarray(h_conv_b[0]).reshape(12, 128).T),
        w1aug=c(np.concatenate([np.asarray(hf_w1[0]), np.asarray(hf_b1[0])[None]], 0)),
        w2aug=c(np.concatenate([np.asarray(hf_w2[0]), np.asarray(hf_b2[0])[None]], 0)),
        freq=c(np.asarray(hf_freq[0]).reshape(64, 1)),
        w3pos_s=w3d[0], w3neg_s=w3d[1], w3zero=w3d[0], decpos_s=decd[0], decneg_s=decd[1],
        feat_p=featP, tpos_p=tposP, feat_s=featS, tpos_s=tposS,
        hyb=c(np.asarray(hy_bias[0]).reshape(4, 128).T), w_out=c(np.asarray(w_out[0])),
        ln1g=c(np.asarray(ln1_g[0])[None]), ln1b=c(np.asarray(ln1_b[0])[None]),
        ln2g=c(np.asarray(ln2_g[0])[None]), ln2b=c(np.asarray(ln2_b[0])[None]),
        wq=c(np.asarray(peer_wq[0])), keys=c(np.asarray(peer_keys[0]).reshape(2048, 128)),
        u=c(np.asarray(peer_u[0])), v=c(np.asarray(peer_v[0])),
        ident=np.eye(128, dtype=f32), jmat=c(np.eye(128, dtype=f32)[::-1]),
        iota128=c(np.tile(np.arange(128, dtype=f32)[None], (128, 1))),
        iota16=c(np.tile(np.arange(16, dtype=f32)[None], (128, 1))),
    )
    shared = {k: np.asarray(v, dtype=f32) for k, v in shared.items()}
    in_maps = []
    xpn = np.asarray(x_prompt); xsn = np.asarray(x_sample)
    for k in range(NCORES):
        b, half = k // 2, k % 2
        m = dict(shared)
        if half == 0:
            m['xp'] = c(xpn[b]); m['acw_p'] = shared['acw_s']; m['hcw_p'] = shared['hcw_s']
            m['w3pos_p'] = w3d[0]; m['w3neg_p'] = w3d[1]; m['decpos_p'] = decd[0]; m['decneg_p'] = decd[1]
        else:
            m['xp'] = c(xpn[b][::-1]); m['acw_p'] = cwl(acw[::-1], 4); m['hcw_p'] = cwl(hcw[::-1], 12)
            m['w3pos_p'] = w3d[1]; m['w3neg_p'] = w3d[0]; m['decpos_p'] = decd[1]; m['decneg_p'] = decd[0]
        m['xs'] = c(xsn[k])
        in_maps.append({kk: np.asarray(vv, dtype=f32) for kk, vv in m.items()})
    return in_maps


def kernel(**inputs):
    xpn = inputs['x_prompt']; xsn = inputs['x_sample']
    B, Lp, _ = xpn.shape
    Bs, Ls, _ = xsn.shape
    assert B == 4 and Bs == 8
    key = (Lp, Ls)
    if key not in _CACHE:
        _CACHE[key] = build_nc(Lp, Ls)[0]
    nc = _CACHE[key]
    in_maps = _prep_inputs(Lp, Ls, **inputs)
    res = run_bass_kernel_spmd(nc, in_maps, core_ids=list(range(NCORES)))
    yp = np.zeros((B, Lp, D), np.float32)
    ys = np.zeros((Bs, Ls, D), np.float32)
    h = Lp // 2
    for k in range(NCORES):
        r = res.results[k]
        b, half = k // 2, k % 2
        if half == 0:
            yp[b, :h] = r['yp']
        else:
            yp[b, h:] = r['yp'][::-1]
        ys[k] = r['ys']
    return (yp, ys)
```
